# Optimizing a Trainium2 kernel written in Bass

```python
import math
import jax, jax.numpy as jnp
from jax import lax
import numpy as np

D_MODEL = 2048
BATCH = 4
SEQ = 4096
DEPTH = 2

CHUNK = 64
N_META = 16
META_OFFSET = CHUNK - N_META
Q_BLOCK = 128
EPS = 1e-6
ROPE_THETA = 10000.0

GLA_HEADS = 4
GLA_DK = 128
GLA_DV = 256
GLA_GATE_RANK = 16
GLA_GATE_NORM = 16.0

RET_HEADS = 4
RET_DK = 256
RET_DV = 256

DIFF_HEADS = 8
DIFF_DK = 128
DIFF_DV = 256

D_FF = 5632
CONV_W = 3

GLA_SPLITS = (GLA_HEADS * GLA_DK, GLA_HEADS * GLA_DK, GLA_HEADS * GLA_DV, GLA_HEADS * GLA_DV, GLA_GATE_RANK)
RET_SPLITS = (RET_HEADS * RET_DK, RET_HEADS * RET_DK, RET_HEADS * RET_DV, RET_HEADS * RET_DV)
EVEN_IN = sum(GLA_SPLITS) + sum(RET_SPLITS)
EVEN_MIX = GLA_HEADS * GLA_DV + RET_HEADS * RET_DV
DIFF_SPLITS = (2 * DIFF_HEADS * DIFF_DK, 2 * DIFF_HEADS * DIFF_DK, DIFF_HEADS * DIFF_DV)
ODD_IN = sum(DIFF_SPLITS)
ODD_MIX = DIFF_HEADS * DIFF_DV

kernel_name = 'hybrid_gla_retnet_diffattn_convffn_meta'


def rms_norm(x, g):
    xf = x.astype(jnp.float32)
    y = xf * lax.rsqrt(jnp.mean(xf * xf, axis=-1, keepdims=True) + EPS)
    return (y * g.astype(jnp.float32)).astype(x.dtype)


def split_cols(t, sizes):
    offs = np.cumsum(sizes)[:-1]
    return jnp.split(t, [int(o) for o in offs], axis=-1)


def to_heads(t, n_heads):
    b, l, _ = t.shape
    return t.reshape(b, l, n_heads, -1).transpose(0, 2, 1, 3)


def from_heads(t):
    b, n, l, d = t.shape
    return t.transpose(0, 2, 1, 3).reshape(b, l, n * d)


def rotate(x, pos, inv_freq):
    ang = pos.astype(jnp.float32)[:, None] * inv_freq[None, :]
    cos, sin = jnp.cos(ang), jnp.sin(ang)
    x1, x2 = jnp.split(x.astype(jnp.float32), 2, axis=-1)
    return jnp.concatenate([x1 * cos - x2 * sin, x2 * cos + x1 * sin], axis=-1).astype(x.dtype)


def to_chunks(t):
    b, h, l, d = t.shape
    return t.reshape(b, h, l // CHUNK, CHUNK, d).transpose(2, 0, 1, 3, 4)


def from_chunks(t):
    nc, b, h, c, d = t.shape
    return t.transpose(1, 2, 0, 3, 4).reshape(b, h, nc * c, d)


def gla_chunk_scan(q, k, v, log_a):
    b, h, _, dk = q.shape
    dv = v.shape[-1]

    def step(state, inp):
        qc, kc, vc, la = inp
        cum = jnp.cumsum(la, axis=-2)
        pair = jnp.exp(-jnp.abs(cum[:, :, :, None, :] - cum[:, :, None, :, :]))
        scores = jnp.einsum('bhid,bhjd,bhijd->bhij', qc, kc, pair)
        o = jnp.einsum('bhij,bhjv->bhiv', scores, vc) + jnp.einsum('bhid,bhdv->bhiv', qc * jnp.exp(cum), state)
        last = cum[:, :, -1:, :]
        state = state * jnp.exp(last)[:, :, 0, :, None] + jnp.einsum('bhjd,bhjv->bhdv', kc * jnp.exp(last - cum), vc)
        return state, o

    init = jnp.zeros((b, h, dk, dv), jnp.float32)
    _, o = lax.scan(step, init, (to_chunks(q), to_chunks(k), to_chunks(v), to_chunks(log_a)))
    return from_chunks(o)


def retention_chunk_scan(q, k, v, log_gamma):
    b, h, _, dk = q.shape
    dv = v.shape[-1]
    n = jnp.arange(CHUNK, dtype=jnp.float32)
    intra = jnp.exp(jnp.abs(n[:, None] - n[None, :])[None] * log_gamma[:, None, None])
    xi = jnp.exp((n[None, :] + 1.0) * log_gamma[:, None])[:, :, None]
    zeta = jnp.exp((CHUNK - 1.0 - n[None, :]) * log_gamma[:, None])[:, :, None]
    g_chunk = jnp.exp(CHUNK * log_gamma)[:, None, None]

    def step(state, inp):
        qc, kc, vc = inp
        o = (jnp.einsum('bhij,bhjv->bhiv', jnp.einsum('bhid,bhjd->bhij', qc, kc) * intra, vc)
             + jnp.einsum('bhid,bhdv->bhiv', qc, state) * xi)
        state = state * g_chunk + jnp.einsum('bhjd,bhjv->bhdv', kc * zeta, vc)
        return state, o

    init = jnp.zeros((b, h, dk, dv), jnp.float32)
    _, o = lax.scan(step, init, (to_chunks(q), to_chunks(k), to_chunks(v)))
    return from_chunks(o)


def diff_attention(q1, q2, k1, k2, v, lam, chunk_id, key_valid):
    b, h, l, d = q1.shape
    nq = l // Q_BLOCK
    scale = d ** -0.5

    def blocks(t):
        return t.reshape(b, h, nq, Q_BLOCK, t.shape[-1]).transpose(2, 0, 1, 3, 4)

    def one_block(args):
        q1b, q2b, qcid = args
        mask = (chunk_id[None, :] <= qcid[:, None]) & key_valid[None, :]

        def probs(qb, kk):
            s = jnp.einsum('bhqd,bhkd->bhqk', qb, kk).astype(jnp.float32) * scale
            return jax.nn.softmax(jnp.where(mask, s, -1e30), axis=-1)

        a = probs(q1b, k1) - lam * probs(q2b, k2)
        return jnp.einsum('bhqk,bhkv->bhqv', a.astype(v.dtype), v)

    o = lax.map(one_block, (blocks(q1), blocks(q2), chunk_id.reshape(nq, Q_BLOCK)))
    return o.transpose(1, 2, 0, 3, 4).reshape(b, h, l, -1)


def even_mixer(hn, valid, pos, w_in, w_gate2, b_gate, gla_norm, ret_norm, w_out):
    dtype = hn.dtype
    f32 = jnp.float32
    qa, ka, va, ga, lr, qb, kb, vb, gb = split_cols(hn @ w_in, GLA_SPLITS + RET_SPLITS)
    vm = valid.astype(f32)[None, None, :, None]
    log_a = jax.nn.log_sigmoid((lr @ w_gate2 + b_gate).astype(f32)) / GLA_GATE_NORM
    o_a = gla_chunk_scan(to_heads(qa, GLA_HEADS).astype(f32) * (GLA_DK ** -0.5),
                         to_heads(ka, GLA_HEADS).astype(f32) * vm,
                         to_heads(va, GLA_HEADS).astype(f32),
                         to_heads(log_a, GLA_HEADS) * vm)
    o_a = from_heads(rms_norm(o_a.astype(dtype), gla_norm[:, None, :])) * jax.nn.silu(ga)
    inv_freq_ret = 1.0 / (ROPE_THETA ** jnp.linspace(0.0, 1.0, RET_DK // 2, dtype=f32))
    log_gamma = jnp.log(1.0 - 2.0 ** (-5.0 - jnp.arange(RET_HEADS, dtype=f32)))
    qr = rotate(to_heads(qb, RET_HEADS), pos, inv_freq_ret).astype(f32)
    kr = rotate(to_heads(kb, RET_HEADS), pos, inv_freq_ret).astype(f32) * (RET_DK ** -0.5) * vm
    o_b = retention_chunk_scan(qr, kr, to_heads(vb, RET_HEADS).astype(f32), log_gamma)
    o_b = from_heads(rms_norm(o_b.astype(dtype), ret_norm[:, None, :])) * jax.nn.silu(gb)
    return jnp.concatenate([o_a, o_b], axis=-1) @ w_out


def odd_mixer(hn, layer, valid, pos, chunk_id, w_in, q_norm, k_norm, lam_q1, lam_k1, lam_q2, lam_k2, out_norm, w_out):
    b, l, _ = hn.shape
    f32 = jnp.float32
    q, k, v = split_cols(hn @ w_in, DIFF_SPLITS)
    inv_freq = 1.0 / (ROPE_THETA ** (jnp.arange(0, DIFF_DK, 2, dtype=f32) / DIFF_DK))

    def qk(t, g):
        t = rms_norm(t.reshape(b, l, DIFF_HEADS, 2, DIFF_DK), g)
        t = rotate(t.transpose(0, 3, 2, 1, 4), pos, inv_freq)
        return t[:, 0], t[:, 1]

    q1, q2 = qk(q, q_norm)
    k1, k2 = qk(k, k_norm)
    lam_init = 0.8 - 0.6 * math.exp(-0.3 * layer)
    lam = (jnp.exp(jnp.sum(lam_q1.astype(f32) * lam_k1.astype(f32)))
           - jnp.exp(jnp.sum(lam_q2.astype(f32) * lam_k2.astype(f32))) + lam_init)
    o = diff_attention(q1, q2, k1, k2, to_heads(v, DIFF_HEADS), lam, chunk_id, valid)
    o = rms_norm(o, out_norm) * (1.0 - lam_init)
    return from_heads(o) @ w_out


def conv_ffn(hn, w_up, conv_w, conv_b, w_down):
    u = hn @ w_up
    l = u.shape[1]
    up = jnp.pad(u, ((0, 0), (CONV_W - 1, 0), (0, 0)))
    c = conv_b
    for t in range(CONV_W):
        c = c + conv_w[t] * up[:, t:t + l]
    val, gate = jnp.split(c, 2, axis=-1)
    return (val * jax.nn.silu(gate)) @ w_down


def setup_inputs(seed: int = 0) -> dict:
    key = jax.random.key(seed)
    ks = iter(jax.random.split(key, 32))
    n_even = (DEPTH + 1) // 2
    n_odd = DEPTH // 2

    def nrm(shape, scale):
        return scale * jax.random.normal(next(ks), shape, jnp.float32)

    def gain(shape):
        return 1.0 + 0.1 * jax.random.normal(next(ks), shape, jnp.float32)

    return {
        'x': nrm((BATCH, SEQ, D_MODEL), 1.0),
        'meta': nrm((N_META, D_MODEL), 1.0),
        'mix_norm_e': gain((n_even, D_MODEL)),
        'w_in_e': nrm((n_even, D_MODEL, EVEN_IN), D_MODEL ** -0.5),
        'gla_w_gate_e': nrm((n_even, GLA_GATE_RANK, GLA_HEADS * GLA_DK), GLA_GATE_RANK ** -0.5),
        'gla_b_gate_e': nrm((n_even, GLA_HEADS * GLA_DK), 0.1),
        'gla_norm_e': gain((n_even, GLA_HEADS, GLA_DV)),
        'ret_norm_e': gain((n_even, RET_HEADS, RET_DV)),
        'w_out_e': nrm((n_even, EVEN_MIX, D_MODEL), EVEN_MIX ** -0.5),
        'mix_norm_o': gain((n_odd, D_MODEL)),
        'w_in_o': nrm((n_odd, D_MODEL, ODD_IN), D_MODEL ** -0.5),
        'q_norm_o': gain((n_odd, DIFF_DK)),
        'k_norm_o': gain((n_odd, DIFF_DK)),
        'lam_q1_o': nrm((n_odd, DIFF_DK), 0.1),
        'lam_k1_o': nrm((n_odd, DIFF_DK), 0.1),
        'lam_q2_o': nrm((n_odd, DIFF_DK), 0.1),
        'lam_k2_o': nrm((n_odd, DIFF_DK), 0.1),
        'diff_norm_o': gain((n_odd, DIFF_DV)),
        'w_out_o': nrm((n_odd, ODD_MIX, D_MODEL), ODD_MIX ** -0.5),
        'ffn_norm': gain((DEPTH, D_MODEL)),
        'w_up': nrm((DEPTH, D_MODEL, 2 * D_FF), D_MODEL ** -0.5),
        'conv_w': nrm((DEPTH, CONV_W, 2 * D_FF), CONV_W ** -0.5),
        'conv_b': nrm((DEPTH, 2 * D_FF), 0.02),
        'w_down': nrm((DEPTH, D_FF, D_MODEL), D_FF ** -0.5),
    }


def reference(x, meta, mix_norm_e, w_in_e, gla_w_gate_e, gla_b_gate_e, gla_norm_e, ret_norm_e, w_out_e,
              mix_norm_o, w_in_o, q_norm_o, k_norm_o, lam_q1_o, lam_k1_o, lam_q2_o, lam_k2_o, diff_norm_o, w_out_o,
              ffn_norm, w_up, conv_w, conv_b, w_down):
    b, s, d = x.shape
    lp = -(-(CHUNK + s) // Q_BLOCK) * Q_BLOCK
    h = jnp.concatenate([jnp.zeros((b, META_OFFSET, d), x.dtype),
                         jnp.broadcast_to(meta.astype(x.dtype)[None], (b, N_META, d)),
                         x,
                         jnp.zeros((b, lp - CHUNK - s, d), x.dtype)], axis=1)
    idx = jnp.arange(lp)
    valid = (idx >= META_OFFSET) & (idx < CHUNK + s)
    pos = idx - META_OFFSET
    chunk_id = idx // CHUNK
    keep = valid.astype(x.dtype)[None, :, None]
    for layer in range(DEPTH):
        i = layer // 2
        if layer % 2 == 0:
            mix = even_mixer(rms_norm(h, mix_norm_e[i]), valid, pos, w_in_e[i], gla_w_gate_e[i], gla_b_gate_e[i],
                             gla_norm_e[i], ret_norm_e[i], w_out_e[i])
        else:
            mix = odd_mixer(rms_norm(h, mix_norm_o[i]), layer, valid, pos, chunk_id, w_in_o[i], q_norm_o[i], k_norm_o[i],
                            lam_q1_o[i], lam_k1_o[i], lam_q2_o[i], lam_k2_o[i], diff_norm_o[i], w_out_o[i])
        h = h + mix * keep
        h = h + conv_ffn(rms_norm(h, ffn_norm[layer]), w_up[layer], conv_w[layer], conv_b[layer], w_down[layer]) * keep
    return h[:, CHUNK:CHUNK + s]
```

```python
import contextlib
import math
import numpy as np
import concourse.bass as bass
import concourse.mybir as mybir
from concourse.bass_utils import run_bass_kernel_spmd

F32 = mybir.dt.float32
BF16 = mybir.dt.bfloat16
AF = mybir.ActivationFunctionType
ALU = mybir.AluOpType
AX = mybir.AxisListType

D = 2048
T = 2112
NSL = 33
TILES = [(i * 128, min(128, T - i * 128)) for i in range(17)]
SEQC = 65
LSEQ = SEQC * 64
DFF = 5632
EPS = 1e-6
LAM_INIT = 0.8 - 0.6 * math.exp(-0.3 * 1)
NSLOT = 8
NCC = 8


class Buf:
    __slots__ = ("name", "w", "r", "rd", "excl")

    def __init__(self, name="", excl=False):
        self.name = name
        self.w = []
        self.r = {}
        self.rd = []
        self.excl = excl


class Sched:
    ENG = ("pe", "act", "dve", "pool", "sp")
    DMAQ = ("sp", "pool", "act")

    def __init__(self, nc):
        self.nc = nc
        self.prog = {e: [] for e in self.ENG}
        self.cnt = {e: 0 for e in self.ENG}
        self.dcnt = {q: 0 for q in self.DMAQ}
        self.ccnt = 0
        self.seen = {e: {} for e in self.ENG}
        self.pending = {e: {} for e in self.ENG}

    def _semkey(self, ev):
        if ev[0] == "e":
            return ("e", ev[1]), ev[2]
        if ev[0] == "c":
            return ("c", ev[1]), 1
        q, k = ev[1], ev[2]
        return ("d", q, k % NSLOT), 16 * (k // NSLOT + 1)

    def _collect(self, X, reads, writes):
        need = dict(self.pending[X])
        self.pending[X] = {}
        seen = self.seen[X]

        def add(ev):
            if ev is None:
                return
            if ev[0] == "e" and ev[1] == X and X == "pe":
                return
            key, val = self._semkey(ev)
            if seen.get(key, 0) >= val:
                return
            if need.get(key, 0) < val:
                need[key] = val

        excl_reads = [b for b in reads if b.excl]
        for b in reads:
            for ev in b.w:
                add(ev)
        for b in list(writes) + excl_reads:
            for ev in b.w:
                add(ev)
            for ev in b.r.values():
                add(ev)
            for ev in b.rd:
                add(ev)
        out = []
        for key, val in need.items():
            if seen.get(key, 0) < val:
                seen[key] = val
                out.append((key, val))
        return out

    def _mark(self, my, reads, writes, is_dma):
        writes = list(writes) + [b for b in reads if b.excl]
        for b in reads:
            if b.excl:
                continue
            if is_dma:
                b.rd.append(my)
            else:
                b.r[my[1]] = my
        for b in writes:
            b.w = [my]
            b.r = {}
            b.rd = []

    def op(self, eng, fn, reads=(), writes=()):
        waits = self._collect(eng, reads, writes)
        self.cnt[eng] += 1
        my = ("e", eng, self.cnt[eng])
        self.prog[eng].append((waits, fn, ("e", eng), 1))
        self._mark(my, reads, writes, False)
        return my

    def dma(self, q, out_ap, in_ap, reads=(), writes=(), add_writes=(), **kw):
        k = self.dcnt[q]
        waits = self._collect(q, reads, writes)
        if k >= NSLOT:
            key, val = ("d", q, k % NSLOT), 16 * (k // NSLOT)
            if self.seen[q].get(key, 0) < val:
                self.seen[q][key] = val
                waits.append((key, val))
        self.dcnt[q] += 1
        my = ("d", q, k)

        def fn(e):
            o = out_ap(e) if callable(out_ap) else out_ap
            i = in_ap(e) if callable(in_ap) else in_ap
            return e.dma_start(out=o, in_=i, **kw)

        self.prog[q].append((waits, fn, ("d", q, k % NSLOT), 16))
        self._mark(my, reads, writes, True)
        for b in add_writes:
            b.w.append(my)
        return my

    def cc(self, fn, reads=(), writes=()):
        i = self.ccnt
        self.ccnt += 1
        waits = self._collect("pool", reads, writes)
        my = ("c", i)
        self.prog["pool"].append((waits, fn, ("c", i), None))
        self._mark(my, reads, writes, True)
        return my

    def barrier(self):
        allw = {}
        for e in self.ENG:
            if self.cnt[e]:
                allw[("e", e)] = self.cnt[e]
        for q in self.DMAQ:
            n = self.dcnt[q]
            for i in range(min(n, NSLOT)):
                last_k = ((n - 1 - i) // NSLOT) * NSLOT + i
                allw[("d", q, i)] = 16 * (last_k // NSLOT + 1)
        for i in range(self.ccnt):
            allw[("c", i)] = 1
        for e in self.ENG:
            p = self.pending[e]
            for k, v in allw.items():
                if k == ("e", e) and e == "pe":
                    continue
                if p.get(k, 0) < v:
                    p[k] = v

    def emit(self, final_eng="sp"):
        nc = self.nc
        self.barrier()
        with contextlib.ExitStack() as st:
            sems = {}
            for e in self.ENG:
                sems[("e", e)] = st.enter_context(nc.semaphore(f"s_{e}"))
            for q in self.DMAQ:
                for i in range(NSLOT):
                    sems[("d", q, i)] = st.enter_context(nc.semaphore(f"d_{q}{i}"))
            for i in range(self.ccnt):
                sems[("c", i)] = st.enter_context(nc.semaphore(f"c_{i}"))
            fin = [(k, v) for k, v in self.pending[final_eng].items() if self.seen[final_eng].get(k, 0) < v]
            block = st.enter_context(nc.Block())
            prog = self.prog

            def run(engname, eng):
                for waits, fn, inc, amt in prog[engname]:
                    for key, val in waits:
                        eng.wait_ge(sems[key], val)
                    ins = fn(eng)
                    if amt is None:
                        ins.then_inc(sems[inc])
                    else:
                        ins.then_inc(sems[inc], amt)
                if engname == final_eng:
                    for key, val in fin:
                        eng.wait_ge(sems[key], val)

            @block.tensor
            def _(e):
                run("pe", e)

            @block.scalar
            def _(e):
                run("act", e)

            @block.vector
            def _(e):
                run("dve", e)

            @block.gpsimd
            def _(e):
                run("pool", e)

            @block.sync
            def _(e):
                run("sp", e)


class Tl:
    def __init__(self, kb, name, shape, dt, n=1, psum=False):
        self.t = []
        self.b = []
        for i in range(n):
            nm = f"{name}_{kb.uid()}"
            if psum:
                t = kb.st.enter_context(kb.nc.psum_tensor(nm, shape, dt))
            else:
                t = kb.st.enter_context(kb.nc.sbuf_tensor(nm, shape, dt))
            self.t.append(t)
            self.b.append(Buf(nm, excl=psum))
        self.n = n
        self.i = -1

    def next(self):
        self.i = (self.i + 1) % self.n
        return self.t[self.i], self.b[self.i]


class KB:
    def __init__(self, nc):
        self.nc = nc
        self.S = Sched(nc)
        self.st = None
        self._uid = 0
        self._rank = {}
        self.dram = {}
        self.dbuf = {}

    def uid(self):
        self._uid += 1
        return self._uid

    def rank(self, e):
        key = id(e)
        if key not in self._rank:
            self._rank[key] = e.snap(e.partition_id() % 2)
        return self._rank[key]

    def dt_in(self, name, shape, dt=F32):
        self.dram[name] = self.nc.dram_tensor(name, list(shape), dt, kind="ExternalInput").ap()
        return self.dram[name]

    def dt_out(self, name, shape, dt=F32):
        self.dram[name] = self.nc.dram_tensor(name, list(shape), dt, kind="ExternalOutput").ap()
        return self.dram[name]

    def dt_int(self, name, shape, dt=F32):
        self.dram[name] = self.nc.dram_tensor(name, list(shape), dt, kind="Internal").ap()
        return self.dram[name]

    @contextlib.contextmanager
    def phase(self):
        old = self.st
        with contextlib.ExitStack() as st:
            self.st = st
            yield
            self.S.barrier()
        self.st = old


def build_identity(kb, ident_t, ident_b):
    S = kb.S
    S.op("pool", lambda e: e.memset(ident_t[:], 0.0), writes=[ident_b])
    S.op("pool", lambda e: e.affine_select(out=ident_t[:], in_=ident_t[:], pattern=[[-1, 128]], compare_op=ALU.not_equal,
                                           fill=1.0, base=0, channel_multiplier=1), reads=[ident_b], writes=[ident_b])


@contextlib.contextmanager
def subscope(kb):
    old = kb.st
    with contextlib.ExitStack() as st:
        kb.st = st
        yield
        kb.S.barrier()
    kb.st = old


def transpose_rows(kb, src_t, src_bufs, ts, nk, dstT, dst_b, tok0, PT, col0=0, dk0=0, evac="act"):
    S = kb.S
    ident, ident_b = kb.ident, kb.ident_b
    for k0 in range(0, nk, 4):
        kn = min(4, nk - k0)
        pt, pb = PT.next()
        for kk in range(kn):
            k = k0 + kk
            S.op("pe", lambda e, k=k, kk=kk, pt=pt: e.transpose(out=pt[:, kk, :ts], in_=src_t[:ts, col0 + k * 128:col0 + (k + 1) * 128],
                                                              identity=ident[:ts, :ts]),
                 reads=list(src_bufs) + [ident_b], writes=[pb])
        if evac == "act":
            S.op("act", lambda e, k0=k0, kn=kn, pt=pt: e.copy(out=dstT[:, dk0 + k0:dk0 + k0 + kn, tok0:tok0 + ts], in_=pt[:, :kn, :ts]),
                 reads=[pb], writes=[dst_b])
        else:
            S.op("dve", lambda e, k0=k0, kn=kn, pt=pt: e.tensor_copy(out=dstT[:, dk0 + k0:dk0 + k0 + kn, tok0:tok0 + ts], in_=pt[:, :kn, :ts]),
                 reads=[pb], writes=[dst_b])


def rstd_inplace(kb, ms_ap, msb, rows, scale):
    S = kb.S
    S.op("act", lambda e: e.activation(out=ms_ap, in_=ms_ap, func=AF.Ln, scale=scale, bias=kb.epsb[:rows, :]), reads=[msb, kb.epsb_b], writes=[msb])
    S.op("act", lambda e: e.activation(out=ms_ap, in_=ms_ap, func=AF.Exp, scale=-0.5), reads=[msb], writes=[msb])


def norm_T(kb, h_ap, hB, gain_row_ap, XT, XTb):
    S = kb.S
    with subscope(kb):
        G = Tl(kb, "ng", [128, D], F32)
        H = Tl(kb, "nh", [128, D], F32, n=3)
        J = Tl(kb, "nj", [128, D], F32)
        XN = Tl(kb, "nx", [128, D], BF16, n=3)
        MS = Tl(kb, "nms", [128, 1], F32, n=3)

        PT = Tl(kb, "npt", [128, 4, 128], BF16, n=2, psum=True)
        S.dma("sp", G.t[0][:], gain_row_ap.partition_broadcast(128), writes=[G.b[0]])
        def stage_a(ti):
            t0, ts = TILES[ti]
            ht, hb = H.next()
            S.dma("sp", ht[:ts, :], h_ap[t0:t0 + ts, :], reads=hB[ti], writes=[hb])
            ms, msb = MS.next()
            S.op("act", lambda e, ht=ht, ms=ms, ts=ts: e.activation(out=J.t[0][:ts, :], in_=ht[:ts, :], func=AF.Square, accum_out=ms[:ts, :]),
                 reads=[hb], writes=[J.b[0], msb])
            rstd_inplace(kb, ms[:ts, :], msb, ts, 1.0 / D)
            xn, xnb = XN.next()
            S.op("dve", lambda e, ht=ht, ms=ms, xn=xn, ts=ts: e.scalar_tensor_tensor(out=xn[:ts, :], in0=ht[:ts, :], scalar=ms[:ts, 0:1], in1=G.t[0][:ts, :],
                                                                                    op0=ALU.mult, op1=ALU.mult),
                 reads=[hb, msb, G.b[0]], writes=[xnb])
            return xn, xnb

        def stage_b(ti, xn, xnb):
            t0, ts = TILES[ti]
            transpose_rows(kb, xn, [xnb], ts, 16, XT, XTb, t0, PT, evac="act" if ti % 2 == 0 else "dve")

        pend = stage_a(0)
        for ti in range(1, len(TILES)):
            nxt = stage_a(ti)
            stage_b(ti - 1, *pend)
            pend = nxt
        stage_b(len(TILES) - 1, *pend)


def gemm_tm(kb, KC, lhsT_prep, wblocks, epilogue, wbufs=2, pbufs=4, tiles=TILES):
    S = kb.S
    with subscope(kb):
        W = Tl(kb, "gw", [128, KC, 512], BF16, n=wbufs)
        PS = Tl(kb, "gp", [128, 512], F32, n=pbufs, psum=True)
        wt_list = []

        def load(bi):
            wt, wb = W.next()
            blk = wblocks[bi]
            for pi, (src_ap, off, wd) in enumerate(blk["pieces"]):
                S.dma("pool", wt[:, :, off:off + wd], src_ap.rearrange("(k p) c -> p k c", p=128),
                      writes=[wb] if pi == 0 else [], add_writes=[] if pi == 0 else [wb])
            wt_list.append((wt, wb))

        load(0)
        for bi, blk in enumerate(wblocks):
            if bi + 1 < len(wblocks):
                load(bi + 1)
            wt, wb = wt_list[bi]
            n = blk["ncols"]
            for ti, (t0, ts) in enumerate(tiles):
                lfn, lbufs = lhsT_prep(bi, ti, t0, ts)
                ps, pb = PS.next()
                for k in range(KC):
                    S.op("pe", lambda e, ps=ps, lap=lfn(k), wt=wt, k=k, n=n, ts=ts: e.matmul(ps[:ts, :n], lhsT=lap, rhs=wt[:, k, :n],
                                                                                          start=(k == 0), stop=(k == KC - 1)),
                         reads=list(lbufs) + [wb], writes=[pb])
                epilogue(bi, ti, t0, ts, ps, pb, n)


def resident_lhsT(XT, XTb):
    def prep(bi, ti, t0, ts):
        return (lambda k: XT[:, k, t0:t0 + ts]), [XTb]
    return prep


def residual_epilogue(kb, h_ap, hB, HS):
    S = kb.S

    def ep(bi, ti, t0, ts, ps, pb, n):
        c0 = bi * 512
        hs, hsb = HS.next()
        S.dma("sp", hs[:ts, :n], h_ap[t0:t0 + ts, c0:c0 + n], reads=[hB[ti][bi]], writes=[hsb])
        S.op("dve", lambda e: e.scalar_tensor_tensor(out=hs[:ts, :n], in0=ps[:ts, :n], scalar=kb.keep[:ts, ti:ti + 1], in1=hs[:ts, :n],
                                                     op0=ALU.mult, op1=ALU.add),
             reads=[pb, hsb, kb.keep_b], writes=[hsb])
        S.dma("sp", h_ap[t0:t0 + ts, c0:c0 + n], hs[:ts, :n], reads=[hsb], writes=[hB[ti][bi]])

    return ep


GROUPS = [[0, 1], [2, 3], [4, 5], [6, 7]]


class Exchange:
    def __init__(self, kb, name, ncols, dt, rpg):
        self.kb, self.ncols, self.rpg = kb, ncols, rpg
        self.P = kb.dt_int(name + "_p", [2 * T, ncols], dt)
        self.P3 = self.P.rearrange("(d t) c -> d t c", d=2)
        self.groups = [(j0, min(rpg, T - j0)) for j0 in range(0, T, rpg)]
        self.G = [kb.dt_int(f"{name}_g{j}", [2, 2 * rows, ncols], dt) for j, (j0, rows) in enumerate(self.groups)]
        self.Gb = [[Buf(), Buf()] for _ in self.groups]
        self.MY = kb.dt_int(name + "_my", [2 * T, ncols], dt)
        self.MY3 = self.MY.rearrange("(s t) c -> s t c", s=2)
        self.MYb = [Buf() for _ in self.groups]
        self.Pb = []

    def wbuf(self):
        b_ = Buf()
        self.Pb.append(b_)
        return b_

    def gather(self):
        S = self.kb.S
        kb = self.kb
        for d in range(2):
            for j, (j0, rows) in enumerate(self.groups):
                src_ap = self.P3[d, j0:j0 + rows, :]
                dst_ap = self.G[j][d]
                S.cc(lambda e, src_ap=src_ap, dst_ap=dst_ap: e.collective_compute("AllGather", ALU.bypass, replica_groups=[list(g) for g in GROUPS],
                                                                                  ins=[src_ap.opt()], outs=[dst_ap.opt()]),
                     reads=self.Pb, writes=[self.Gb[j][d]])
        self.Pb = []
        for j, (j0, rows) in enumerate(self.groups):
            G = self.G[j]
            S.dma("pool", self.MY3[:, j0:j0 + rows, :],
                  (lambda e, G=G: G[bass.ds(kb.rank(e), 1), :, :].rearrange("o (s t) c -> (o s) t c", s=2)),
                  reads=self.Gb[j], writes=[self.MYb[j]])

    def read(self, src, t0, n, c0, wd):
        j = t0 // self.rpg
        j0, rows = self.groups[j]
        assert t0 + n <= j0 + rows
        return self.MY3[src, t0:t0 + n, c0:c0 + wd], [self.MYb[j]]


def mixer_out_proj(kb, EX, colmap, w_ap, h_ap, hB):
    S = kb.S
    with kb.phase():
        XT = Tl(kb, "mx", [128, 16, T], BF16)
        with subscope(kb):
            O = Tl(kb, "go", [128, D], BF16, n=3)
            PT = Tl(kb, "gpt", [128, 4, 128], BF16, n=2, psum=True)
            for ti, (t0, ts) in enumerate(TILES):
                ot, ob = O.next()
                for pi, (src, c_src, c_dst, wd) in enumerate(colmap):
                    apf, gb = EX.read(src, t0, ts, c_src, wd)
                    S.dma("sp", ot[:ts, c_dst:c_dst + wd], apf,
                          reads=gb, writes=[ob] if pi == 0 else [], add_writes=[] if pi == 0 else [ob])
                transpose_rows(kb, ot, [ob], ts, 16, XT.t[0], XT.b[0], t0, PT)
        HS = Tl(kb, "mh", [128, 512], F32, n=3)
        wblocks = [{"pieces": [(w_ap[:, c0:c0 + 512], 0, 512)], "ncols": 512} for c0 in range(0, D, 512)]
        gemm_tm(kb, 16, resident_lhsT(XT.t[0], XT.b[0]), wblocks, residual_epilogue(kb, h_ap, hB, HS))


def ffn(kb, layer, h_ap, hB):
    S = kb.S
    dr = kb.dram
    aT = dr["aT"]
    aTb = [Buf() for _ in range(44)]
    with kb.phase():
        XT = Tl(kb, "fx", [128, 16, T], BF16)
        norm_T(kb, h_ap, hB, dr["ffn_norm"][layer:layer + 1, :], XT.t[0], XT.b[0])
        W = Tl(kb, "fw", [128, 16, 2, 256], BF16, n=2)
        CW = Tl(kb, "fcw", [128, 3, 88], F32)
        CB = Tl(kb, "fcb", [128, 88], F32)
        U = Tl(kb, "fu", [128, 2, T + 2], F32, n=2)
        C = Tl(kb, "fc", [128, 2, T], F32, n=2)
        A = Tl(kb, "fa", [128, T], BF16, n=2)
        PS = Tl(kb, "fp", [128, 512], F32, n=4, psum=True)
        S.dma("sp", CW.t[0][:], dr["conv_w"][layer].rearrange("t (c p) -> p t c", p=128), writes=[CW.b[0]], allow_slow_non_contiguous=True)
        S.dma("sp", CB.t[0][:], dr["conv_b"][layer:layer + 1, :].rearrange("o (c p) -> p (o c)", p=128), writes=[CB.b[0]], allow_slow_non_contiguous=True)
        for i in range(2):
            S.op("pool", lambda e, i=i: e.memset(U.t[i][:, :, 0:2], 0.0), writes=[U.b[i]])
        w_up = dr["w_up"][layer]
        NB = 22
        wl = []

        def loadw(b):
            wt, wb = W.next()
            for vg in range(2):
                c0 = vg * DFF + b * 256
                S.dma("pool", wt[:, :, vg, :], w_up[:, c0:c0 + 256].rearrange("(k p) c -> p k c", p=128),
                      writes=[wb] if vg == 0 else [], add_writes=[] if vg == 0 else [wb])
            wl.append((wt, wb))

        tslices = [(s0, min(512, T - s0)) for s0 in range(0, T, 512)]
        loadw(0)
        for b in range(NB):
            if b + 1 < NB:
                loadw(b + 1)
            wt, wb = wl[b]
            for j in range(2):
                fc = b * 2 + j
                ut, ub = U.next()
                for vg in range(2):
                    for (s0, sn) in tslices:
                        ps, pb = PS.next()
                        for k in range(16):
                            S.op("pe", lambda e, ps=ps, wt=wt, k=k, vg=vg, j=j, s0=s0, sn=sn: e.matmul(
                                ps[:, :sn], lhsT=wt[:, k, vg, j * 128:(j + 1) * 128], rhs=XT.t[0][:, k, s0:s0 + sn], start=(k == 0), stop=(k == 15)),
                                reads=[wb, XT.b[0]], writes=[pb])
                        S.op("act", lambda e, ps=ps, ut=ut, vg=vg, s0=s0, sn=sn: e.copy(out=ut[:, vg, 2 + s0:2 + s0 + sn], in_=ps[:, :sn]),
                             reads=[pb], writes=[ub])
                ct, cb = C.next()
                for vg in range(2):
                    ch = vg * 44 + fc
                    S.op("act", lambda e, ut=ut, ct=ct, vg=vg, ch=ch: e.activation(out=ct[:, vg, :], in_=ut[:, vg, 0:T], func=AF.Identity,
                                                                               scale=CW.t[0][:, 0, ch:ch + 1], bias=CB.t[0][:, ch:ch + 1]),
                         reads=[ub, CW.b[0], CB.b[0]], writes=[cb])
                    for tap in (1, 2):
                        S.op("dve", lambda e, ut=ut, ct=ct, vg=vg, ch=ch, tap=tap: e.scalar_tensor_tensor(
                            out=ct[:, vg, :], in0=ut[:, vg, tap:tap + T], scalar=CW.t[0][:, tap, ch:ch + 1], in1=ct[:, vg, :], op0=ALU.mult, op1=ALU.add),
                            reads=[ub, cb, CW.b[0]], writes=[cb])
                S.op("act", lambda e, ct=ct: e.activation(out=ct[:, 1, :], in_=ct[:, 1, :], func=AF.Silu), reads=[cb], writes=[cb])
                at, ab = A.next()
                S.op("dve", lambda e, ct=ct, at=at: e.tensor_tensor(out=at[:], in0=ct[:, 0, :], in1=ct[:, 1, :], op=ALU.mult), reads=[cb], writes=[ab])
                S.dma("sp", aT[fc * 128:(fc + 1) * 128, :], at[:], reads=[ab], writes=[aTb[fc]])
    with kb.phase():
        HS = Tl(kb, "dh", [128, 512], F32, n=3)
        AT = Tl(kb, "da", [128, 44, 256], BF16, n=2)
        w_down = dr["w_down"][layer]
        wblocks = [{"pieces": [(w_down[:, c0:c0 + 512], 0, 512)], "ncols": 512} for c0 in range(0, D, 512)]
        aT3 = aT.rearrange("(k p) t -> p k t", p=128)
        loaded = {}
        order = [(bi, g) for bi in range(len(wblocks)) for g in range((len(TILES) + 1) // 2)]

        def load_group(key):
            if key in loaded:
                return
            at, ab = AT.next()
            g0 = key[1] * 256
            gn = min(256, T - g0)
            S.dma("sp", at[:, :, :gn], aT3[:, :, g0:g0 + gn], reads=aTb, writes=[ab])
            loaded[key] = (at, ab)

        def prep(bi, ti, t0, ts):
            g = ti // 2
            if ti % 2 == 0:
                load_group((bi, g))
                nxt = order.index((bi, g)) + 1
                if nxt < len(order):
                    load_group(order[nxt])
            at, ab = loaded[(bi, g)]
            off = t0 - g * 256
            return (lambda k: at[:, k, off:off + ts]), [ab]

        gemm_tm(kb, 44, prep, wblocks, residual_epilogue(kb, h_ap, hB, HS))
C_QA, C_KA, C_VA, C_GA, C_QB, C_KB, C_VB, C_GB = 0, 256, 512, 1024, 1536, 2048, 2560, 3072
NP0 = 3584
DBG_L0 = {"la", "gemm", "ag"}
SC_INTRA, SC_XIB, SC_ZETA, SC_GCH, SC_MASK, SC_TRIU, SC_TRIL, SC_KBIAS, SC_N = 0, 128, 256, 258, 260, 516, 580, 644, 648


def stage_store(kb, ST, ps, pb, ts, n, dst_ap, dst_bufs, eng="act"):
    S = kb.S
    stt, stb = ST.next()
    if eng == "act":
        S.op("act", lambda e: e.copy(out=stt[:ts, :n], in_=ps[:ts, :n]), reads=[pb], writes=[stb])
    else:
        S.op("dve", lambda e: e.tensor_copy(out=stt[:ts, :n], in_=ps[:ts, :n]), reads=[pb], writes=[stb])
    S.dma("sp", dst_ap, stt[:ts, :n], reads=[stb], writes=dst_bufs)


def l0_proj(kb, h_ap, hB):
    S = kb.S
    dr = kb.dram
    w = dr["w_in_e"]
    P0 = kb.EX0.P3
    LA0 = kb.EXL.P3
    with kb.phase():
        XT = Tl(kb, "px", [128, 16, T], BF16)
        norm_T(kb, h_ap, hB, dr["mix_norm_e"], XT.t[0], XT.b[0])
        with subscope(kb) if "la" in DBG_L0 else contextlib.nullcontext():
          if "la" in DBG_L0:
              WL = Tl(kb, "wl", [128, 16, 16], BF16)
              LR = Tl(kb, "lr", [17, T], F32)
              LRH = Tl(kb, "lrh", [17, T], BF16)
              LRL = Tl(kb, "lrl", [17, T], BF16)
              W2 = Tl(kb, "w2", [17, 512], F32)
              W2H = Tl(kb, "w2h", [17, 512], BF16)
              W2L = Tl(kb, "w2l", [17, 512], BF16)
              PL = Tl(kb, "pl", [128, 512], F32, n=2, psum=True)
              E1 = Tl(kb, "e1", [128, 512], F32, n=2)
              S.dma("pool", WL.t[0][:], w[:, 3072:3088].rearrange("(k p) c -> p k c", p=128), writes=[WL.b[0]])
              S.dma("sp", W2.t[0][:], dr["gw2"], writes=[W2.b[0]])
              S.op("dve", lambda e: e.memset(LR.t[0][:], 1.0), writes=[LR.b[0]])
              S.op("dve", lambda e: e.tensor_copy(out=W2H.t[0][:], in_=W2.t[0][:]), reads=[W2.b[0]], writes=[W2H.b[0]])
              S.op("dve", lambda e: e.tensor_tensor(out=W2L.t[0][:], in0=W2.t[0][:], in1=W2H.t[0][:], op=ALU.subtract), reads=[W2.b[0], W2H.b[0]], writes=[W2L.b[0]])
              for s0 in range(0, T, 512):
                  sn = min(512, T - s0)
                  ps, pb = PL.next()
                  for k in range(16):
                      S.op("pe", lambda e, ps=ps, k=k, s0=s0, sn=sn: e.matmul(ps[:16, :sn], lhsT=WL.t[0][:, k, :], rhs=XT.t[0][:, k, s0:s0 + sn], start=(k == 0), stop=(k == 15)),
                           reads=[WL.b[0], XT.b[0]], writes=[pb])
                  S.op("act", lambda e, ps=ps, s0=s0, sn=sn: e.copy(out=LR.t[0][0:16, s0:s0 + sn], in_=ps[:16, :sn]), reads=[pb], writes=[LR.b[0]])
              S.op("dve", lambda e: e.tensor_copy(out=LRH.t[0][:], in_=LR.t[0][:]), reads=[LR.b[0]], writes=[LRH.b[0]])
              S.op("dve", lambda e: e.tensor_tensor(out=LRL.t[0][:], in0=LR.t[0][:], in1=LRH.t[0][:], op=ALU.subtract), reads=[LR.b[0], LRH.b[0]], writes=[LRL.b[0]])
              for ti, (t0, ts) in enumerate(TILES):
                  ps, pb = PL.next()
                  combos = [(LRH, W2H), (LRL, W2H), (LRH, W2L)]
                  for ci, (a, b_) in enumerate(combos):
                      S.op("pe", lambda e, ps=ps, a=a, b_=b_, ci=ci, t0=t0, ts=ts: e.matmul(ps[:ts, :], lhsT=a.t[0][:, t0:t0 + ts], rhs=b_.t[0][:, :], start=(ci == 0), stop=(ci == 2)),
                           reads=[a.b[0], b_.b[0]], writes=[pb])
                  e1, e1b = E1.next()
                  S.op("act", lambda e, ps=ps, e1=e1, ts=ts: e.activation(out=e1[:ts, :], in_=ps[:ts, :], func=AF.Exp, scale=-1.0), reads=[pb], writes=[e1b])
                  S.op("act", lambda e, e1=e1, ts=ts: e.activation(out=e1[:ts, :], in_=e1[:ts, :], func=AF.Ln, bias=kb.oneb[:ts, :], scale=1.0), reads=[e1b, kb.oneb_b], writes=[e1b])
                  S.op("dve", lambda e, e1=e1, ts=ts, ti=ti: e.tensor_scalar(out=e1[:ts, :], in0=e1[:ts, :], scalar1=kb.keep[:ts, ti:ti + 1], scalar2=-1.0 / 16.0,
                                                                          op0=ALU.mult, op1=ALU.mult), reads=[e1b, kb.keep_b], writes=[e1b])
                  for dst in range(2):
                      b_ = kb.EXL.wbuf()
                      S.dma("sp", LA0[dst, t0:t0 + ts, :], e1[:ts, dst * 256:(dst + 1) * 256], reads=[e1b], writes=[b_])
        wblocks = []
        for dst in range(2):
            wblocks.append({"pieces": [(w[:, dst * 256:dst * 256 + 256], 0, 256), (w[:, 512 + dst * 256:512 + dst * 256 + 256], 256, 256)],
                            "ncols": 512, "dst": dst, "dcol": 0})
            for base, dcol in ((1024, C_VA), (2048, C_GA), (3088, C_QB), (4112, C_KB), (5136, C_VB), (6160, C_GB)):
                c0 = base + dst * 512
                wblocks.append({"pieces": [(w[:, c0:c0 + 512], 0, 512)], "ncols": 512, "dst": dst, "dcol": dcol})
        ST = Tl(kb, "pst", [128, 512], BF16, n=3)

        def ep(bi, ti, t0, ts, ps, pb, n):
            blk = wblocks[bi]
            b_ = kb.EX0.wbuf()
            stage_store(kb, ST, ps, pb, ts, n, P0[blk["dst"], t0:t0 + ts, blk["dcol"]:blk["dcol"] + n], [b_], eng="act" if (ti % 2 == 0) else "dve")

        if "gemm" in DBG_L0:
            gemm_tm(kb, 16, resident_lhsT(XT.t[0], XT.b[0]), wblocks, ep)


def rms_gate_store(kb, OPt, OPb, nh, hd, rows, GN, GNb, gate_fn, OUT, out_cb, W):
    S = kb.S
    sq, sqb = W["SQ"].next()
    S.op("act", lambda e: e.activation(out=sq[:rows, :nh * hd], in_=OPt, func=AF.Square), reads=[OPb], writes=[sqb])
    ms, msb = W["MS"].next()
    S.op("dve", lambda e: e.tensor_reduce(out=ms[:rows, :nh], in_=sq[:rows, :nh * hd].rearrange("p (h d) -> p h d", h=nh), axis=AX.X, op=ALU.add),
         reads=[sqb], writes=[msb])
    rstd_inplace(kb, ms[:rows, :nh], msb, rows, 1.0 / hd)
    on, onb = W["ON"].next()
    S.op("dve", lambda e: e.tensor_tensor(out=on[:rows, :nh * hd].rearrange("p (h d) -> p h d", h=nh), in0=OPt.rearrange("p (h d) -> p h d", h=nh) if len(OPt.shape) == 2 else OPt,
                                          in1=ms[:rows, :nh].unsqueeze(2).to_broadcast([rows, nh, hd]), op=ALU.mult),
         reads=[OPb, msb], writes=[onb])
    S.op("pool", lambda e: e.tensor_tensor(out=on[:rows, :nh * hd], in0=on[:rows, :nh * hd], in1=GN[:rows, :nh * hd], op=ALU.mult),
         reads=[onb, GNb], writes=[onb])
    ot, ob = OUT.next()
    if gate_fn is not None:
        sg, sgb = gate_fn()
        S.op("dve", lambda e: e.tensor_tensor(out=ot[:rows, :nh * hd], in0=on[:rows, :nh * hd], in1=sg[:rows, :nh * hd], op=ALU.mult),
             reads=[onb, sgb], writes=[ob])
    else:
        S.op("dve", lambda e: e.tensor_copy(out=ot[:rows, :nh * hd], in_=on[:rows, :nh * hd]), reads=[onb], writes=[ob])
    out_cb(ot, ob)


def l0_scan(kb):
    S = kb.S
    dr = kb.dram
    O1 = kb.EXO.P3
    SCL = 128.0 ** -0.5
    with kb.phase():
        SC = Tl(kb, "sc", [128, SC_N], F32)
        GN = Tl(kb, "sgn", [64, 1024], F32)
        TRI = Tl(kb, "tri", [64, 2, 64], BF16)
        ONE = Tl(kb, "one", [64, 1], BF16)
        Sg = Tl(kb, "Sg", [128, 2, 256], F32)
        Sgb = Tl(kb, "Sgb", [128, 2, 256], BF16)
        Sr = Tl(kb, "Sr", [128, 2, 2, 256], F32)
        Srb = Tl(kb, "Srb", [128, 2, 2, 256], BF16)
        S.dma("sp", SC.t[0][:], dr["scc"], writes=[SC.b[0]])
        S.dma("sp", GN.t[0][:], dr["gnorm"].partition_broadcast(64), writes=[GN.b[0]])
        S.op("dve", lambda e: e.tensor_copy(out=TRI.t[0][:], in_=SC.t[0][0:64, SC_TRIU:SC_TRIU + 128].rearrange("p (a b) -> p a b", a=2)), reads=[SC.b[0]], writes=[TRI.b[0]])
        S.op("dve", lambda e: e.memset(ONE.t[0][:], 1.0), writes=[ONE.b[0]])
        S.op("dve", lambda e: e.memset(Sg.t[0][:], 0.0), writes=[Sg.b[0]])
        S.op("dve", lambda e: e.memset(Sgb.t[0][:], 0.0), writes=[Sgb.b[0]])
        S.op("pool", lambda e: e.memset(Sr.t[0][:], 0.0), writes=[Sr.b[0]])
        S.op("pool", lambda e: e.memset(Srb.t[0][:], 0.0), writes=[Srb.b[0]])
        sc = SC.t[0]
        scb = SC.b[0]
        Dt = Tl(kb, "sD", [64, NP0], BF16, n=3)
        LAt = Tl(kb, "sLA", [64, 256], F32, n=3)
        RTt = Tl(kb, "sRT", [64, 256], F32, n=3)
        LH = Tl(kb, "sLH", [64, 2, 256], BF16, n=2)
        EX = Tl(kb, "sEX", [64, 3, 256], F32, n=2)
        DEC = Tl(kb, "sDEC", [128, 2], F32, n=2)
        GPa = Tl(kb, "sGPa", [64, 3, 256], BF16, n=2)
        GPb = Tl(kb, "sGPb", [64, 2, 256], BF16, n=2)
        TT = Tl(kb, "sTT", [128, 8, 64], BF16, n=2)
        STt = Tl(kb, "sST", [64, 256], BF16, n=2)
        RR = Tl(kb, "sR", [64, 4, 2, 128], BF16, n=2)
        TMa = Tl(kb, "sTMa", [64, 4, 2, 128], F32, n=1)
        TMb = Tl(kb, "sTMb", [64, 4, 2, 128], F32, n=1)
        KBr = Tl(kb, "sKB", [64, 2, 256], BF16, n=2)
        TR = Tl(kb, "sTR", [128, 8, 64], BF16, n=2)
        TQ = Tl(kb, "sTQ", [128, 4, 64], BF16, n=2)
        STr = Tl(kb, "sSTr", [64, 2, 64], BF16, n=2)
        SG = Tl(kb, "sSG", [64, 1024], F32, n=2)
        OUT = Tl(kb, "sOUT", [64, 1024], BF16, n=2)
        Wk = {"SQ": Tl(kb, "sSQ", [64, 1024], F32), "MS": Tl(kb, "sMS", [64, 4], F32, n=2), "ON": Tl(kb, "sON", [64, 1024], F32)}
        cumP = Tl(kb, "pcum", [64, 2, 256], F32, psum=True)
        MISC = Tl(kb, "pmisc", [128, 512], F32, psum=True)
        TPB = Tl(kb, "ptp", [128, 16, 64], BF16, psum=True)
        OP = Tl(kb, "pop", [64, 4, 256], F32, psum=True)
        SPs = Tl(kb, "psps", [128, 2, 256], F32, n=3, psum=True)
        misc = MISC.t[0]
        spB = sprB = decB = MISC.b[0]
        tpB = [TPB.b[0], TPB.b[0]]
        ident, ident_b = kb.ident, kb.ident_b

        def stage_a(c):
                src, slot = (0, c) if c <= 32 else (1, c - 32)
                r0 = slot * 64
                dt_, db = Dt.next()
                apf, gb = kb.EX0.read(src, r0, 64, 0, NP0)
                S.dma("sp", dt_[:], apf, reads=gb, writes=[db])
                lat, lab = LAt.next()
                apf, gb = kb.EXL.read(src, r0, 64, 0, 256)
                S.dma("sp", lat[:], apf, reads=gb, writes=[lab])
                rt, rtb = RTt.next()
                S.dma("sp", rt[:], dr["ret_tab"][c * 64:(c + 1) * 64, :], writes=[rtb])
                lh, lhb = LH.next()
                S.op("dve", lambda e, lh=lh, lat=lat: e.tensor_copy(out=lh[:, 0, :], in_=lat[:]), reads=[lab], writes=[lhb])
                S.op("dve", lambda e, lh=lh, lat=lat: e.tensor_tensor(out=lh[:, 1, :], in0=lat[:], in1=lh[:, 0, :], op=ALU.subtract), reads=[lab, lhb], writes=[lhb])
                cp, cpb = cumP.next()
                for w_ in range(2):
                    for hl in range(2):
                        S.op("pe", lambda e, cp=cp, lh=lh, w_=w_, hl=hl: e.matmul(cp[:, w_, :], lhsT=TRI.t[0][:, w_, :], rhs=lh[:, hl, :], start=(hl == 0), stop=(hl == 1)),
                             reads=[TRI.b[0], lhb], writes=[cpb])
                dp, dpb = misc[:, 384:386], decB
                for h in range(2):
                    for hl in range(2):
                        S.op("pe", lambda e, dp=dp, lh=lh, h=h, hl=hl: e.matmul(dp[:, h:h + 1], lhsT=lh[:, hl, h * 128:(h + 1) * 128], rhs=ONE.t[0][:, :], start=(hl == 0), stop=(hl == 1)),
                             reads=[ONE.b[0], lhb], writes=[dpb])
                ex, exb = EX.next()
                S.op("act", lambda e, ex=ex, cp=cp: e.activation(out=ex[:, 0, :], in_=cp[:, 0, :], func=AF.Exp), reads=[cpb], writes=[exb])
                S.op("act", lambda e, ex=ex, cp=cp: e.activation(out=ex[:, 1, :], in_=cp[:, 0, :], func=AF.Exp, scale=-1.0), reads=[cpb], writes=[exb])
                S.op("act", lambda e, ex=ex, cp=cp: e.activation(out=ex[:, 2, :], in_=cp[:, 1, :], func=AF.Exp), reads=[cpb], writes=[exb])
                dec, decb = DEC.next()
                S.op("act", lambda e, dec=dec, dp=dp: e.activation(out=dec[:], in_=dp, func=AF.Exp), reads=[dpb], writes=[decb])
                ga_, gab = GPa.next()
                gb_, gbb = GPb.next()
                q_ap = dt_[:, C_QA:C_QA + 256]
                k_ap = dt_[:, C_KA:C_KA + 256]
                S.op("dve", lambda e, ga_=ga_, ex=ex, q_ap=q_ap: e.scalar_tensor_tensor(out=ga_[:, 0, :], in0=q_ap, scalar=SCL, in1=ex[:, 0, :], op0=ALU.mult, op1=ALU.mult), reads=[db, exb], writes=[gab])
                S.op("pool", lambda e, gb_=gb_, ex=ex, q_ap=q_ap: e.tensor_tensor(out=gb_[:, 0, :], in0=q_ap, in1=ex[:, 1, :], op=ALU.mult), reads=[db, exb], writes=[gbb])
                S.op("dve", lambda e, ga_=ga_, ex=ex, k_ap=k_ap: e.tensor_tensor(out=ga_[:, 1, :], in0=k_ap, in1=ex[:, 1, :], op=ALU.mult), reads=[db, exb, gab], writes=[gab])
                S.op("pool", lambda e, gb_=gb_, ex=ex, k_ap=k_ap: e.tensor_tensor(out=gb_[:, 1, :], in0=k_ap, in1=ex[:, 0, :], op=ALU.mult), reads=[db, exb, gbb], writes=[gbb])
                S.op("dve", lambda e, ga_=ga_, ex=ex, k_ap=k_ap: e.tensor_tensor(out=ga_[:, 2, :], in0=k_ap, in1=ex[:, 2, :], op=ALU.mult), reads=[db, exb, gab], writes=[gab])
                tp, tpb = TPB.t[0][:, 0:8, :], tpB[0]
                srcs = [(ga_, 0, gab), (gb_, 0, gbb), (ga_, 1, gab), (gb_, 1, gbb)]
                for it in range(4):
                    st_, si, sb_ = srcs[it]
                    for h in range(2):
                        S.op("pe", lambda e, tp=tp, st_=st_, si=si, it=it, h=h: e.transpose(out=tp[:, it * 2 + h, :], in_=st_[:, si, h * 128:(h + 1) * 128], identity=ident[:64, :64]),
                             reads=[sb_, ident_b], writes=[tpb])
                tt, ttb = TT.next()
                S.op("act", lambda e, tt=tt, tp=tp: e.copy(out=tt[:], in_=tp), reads=[tpb], writes=[ttb])
                rr, rrb = RR.next()
                tma, tmab = TMa.next()
                tmb_, tmbb = TMb.next()
                xin = dt_[:, C_QB:C_QB + 1024].rearrange("p (g f d) -> p g f d", g=4, f=2)
                cosb = rt[:, 0:128].unsqueeze(1).to_broadcast([64, 4, 128])
                sinb = rt[:, 128:256].unsqueeze(1).to_broadcast([64, 4, 128])
                S.op("dve", lambda e, tma=tma, xin=xin, cosb=cosb: e.tensor_tensor(out=tma[:, :, 0, :], in0=xin[:, :, 0, :], in1=cosb, op=ALU.mult), reads=[db, rtb], writes=[tmab])
                S.op("pool", lambda e, tmb_=tmb_, xin=xin, sinb=sinb: e.tensor_tensor(out=tmb_[:, :, 0, :], in0=xin[:, :, 1, :], in1=sinb, op=ALU.mult), reads=[db, rtb], writes=[tmbb])
                S.op("dve", lambda e, tma=tma, xin=xin, cosb=cosb: e.tensor_tensor(out=tma[:, :, 1, :], in0=xin[:, :, 1, :], in1=cosb, op=ALU.mult), reads=[db, rtb, tmab], writes=[tmab])
                S.op("pool", lambda e, tmb_=tmb_, xin=xin, sinb=sinb: e.tensor_tensor(out=tmb_[:, :, 1, :], in0=xin[:, :, 0, :], in1=sinb, op=ALU.mult), reads=[db, rtb, tmbb], writes=[tmbb])
                S.op("dve", lambda e, tma=tma, tmb_=tmb_, rr=rr: e.tensor_tensor(out=rr[:, :, 0, :], in0=tma[:, :, 0, :], in1=tmb_[:, :, 0, :], op=ALU.subtract), reads=[tmab, tmbb], writes=[rrb])
                S.op("dve", lambda e, tma=tma, tmb_=tmb_, rr=rr: e.tensor_tensor(out=rr[:, :, 1, :], in0=tma[:, :, 1, :], in1=tmb_[:, :, 1, :], op=ALU.add), reads=[tmab, tmbb, rrb], writes=[rrb])
                kbr, kbrb = KBr.next()
                S.op("dve", lambda e, kbr=kbr, rr=rr: e.tensor_tensor(out=kbr[:], in0=rr[:, 2:4, :, :].rearrange("p g f d -> p g (f d)"),
                                                                      in1=sc[0:64, SC_ZETA:SC_ZETA + 2].unsqueeze(2).to_broadcast([64, 2, 256]), op=ALU.mult),
                     reads=[rrb, scb], writes=[kbrb])
                tp2, tp2b = TPB.t[0][:, 8:16, :], tpB[1]
                for g in range(4):
                    for f in range(2):
                        S.op("pe", lambda e, tp2=tp2, rr=rr, g=g, f=f: e.transpose(out=tp2[:, g * 2 + f, :], in_=rr[:, g, f, :], identity=ident[:64, :64]),
                             reads=[rrb, ident_b], writes=[tp2b])
                tr, trb = TR.next()
                S.op("act", lambda e, tr=tr, tp2=tp2: e.copy(out=tr[:], in_=tp2), reads=[tp2b], writes=[trb])
                tq, tqb = TQ.next()
                S.op("dve", lambda e, tq=tq, tp2=tp2: e.tensor_tensor(out=tq[:].rearrange("p (h f) i -> p h f i", h=2), in0=tp2[:, 0:4, :].rearrange("p (h f) i -> p h f i", h=2),
                                                                      in1=sc[:, SC_XIB:SC_XIB + 128].rearrange("p (h i) -> p h i", h=2).unsqueeze(2).to_broadcast([128, 2, 2, 64]), op=ALU.mult),
                     reads=[tp2b, scb], writes=[tqb])
                sg, sgb = SG.next()
                S.op("act", lambda e, sg=sg, dt_=dt_: e.activation(out=sg[:, 0:512], in_=dt_[:, C_GA:C_GA + 512], func=AF.Exp, scale=-1.0), reads=[db], writes=[sgb])
                S.op("act", lambda e, sg=sg, dt_=dt_: e.activation(out=sg[:, 512:1024], in_=dt_[:, C_GB:C_GB + 512], func=AF.Exp, scale=-1.0), reads=[db, sgb], writes=[sgb])
                S.op("act", lambda e, sg=sg: e.activation(out=sg[:, :], in_=sg[:, :], func=AF.Ln, bias=kb.oneb[:64, :], scale=1.0), reads=[sgb, kb.oneb_b], writes=[sgb])
                S.op("act", lambda e, sg=sg: e.activation(out=sg[:, :], in_=sg[:, :], func=AF.Exp, scale=-1.0), reads=[sgb], writes=[sgb])
                S.op("pool", lambda e, sg=sg, dt_=dt_: e.tensor_tensor(out=sg[:, 0:512], in0=sg[:, 0:512], in1=dt_[:, C_GA:C_GA + 512], op=ALU.mult), reads=[db, sgb], writes=[sgb])
                S.op("pool", lambda e, sg=sg, dt_=dt_: e.tensor_tensor(out=sg[:, 512:1024], in0=sg[:, 512:1024], in1=dt_[:, C_GB:C_GB + 512], op=ALU.mult), reads=[db, sgb], writes=[sgb])

                return dict(c=c, dt_=dt_, db=db, dec=dec, decb=decb, ga_=ga_, gab=gab, tt=tt, ttb=ttb, kbr=kbr, kbrb=kbrb, tr=tr, trb=trb, tq=tq, tqb=tqb, sg=sg, sgb=sgb)

        def stage_b(Lc):
                c, dt_, db, dec, decb, ga_, gab, tt, ttb = Lc['c'], Lc['dt_'], Lc['db'], Lc['dec'], Lc['decb'], Lc['ga_'], Lc['gab'], Lc['tt'], Lc['ttb']
                kbr, kbrb, tr, trb, tq, tqb, sg, sgb = Lc['kbr'], Lc['kbrb'], Lc['tr'], Lc['trb'], Lc['tq'], Lc['tqb'], Lc['sg'], Lc['sgb']
                sp_, spb = misc[0:64, 0:256], spB
                for h in range(2):
                    S.op("pe", lambda e, sp_=sp_, tt=tt, h=h: e.matmul(sp_[:, (h * 2 + 0) * 64:(h * 2 + 1) * 64], lhsT=tt[:, 2 * 2 + h, :], rhs=tt[:, 0 * 2 + h, :], start=True, stop=True),
                         reads=[ttb], writes=[spb])
                    S.op("pe", lambda e, sp_=sp_, tt=tt, h=h: e.matmul(sp_[:, (h * 2 + 1) * 64:(h * 2 + 2) * 64], lhsT=tt[:, 3 * 2 + h, :], rhs=tt[:, 1 * 2 + h, :], start=True, stop=True),
                         reads=[ttb], writes=[spb])
                stt, stb = STt.next()
                S.op("dve", lambda e, stt=stt, sp_=sp_: e.tensor_tensor(out=stt[:], in0=sp_, in1=sc[0:64, SC_MASK:SC_MASK + 256], op=ALU.mult), reads=[spb, scb], writes=[stb])
                op_, opb = OP.next()
                for h in range(2):
                    v_ap = dt_[:, C_VA + h * 256:C_VA + (h + 1) * 256]
                    S.op("pe", lambda e, op_=op_, stt=stt, h=h, v_ap=v_ap: e.matmul(op_[:, h, :], lhsT=stt[:, (h * 2) * 64:(h * 2 + 1) * 64], rhs=v_ap, start=True, stop=False),
                         reads=[stb, db], writes=[opb])
                    S.op("pe", lambda e, op_=op_, stt=stt, h=h, v_ap=v_ap: e.matmul(op_[:, h, :], lhsT=stt[:, (h * 2 + 1) * 64:(h * 2 + 2) * 64], rhs=v_ap, start=False, stop=False),
                         reads=[stb, db], writes=[opb])
                    S.op("pe", lambda e, op_=op_, tt=tt, h=h: e.matmul(op_[:, h, :], lhsT=tt[:, 0 * 2 + h, :], rhs=Sgb.t[0][:, h, :], start=False, stop=True),
                         reads=[ttb, Sgb.b[0]], writes=[opb])
                ss, ssb = SPs.next()
                for h in range(2):
                    v_ap = dt_[:, C_VA + h * 256:C_VA + (h + 1) * 256]
                    S.op("pe", lambda e, ss=ss, ga_=ga_, h=h, v_ap=v_ap: e.matmul(ss[:, h, :], lhsT=ga_[:, 2, h * 128:(h + 1) * 128], rhs=v_ap, start=True, stop=True),
                         reads=[gab, db], writes=[ssb])
                for h in range(2):
                    S.op("dve", lambda e, ss=ss, dec=dec, h=h: e.scalar_tensor_tensor(out=Sg.t[0][:, h, :], in0=Sg.t[0][:, h, :], scalar=dec[:, h:h + 1], in1=ss[:, h, :], op0=ALU.mult, op1=ALU.add),
                         reads=[ssb, decb, Sg.b[0]], writes=[Sg.b[0]])
                S.op("pool", lambda e: e.tensor_copy(out=Sgb.t[0][:], in_=Sg.t[0][:]), reads=[Sg.b[0]], writes=[Sgb.b[0]])
                spr, sprb = misc[0:64, 256:384], sprB
                for h in range(2):
                    for f in range(2):
                        S.op("pe", lambda e, spr=spr, tr=tr, h=h, f=f: e.matmul(spr[:, h * 64:(h + 1) * 64], lhsT=tr[:, 4 + h * 2 + f, :], rhs=tr[:, h * 2 + f, :], start=(f == 0), stop=(f == 1)),
                             reads=[trb], writes=[sprb])
                strt, strb = STr.next()
                S.op("dve", lambda e, strt=strt, spr=spr: e.tensor_tensor(out=strt[:].rearrange("p h i -> p (h i)"), in0=spr, in1=sc[0:64, SC_INTRA:SC_INTRA + 128], op=ALU.mult),
                     reads=[sprb, scb], writes=[strb])
                for h in range(2):
                    v_ap = dt_[:, C_VB + h * 256:C_VB + (h + 1) * 256]
                    S.op("pe", lambda e, op_=op_, strt=strt, h=h, v_ap=v_ap: e.matmul(op_[:, 2 + h, :], lhsT=strt[:, h, :], rhs=v_ap, start=True, stop=False),
                         reads=[strb, db], writes=[opb])
                    for f in range(2):
                        S.op("pe", lambda e, op_=op_, tq=tq, h=h, f=f: e.matmul(op_[:, 2 + h, :], lhsT=tq[:, h * 2 + f, :], rhs=Srb.t[0][:, h, f, :], start=False, stop=(f == 1)),
                             reads=[tqb, Srb.b[0]], writes=[opb])
                for h in range(2):
                    v_ap = dt_[:, C_VB + h * 256:C_VB + (h + 1) * 256]
                    ss, ssb = SPs.next()
                    for f in range(2):
                        S.op("pe", lambda e, ss=ss, kbr=kbr, h=h, f=f, v_ap=v_ap: e.matmul(ss[:, f, :], lhsT=kbr[:, h, f * 128:(f + 1) * 128], rhs=v_ap, start=True, stop=True),
                             reads=[kbrb, db], writes=[ssb])
                    S.op("dve", lambda e, ss=ss, h=h: e.scalar_tensor_tensor(out=Sr.t[0][:, h, :, :], in0=Sr.t[0][:, h, :, :], scalar=sc[:, SC_GCH + h:SC_GCH + h + 1], in1=ss[:],
                                                                            op0=ALU.mult, op1=ALU.add),
                         reads=[ssb, scb, Sr.b[0]], writes=[Sr.b[0]])
                S.op("pool", lambda e: e.tensor_copy(out=Srb.t[0][:], in_=Sr.t[0][:]), reads=[Sr.b[0]], writes=[Srb.b[0]])
                def out_cb(ot, ob, c=c):
                    if c <= 32:
                        S.dma("sp", O1[0, c * 64:(c + 1) * 64, :], ot[:, :], reads=[ob], writes=[kb.EXO.wbuf()])
                    if c >= 32:
                        S.dma("sp", O1[1, (c - 32) * 64:(c - 31) * 64, :], ot[:, :], reads=[ob], writes=[kb.EXO.wbuf()])

                rms_gate_store(kb, op_[:].rearrange("p h d -> p (h d)"), opb, 4, 256, 64, GN.t[0], GN.b[0], lambda sg=sg, sgb=sgb: (sg, sgb), OUT, out_cb, Wk)

        pend = stage_a(0)
        for c in range(1, SEQC):
            nxt = stage_a(c)
            stage_b(pend)
            pend = nxt
        stage_b(pend)
NP1 = 3072


def l1_proj(kb, h_ap, hB):
    dr = kb.dram
    w = dr["w_in_o"]
    P1 = kb.EX1.P3
    with kb.phase():
        XT = Tl(kb, "px", [128, 16, T], BF16)
        norm_T(kb, h_ap, hB, dr["mix_norm_o"], XT.t[0], XT.b[0])
        wblocks = []
        for dst in range(2):
            for base, dcol in ((0, 0), (2048, 1024), (4096, 2048)):
                for j in range(2):
                    c0 = base + dst * 1024 + j * 512
                    wblocks.append({"pieces": [(w[:, c0:c0 + 512], 0, 512)], "ncols": 512, "dst": dst, "dcol": dcol + j * 512})
        ST = Tl(kb, "pst", [128, 512], BF16, n=3)

        def ep(bi, ti, t0, ts, ps, pb, n):
            blk = wblocks[bi]
            b_ = kb.EX1.wbuf()
            stage_store(kb, ST, ps, pb, ts, n, P1[blk["dst"], t0:t0 + ts, blk["dcol"]:blk["dcol"] + n], [b_], eng="act" if (ti % 2 == 0) else "dve")

        gemm_tm(kb, 16, resident_lhsT(XT.t[0], XT.b[0]), wblocks, ep)


def seq_rows(n0, n):
    out = []
    c0, c1 = n0 // 64, (n0 + n) // 64
    c = c0
    while c < c1:
        if c <= 32:
            ce = min(c1, 33)
            out.append((0, c * 64, (ce - c) * 64, (c - c0) * 64))
        else:
            ce = c1
            out.append((1, (c - 32) * 64, (ce - c) * 64, (c - c0) * 64))
        c = ce
    return out


def l1_attn(kb):
    S = kb.S
    dr = kb.dram
    O2 = kb.EXO.P3
    SCALE = 128.0 ** -0.5
    NTT = 33
    with kb.phase():
        GQ4 = Tl(kb, "aGQ", [128, 4, 128], F32)
        GD = Tl(kb, "aGD", [128, 256], F32)
        LV = Tl(kb, "aLV", [128, 4, 128], F32)
        LS = Tl(kb, "aLS", [128, 4], F32)
        NLAM = Tl(kb, "aNL", [128, 1], F32)
        KBI = Tl(kb, "aKB", [128, 1], F32)
        ZB = Tl(kb, "aZB", [128, 1], F32)
        S.dma("sp", GQ4.t[0][:, 0, :], dr["qk_norm"][:, 0:128].partition_broadcast(128), writes=[GQ4.b[0]])
        S.dma("sp", GQ4.t[0][:, 2, :], dr["qk_norm"][:, 128:256].partition_broadcast(128), add_writes=[GQ4.b[0]])
        S.op("dve", lambda e: e.tensor_copy(out=GQ4.t[0][:, 1, :], in_=GQ4.t[0][:, 0, :]), reads=[GQ4.b[0]], writes=[GQ4.b[0]])
        S.op("dve", lambda e: e.tensor_copy(out=GQ4.t[0][:, 3, :], in_=GQ4.t[0][:, 2, :]), reads=[GQ4.b[0]], writes=[GQ4.b[0]])
        S.dma("sp", GD.t[0][:], dr["diff_norm"].partition_broadcast(128), writes=[GD.b[0]])
        S.op("dve", lambda e: e.tensor_scalar(out=GD.t[0][:], in0=GD.t[0][:], scalar1=1.0 - LAM_INIT, scalar2=None, op0=ALU.mult), reads=[GD.b[0]], writes=[GD.b[0]])
        S.dma("sp", LV.t[0][:].rearrange("p a b -> p (a b)"), dr["lamv"].partition_broadcast(128), writes=[LV.b[0]])
        S.op("dve", lambda e: e.tensor_tensor(out=LV.t[0][:, 0, :], in0=LV.t[0][:, 0, :], in1=LV.t[0][:, 1, :], op=ALU.mult), reads=[LV.b[0]], writes=[LV.b[0]])
        S.op("dve", lambda e: e.tensor_tensor(out=LV.t[0][:, 2, :], in0=LV.t[0][:, 2, :], in1=LV.t[0][:, 3, :], op=ALU.mult), reads=[LV.b[0]], writes=[LV.b[0]])
        S.op("dve", lambda e: e.tensor_reduce(out=LS.t[0][:], in_=LV.t[0][:], axis=AX.X, op=ALU.add), reads=[LV.b[0]], writes=[LS.b[0]])
        S.op("act", lambda e: e.activation(out=LS.t[0][:], in_=LS.t[0][:], func=AF.Exp), reads=[LS.b[0]], writes=[LS.b[0]])
        S.op("dve", lambda e: e.tensor_tensor(out=NLAM.t[0][:], in0=LS.t[0][:, 2:3], in1=LS.t[0][:, 0:1], op=ALU.subtract), reads=[LS.b[0]], writes=[NLAM.b[0]])
        S.op("dve", lambda e: e.tensor_scalar(out=NLAM.t[0][:], in0=NLAM.t[0][:], scalar1=-LAM_INIT, scalar2=None, op0=ALU.add), reads=[NLAM.b[0]], writes=[NLAM.b[0]])
        S.dma("sp", KBI.t[0][:], dr["scc"][:, SC_KBIAS:SC_KBIAS + 1], writes=[KBI.b[0]], allow_slow_non_contiguous=True)
        S.op("dve", lambda e: e.memset(ZB.t[0][:], 0.0), writes=[ZB.b[0]])

        QKT = Tl(kb, "aQKT", [128, 4, LSEQ], BF16, n=2)
        VA = Tl(kb, "aVA", [128, NTT, 257], BF16, n=2)
        for i in range(2):
            S.op("pool", lambda e, i=i: e.memset(VA.t[i][:, :, 256:257], 1.0), writes=[VA.b[i]])
        X = Tl(kb, "aX", [128, 768], BF16, n=3)
        SQ = Tl(kb, "aSQ", [128, 512], F32)
        MS4 = Tl(kb, "aMS4", [128, 4], F32, n=2)
        XN = Tl(kb, "aXN", [128, 4, 128], F32, n=2)
        DT = Tl(kb, "aDT", [128, 128], F32, n=3)
        TMa = Tl(kb, "aTMa", [128, 4, 2, 64], F32)
        TMb = Tl(kb, "aTMb", [128, 4, 2, 64], F32)
        XR = Tl(kb, "aXR", [128, 4, 128], BF16, n=3)
        PTr = Tl(kb, "aPTr", [128, 4, 128], BF16, n=2, psum=True)
        PT = Tl(kb, "aPT", [128, 512], BF16, n=5)
        SPS = Tl(kb, "aSPS", [128, 512], F32, n=2, psum=True)
        OPS = Tl(kb, "aOPS", [128, 512], F32, n=4, psum=True)
        O1s = Tl(kb, "aO1s", [128, 4, 256], F32)
        O1sb = [Buf() for _ in range(4)]
        RD = Tl(kb, "aRD", [128, 1], F32, n=6)
        OC = Tl(kb, "aOC", [128, 256], F32, n=3)
        OUT = Tl(kb, "aOUT", [128, 256], BF16, n=3)
        Wk = {"SQ": Tl(kb, "aSQ2", [128, 256], F32), "MS": Tl(kb, "aMS", [128, 1], F32, n=2), "ON": Tl(kb, "aON", [128, 256], F32)}

        def prep_gen(h):
            qkt, qktb = QKT.t[h % 2], QKT.b[h % 2]
            va, vab = VA.t[h % 2], VA.b[h % 2]
            pend = [None]
            for tt in range(NTT):
                n0 = tt * 128
                ts = min(128, LSEQ - n0)
                xt, xb = X.next()
                first = True
                for (src, r0, nr, doff) in seq_rows(n0, ts):
                    for ci, cbase in enumerate((0, 1024, 2048)):
                        apf, gb = kb.EX1.read(src, r0, nr, cbase + h * 256, 256)
                        S.dma("sp", xt[doff:doff + nr, ci * 256:(ci + 1) * 256], apf,
                              reads=gb, writes=[xb] if first else [], add_writes=[] if first else [xb])
                        first = False
                dtt, dtb = DT.next()
                S.dma("sp", dtt[:ts, :], dr["diff_tab"][n0:n0 + ts, :], writes=[dtb])
                S.op("act", lambda e, xt=xt, ts=ts: e.activation(out=SQ.t[0][:ts, :], in_=xt[:ts, 0:512], func=AF.Square), reads=[xb], writes=[SQ.b[0]])
                ms, msb = MS4.next()
                S.op("dve", lambda e, ms=ms, ts=ts: e.tensor_reduce(out=ms[:ts, :], in_=SQ.t[0][:ts, :].rearrange("p (g d) -> p g d", g=4), axis=AX.X, op=ALU.add),
                     reads=[SQ.b[0]], writes=[msb])
                rstd_inplace(kb, ms[:ts, :], msb, ts, 1.0 / 128)
                xn, xnb = XN.next()
                S.op("dve", lambda e, xn=xn, xt=xt, ms=ms, ts=ts: e.tensor_tensor(out=xn[:ts], in0=xt[:ts, 0:512].rearrange("p (g d) -> p g d", g=4),
                                                                               in1=ms[:ts, :].unsqueeze(2).to_broadcast([ts, 4, 128]), op=ALU.mult), reads=[xb, msb], writes=[xnb])
                S.op("pool", lambda e, xn=xn, ts=ts: e.tensor_tensor(out=xn[:ts], in0=xn[:ts], in1=GQ4.t[0][:ts], op=ALU.mult), reads=[xnb, GQ4.b[0]], writes=[xnb])
                cosb = dtt[:ts, 0:64].unsqueeze(1).to_broadcast([ts, 4, 64])
                sinb = dtt[:ts, 64:128].unsqueeze(1).to_broadcast([ts, 4, 64])
                x1 = xn[:ts, :, 0:64]
                x2 = xn[:ts, :, 64:128]
                S.op("dve", lambda e, x1=x1, cosb=cosb, ts=ts: e.tensor_tensor(out=TMa.t[0][:ts, :, 0, :], in0=x1, in1=cosb, op=ALU.mult), reads=[xnb, dtb], writes=[TMa.b[0]])
                S.op("pool", lambda e, x2=x2, sinb=sinb, ts=ts: e.tensor_tensor(out=TMb.t[0][:ts, :, 0, :], in0=x2, in1=sinb, op=ALU.mult), reads=[xnb, dtb], writes=[TMb.b[0]])
                S.op("dve", lambda e, x2=x2, cosb=cosb, ts=ts: e.tensor_tensor(out=TMa.t[0][:ts, :, 1, :], in0=x2, in1=cosb, op=ALU.mult), reads=[xnb, dtb, TMa.b[0]], writes=[TMa.b[0]])
                S.op("pool", lambda e, x1=x1, sinb=sinb, ts=ts: e.tensor_tensor(out=TMb.t[0][:ts, :, 1, :], in0=x1, in1=sinb, op=ALU.mult), reads=[xnb, dtb, TMb.b[0]], writes=[TMb.b[0]])
                xr, xrb = XR.next()
                S.op("dve", lambda e, xr=xr, ts=ts: e.tensor_tensor(out=xr[:ts, :, 0:64], in0=TMa.t[0][:ts, :, 0, :], in1=TMb.t[0][:ts, :, 0, :], op=ALU.subtract),
                     reads=[TMa.b[0], TMb.b[0]], writes=[xrb])
                S.op("dve", lambda e, xr=xr, ts=ts: e.tensor_tensor(out=xr[:ts, :, 64:128], in0=TMa.t[0][:ts, :, 1, :], in1=TMb.t[0][:ts, :, 1, :], op=ALU.add),
                     reads=[TMa.b[0], TMb.b[0], xrb], writes=[xrb])
                S.op("pool", lambda e, va=va, xt=xt, tt=tt, ts=ts: e.tensor_copy(out=va[:ts, tt, 0:256], in_=xt[:ts, 512:768]), reads=[xb], writes=[vab])
                if pend[0] is not None:
                    pxr, pxrb, pts, pn0 = pend[0]
                    transpose_rows(kb, pxr[:].rearrange("p g d -> p (g d)"), [pxrb], pts, 4, qkt, qktb, pn0, PTr, evac="dve")
                pend[0] = (xr, xrb, ts, n0)
                yield
            pxr, pxrb, pts, pn0 = pend[0]
            transpose_rows(kb, pxr[:].rearrange("p g d -> p (g d)"), [pxrb], pts, 4, qkt, qktb, pn0, PTr, evac="dve")

        def chunk_store(ot, ob, h, a, rows):
            for cc in range(rows // 64):
                c = a // 64 + cc
                dsts = ([(0, c)] if c <= 32 else []) + ([(1, c - 32)] if c >= 32 else [])
                for (d_, sl) in dsts:
                    S.dma("sp", O2[d_, sl * 64:(sl + 1) * 64, h * 256:(h + 1) * 256], ot[cc * 64:(cc + 1) * 64, :], reads=[ob], writes=[kb.EXO.wbuf()])

        def attention(h, bg):
            qkt, qktb = QKT.t[h % 2], QKT.b[h % 2]
            va, vab = VA.t[h % 2], VA.b[h % 2]
            for q0 in range(0, LSEQ, 512):
                qn = min(512, LSEQ - q0)
                subs = [(a, min(128, q0 + qn - a)) for a in range(q0, q0 + qn, 128)]
                for p in range(2):
                    ops = [OPS.next() for _ in subs]
                    tiles = [t for t in range(NTT) if t * 128 < q0 + qn]

                    def stage1(t):
                        k0 = t * 128
                        kn = min(128, LSEQ - k0)
                        vstart = max(q0, k0)
                        n = q0 + qn - vstart
                        sp_, spb = SPS.next()
                        S.op("pe", lambda e, sp_=sp_, k0=k0, kn=kn, vstart=vstart, n=n, p=p: e.matmul(sp_[:kn, :n], lhsT=qkt[:, 2 + p, k0:k0 + kn], rhs=qkt[:, p, vstart:vstart + n],
                                                                                             start=True, stop=True), reads=[qktb], writes=[spb])
                        pt, ptb = PT.next()
                        bias_t, bias_b = (KBI, KBI.b[0]) if t == 0 else (ZB, ZB.b[0])
                        S.op("act", lambda e, pt=pt, sp_=sp_, kn=kn, n=n, bias_t=bias_t: e.activation(out=pt[:kn, :n], in_=sp_[:kn, :n], func=AF.Exp, scale=SCALE, bias=bias_t.t[0][:kn, :]),
                             reads=[spb, bias_b], writes=[ptb])
                        if k0 >= q0 and kn == 128:
                            S.op("pool", lambda e, pt=pt: e.memset(pt[64:128, 0:64], 0.0), reads=[ptb], writes=[ptb])
                        return (t, kn, vstart, pt, ptb)

                    def finish(m):
                        a, rows = subs[m]
                        op_, opb = ops[m]
                        rd, rdb = RD.next()
                        S.op("dve", lambda e, rd=rd, op_=op_, rows=rows: e.reciprocal(out=rd[:rows, :], in_=op_[:rows, 256:257]), reads=[opb], writes=[rdb])
                        if p == 0:
                            S.op("dve", lambda e, rd=rd, op_=op_, rows=rows, m=m: e.tensor_scalar(out=O1s.t[0][:rows, m, :], in0=op_[:rows, 0:256], scalar1=rd[:rows, 0:1], scalar2=None, op0=ALU.mult),
                                 reads=[opb, rdb], writes=[O1sb[m]])
                        else:
                            oc, ocb = OC.next()
                            S.op("dve", lambda e, rd=rd, op_=op_, rows=rows, oc=oc: e.tensor_scalar(out=oc[:rows, :], in0=op_[:rows, 0:256], scalar1=rd[:rows, 0:1], scalar2=NLAM.t[0][:rows, 0:1],
                                                                                                op0=ALU.mult, op1=ALU.mult), reads=[opb, rdb, NLAM.b[0]], writes=[ocb])
                            S.op("pool", lambda e, rows=rows, oc=oc, m=m: e.tensor_tensor(out=oc[:rows, :], in0=oc[:rows, :], in1=O1s.t[0][:rows, m, :], op=ALU.add),
                                 reads=[ocb, O1sb[m]], writes=[ocb])
                            rms_gate_store(kb, oc[:rows, :], ocb, 1, 256, rows, GD.t[0], GD.b[0], None, OUT,
                                           lambda ot, ob, a=a, rows=rows: chunk_store(ot, ob, h, a, rows), Wk)

                    def stage2(st):
                        t, kn, vstart, pt, ptb = st
                        for m, (a, rows) in enumerate(subs):
                            if a < vstart:
                                continue
                            rel = a - vstart
                            op_, opb = ops[m]
                            last = (t == a // 128)
                            S.op("pe", lambda e, op_=op_, pt=pt, kn=kn, rel=rel, rows=rows, t=t, last=last: e.matmul(op_[:rows, 0:257], lhsT=pt[:kn, rel:rel + rows], rhs=va[:kn, t, :],
                                                                                                                 start=(t == 0), stop=last), reads=[ptb, vab], writes=[opb])
                            if last:
                                finish(m)

                    SKEW = 2
                    pend = []
                    for t in tiles:
                        pend.append(stage1(t))
                        if len(pend) > SKEW:
                            stage2(pend.pop(0))
                    while pend:
                        stage2(pend.pop(0))
                    if bg is not None:
                        for _ in range(2):
                            next(bg, None)
            if bg is not None:
                for _ in bg:
                    pass

        g0 = prep_gen(0)
        for _ in g0:
            pass
        for h in range(4):
            bg = prep_gen(h + 1) if h + 1 < 4 else None
            attention(h, bg)
INPUT_SPECS = [
    ("xin", [T, D]), ("keep", [128, 17]), ("scc", [128, SC_N]), ("gnorm", [1, 1024]), ("ret_tab", [LSEQ, 256]), ("diff_tab", [LSEQ, 128]),
    ("mix_norm_e", [1, D]), ("w_in_e", [D, 7184]), ("gw2", [17, 512]), ("w_out_e", [D, D]),
    ("mix_norm_o", [1, D]), ("w_in_o", [D, 6144]), ("qk_norm", [1, 256]), ("lamv", [1, 512]), ("diff_norm", [1, 256]), ("w_out_o", [D, D]),
    ("ffn_norm", [2, D]), ("w_up", [2, D, 2 * DFF]), ("conv_w", [2, 3, 2 * DFF]), ("conv_b", [2, 2 * DFF]), ("w_down", [2, DFF, D]),
]
PHASES = ["none", "l0_proj", "l0_scan", "l0_out", "l0_ffn", "l1_proj", "l1_attn", "l1_out", "l1_ffn"]


def build(stop_after=None, dump=(), start_at=None):
    nc = bass.Bass("TRN2", target_bir_lowering=False)
    kb = KB(nc)
    S = kb.S
    for name, shape in INPUT_SPECS:
        kb.dt_in(name, shape)
    out = kb.dt_out("out", [2048, D])
    hbuf = kb.dt_int("hbuf", [T, D])
    kb.dt_int("aT", [DFF, T], BF16)
    kb.EX0 = Exchange(kb, "ex0", NP0, BF16, 256)
    kb.EXL = Exchange(kb, "exl", 256, F32, 1024)
    kb.EXO = Exchange(kb, "exo", 1024, BF16, 1024)
    kb.EX1 = Exchange(kb, "ex1", NP1, BF16, 256)
    kb.dram["P0"] = kb.EX0.P
    kb.dram["LA0"] = kb.EXL.P
    kb.dram["O1"] = kb.EXO.P
    kb.dram["P1"] = kb.EX1.P
    dr = kb.dram
    dump_out = {}
    for name in dump:
        src = dr[name]
        dump_out[name] = nc.dram_tensor("dbg_" + name, list(src.shape), src.dtype, kind="ExternalOutput").ap()
    with contextlib.ExitStack() as st:
        kb.st = st
        ident = Tl(kb, "ident", [128, 128], BF16)
        kb.ident, kb.ident_b = ident.t[0], ident.b[0]
        build_identity(kb, kb.ident, kb.ident_b)
        keep = Tl(kb, "keep", [128, 17], F32)
        kb.keep, kb.keep_b = keep.t[0], keep.b[0]
        S.dma("sp", kb.keep[:], dr["keep"], writes=[kb.keep_b])
        epsb = Tl(kb, "epsb", [128, 1], F32)
        kb.epsb, kb.epsb_b = epsb.t[0], epsb.b[0]
        S.op("dve", lambda e: e.memset(kb.epsb[:], EPS), writes=[kb.epsb_b])
        oneb = Tl(kb, "oneb", [128, 1], F32)
        kb.oneb, kb.oneb_b = oneb.t[0], oneb.b[0]
        S.op("dve", lambda e: e.memset(kb.oneb[:], 1.0), writes=[kb.oneb_b])
        hB = [[Buf() for _ in range(4)] for _ in TILES]
        allh = [b for row in hB for b in row]
        S.dma("sp", hbuf, dr["xin"], writes=allh)
        S.barrier()

        def done(phase):
            return stop_after is not None and PHASES.index(phase) >= PHASES.index(stop_after)

        def active(phase):
            return start_at is None or PHASES.index(phase) >= PHASES.index(start_at)

        def run():
            if done("none"):
                return
            if active("l0_proj"):
                l0_proj(kb, hbuf, hB)
                if "ag" in DBG_L0:
                    kb.EX0.gather()
                    kb.EXL.gather()
            if done("l0_proj"):
                return
            if active("l0_scan"):
                l0_scan(kb)
                kb.EXO.gather()
            if done("l0_scan"):
                return
            if active("l0_out"):
                cm0 = [(0, 0, 0, 512), (0, 512, 1024, 512), (1, 0, 512, 512), (1, 512, 1536, 512)]
                mixer_out_proj(kb, kb.EXO, cm0, dr["w_out_e"], hbuf, hB)
            if done("l0_out"):
                return
            if active("l0_ffn"):
                ffn(kb, 0, hbuf, hB)
            if done("l0_ffn"):
                return
            if active("l1_proj"):
                l1_proj(kb, hbuf, hB)
                kb.EX1.gather()
            if done("l1_proj"):
                return
            if active("l1_attn"):
                l1_attn(kb)
                kb.EXO.gather()
            if done("l1_attn"):
                return
            if active("l1_out"):
                cm1 = [(0, 0, 0, 1024), (1, 0, 1024, 1024)]
                mixer_out_proj(kb, kb.EXO, cm1, dr["w_out_o"], hbuf, hB)
            if done("l1_out"):
                return
            if active("l1_ffn"):
                ffn(kb, 1, hbuf, hB)

        run()
        S.barrier()
        S.dma("sp", out, hbuf[64:T, :], reads=allh, writes=[Buf()])
        for name in dump:
            S.dma("sp", dump_out[name], dr[name], writes=[Buf()])
        S.emit()
    return nc


def _consts(rank):
    scc = np.zeros((128, SC_N), np.float64)
    j = np.arange(64)[:, None]
    i = np.arange(64)[None, :]
    for hl in range(2):
        h = 2 * rank + hl
        lg = math.log(1.0 - 2.0 ** (-5.0 - h))
        scc[0:64, SC_INTRA + hl * 64:SC_INTRA + (hl + 1) * 64] = np.exp(np.abs(i - j) * lg) / 16.0
        scc[:, SC_XIB + hl * 64:SC_XIB + (hl + 1) * 64] = (np.exp((np.arange(64) + 1.0) * lg) / 16.0)[None, :]
        scc[0:64, SC_ZETA + hl] = np.exp((63.0 - np.arange(64)) * lg)
        scc[:, SC_GCH + hl] = math.exp(64.0 * lg)
        scc[0:64, SC_MASK + (hl * 2 + 0) * 64:SC_MASK + (hl * 2 + 1) * 64] = (i >= j)
        scc[0:64, SC_MASK + (hl * 2 + 1) * 64:SC_MASK + (hl * 2 + 2) * 64] = (i < j) * (128.0 ** -0.5)
    scc[0:64, SC_TRIU:SC_TRIU + 64] = (i >= j)
    scc[0:64, SC_TRIL:SC_TRIL + 64] = (i < j)
    scc[0:48, SC_KBIAS] = -30000.0
    return scc.astype(np.float32)


def _tables():
    f32 = np.float32
    pos = (np.arange(LSEQ) - 48).astype(f32)
    inv_r = (f32(1.0) / (f32(10000.0) ** np.linspace(0.0, 1.0, 128, dtype=f32))).astype(f32)
    ang = (pos[:, None] * inv_r[None, :]).astype(f32)
    ret_tab = np.concatenate([np.cos(ang), np.sin(ang)], axis=1).astype(f32)
    inv_d = (f32(1.0) / (f32(10000.0) ** (np.arange(0, 128, 2, dtype=f32) / f32(128)))).astype(f32)
    ang = (pos[:, None] * inv_d[None, :]).astype(f32)
    diff_tab = np.concatenate([np.cos(ang), np.sin(ang)], axis=1).astype(f32)
    return ret_tab, diff_tab


def make_in_maps(x, meta, mix_norm_e, w_in_e, gla_w_gate_e, gla_b_gate_e, gla_norm_e, ret_norm_e, w_out_e,
                 mix_norm_o, w_in_o, q_norm_o, k_norm_o, lam_q1_o, lam_k1_o, lam_q2_o, lam_k2_o, diff_norm_o, w_out_o,
                 ffn_norm, w_up, conv_w, conv_b, w_down):
    f = lambda a: np.ascontiguousarray(np.asarray(a, dtype=np.float32))
    x = f(x)
    meta = f(meta)
    ret_tab, diff_tab = _tables()
    shared = {
        "ret_tab": ret_tab, "diff_tab": diff_tab,
        "mix_norm_e": f(mix_norm_e).reshape(1, D), "w_in_e": f(w_in_e)[0],
        "gw2": np.concatenate([f(gla_w_gate_e)[0], f(gla_b_gate_e)[0][None, :]], axis=0),
        "w_out_e": f(w_out_e)[0], "mix_norm_o": f(mix_norm_o).reshape(1, D), "w_in_o": f(w_in_o)[0],
        "qk_norm": np.concatenate([f(q_norm_o)[0], f(k_norm_o)[0]])[None, :],
        "lamv": np.concatenate([f(lam_q1_o)[0], f(lam_k1_o)[0], f(lam_q2_o)[0], f(lam_k2_o)[0]])[None, :],
        "diff_norm": f(diff_norm_o).reshape(1, 256), "w_out_o": f(w_out_o)[0],
        "ffn_norm": f(ffn_norm), "w_up": f(w_up), "conv_w": f(conv_w), "conv_b": f(conv_b), "w_down": f(w_down),
    }
    consts = [_consts(0), _consts(1)]
    gn = f(gla_norm_e)[0]
    rn = f(ret_norm_e)[0]
    in_maps = []
    for c in range(8):
        b, r = c // 2, c % 2
        if r == 0:
            xin = np.concatenate([np.zeros((48, D), np.float32), meta, x[b, 0:2048]], axis=0)
        else:
            xin = x[b, 1984:4096]
        tok = np.arange(17 * 128)
        valid = (tok < T) & ((tok >= 48) if r == 0 else True)
        keep = np.ascontiguousarray(valid.reshape(17, 128).T.astype(np.float32))
        m = dict(shared)
        m["xin"] = np.ascontiguousarray(xin)
        m["keep"] = keep
        m["scc"] = consts[r]
        m["gnorm"] = np.concatenate([gn[2 * r:2 * r + 2].reshape(-1), rn[2 * r:2 * r + 2].reshape(-1)])[None, :]
        in_maps.append(m)
    return in_maps


_NC_CACHE = {}


def kernel(**inputs):
    in_maps = make_in_maps(**inputs)
    if "nc" not in _NC_CACHE:
        _NC_CACHE["nc"] = build()
    res = run_bass_kernel_spmd(_NC_CACHE["nc"], in_maps, core_ids=list(range(8)))
    outp = np.zeros((4, 4096, D), np.float32)
    for c in range(8):
        b, r = c // 2, c % 2
        outp[b, r * 2048:(r + 1) * 2048] = res.results[c]["out"]
    return outp
```

```python
import contextlib
import math
import numpy as np
import concourse.bass as bass
import concourse.mybir as mybir
from concourse.bass_utils import run_bass_kernel_spmd

F32 = mybir.dt.float32
BF16 = mybir.dt.bfloat16
AF = mybir.ActivationFunctionType
ALU = mybir.AluOpType
AX = mybir.AxisListType

D = 2048
T = 2112
NSL = 33
TILES = [(i * 128, min(128, T - i * 128)) for i in range(17)]
SEQC = 65
LSEQ = SEQC * 64
DFF = 5632
EPS = 1e-6
LAM_INIT = 0.8 - 0.6 * math.exp(-0.3 * 1)
NSLOT = 8
NCC = 8


class Buf:
    __slots__ = ("name", "w", "r", "rd", "excl")

    def __init__(self, name="", excl=False):
        self.name = name
        self.w = []
        self.r = {}
        self.rd = []
        self.excl = excl


class Sched:
    ENG = ("pe", "act", "dve", "pool", "sp")
    DMAQ = ("sp", "pool", "act")

    def __init__(self, nc):
        self.nc = nc
        self.prog = {e: [] for e in self.ENG}
        self.cnt = {e: 0 for e in self.ENG}
        self.dcnt = {q: 0 for q in self.DMAQ}
        self.ccnt = 0
        self.seen = {e: {} for e in self.ENG}
        self.pending = {e: {} for e in self.ENG}

    def _semkey(self, ev):
        if ev[0] == "e":
            return ("e", ev[1]), ev[2]
        if ev[0] == "c":
            return ("c", ev[1]), 1
        q, k = ev[1], ev[2]
        return ("d", q, k % NSLOT), 16 * (k // NSLOT + 1)

    def _collect(self, X, reads, writes):
        need = dict(self.pending[X])
        self.pending[X] = {}
        seen = self.seen[X]

        def add(ev):
            if ev is None:
                return
            if ev[0] == "e" and ev[1] == X and X == "pe":
                return
            key, val = self._semkey(ev)
            if seen.get(key, 0) >= val:
                return
            if need.get(key, 0) < val:
                need[key] = val

        excl_reads = [b for b in reads if b.excl]
        for b in reads:
            for ev in b.w:
                add(ev)
        for b in list(writes) + excl_reads:
            for ev in b.w:
                add(ev)
            for ev in b.r.values():
                add(ev)
            for ev in b.rd:
                add(ev)
        out = []
        for key, val in need.items():
            if seen.get(key, 0) < val:
                seen[key] = val
                out.append((key, val))
        return out

    def _mark(self, my, reads, writes, is_dma):
        writes = list(writes) + [b for b in reads if b.excl]
        for b in reads:
            if b.excl:
                continue
            if is_dma:
                b.rd.append(my)
            else:
                b.r[my[1]] = my
        for b in writes:
            b.w = [my]
            b.r = {}
            b.rd = []

    def op(self, eng, fn, reads=(), writes=()):
        waits = self._collect(eng, reads, writes)
        self.cnt[eng] += 1
        my = ("e", eng, self.cnt[eng])
        self.prog[eng].append((waits, fn, ("e", eng), 1))
        self._mark(my, reads, writes, False)
        return my

    def dma(self, q, out_ap, in_ap, reads=(), writes=(), add_writes=(), **kw):
        k = self.dcnt[q]
        waits = self._collect(q, reads, writes)
        if k >= NSLOT:
            key, val = ("d", q, k % NSLOT), 16 * (k // NSLOT)
            if self.seen[q].get(key, 0) < val:
                self.seen[q][key] = val
                waits.append((key, val))
        self.dcnt[q] += 1
        my = ("d", q, k)

        def fn(e):
            o = out_ap(e) if callable(out_ap) else out_ap
            i = in_ap(e) if callable(in_ap) else in_ap
            return e.dma_start(out=o, in_=i, **kw)

        self.prog[q].append((waits, fn, ("d", q, k % NSLOT), 16))
        self._mark(my, reads, writes, True)
        for b in add_writes:
            b.w.append(my)
        return my

    def cc(self, fn, reads=(), writes=()):
        i = self.ccnt
        self.ccnt += 1
        waits = self._collect("pool", reads, writes)
        my = ("c", i)
        self.prog["pool"].append((waits, fn, ("c", i), None))
        self._mark(my, reads, writes, True)
        return my

    def barrier(self):
        allw = {}
        for e in self.ENG:
            if self.cnt[e]:
                allw[("e", e)] = self.cnt[e]
        for q in self.DMAQ:
            n = self.dcnt[q]
            for i in range(min(n, NSLOT)):
                last_k = ((n - 1 - i) // NSLOT) * NSLOT + i
                allw[("d", q, i)] = 16 * (last_k // NSLOT + 1)
        for i in range(self.ccnt):
            allw[("c", i)] = 1
        for e in self.ENG:
            p = self.pending[e]
            for k, v in allw.items():
                if k == ("e", e) and e == "pe":
                    continue
                if p.get(k, 0) < v:
                    p[k] = v

    def emit(self, final_eng="sp"):
        nc = self.nc
        self.barrier()
        with contextlib.ExitStack() as st:
            sems = {}
            for e in self.ENG:
                sems[("e", e)] = st.enter_context(nc.semaphore(f"s_{e}"))
            for q in self.DMAQ:
                for i in range(NSLOT):
                    sems[("d", q, i)] = st.enter_context(nc.semaphore(f"d_{q}{i}"))
            for i in range(self.ccnt):
                sems[("c", i)] = st.enter_context(nc.semaphore(f"c_{i}"))
            fin = [(k, v) for k, v in self.pending[final_eng].items() if self.seen[final_eng].get(k, 0) < v]
            block = st.enter_context(nc.Block())
            prog = self.prog

            def run(engname, eng):
                for waits, fn, inc, amt in prog[engname]:
                    for key, val in waits:
                        eng.wait_ge(sems[key], val)
                    ins = fn(eng)
                    if amt is None:
                        ins.then_inc(sems[inc])
                    else:
                        ins.then_inc(sems[inc], amt)
                if engname == final_eng:
                    for key, val in fin:
                        eng.wait_ge(sems[key], val)

            @block.tensor
            def _(e):
                run("pe", e)

            @block.scalar
            def _(e):
                run("act", e)

            @block.vector
            def _(e):
                run("dve", e)

            @block.gpsimd
            def _(e):
                run("pool", e)

            @block.sync
            def _(e):
                run("sp", e)


class Tl:
    def __init__(self, kb, name, shape, dt, n=1, psum=False):
        self.t = []
        self.b = []
        for i in range(n):
            nm = f"{name}_{kb.uid()}"
            if psum:
                t = kb.st.enter_context(kb.nc.psum_tensor(nm, shape, dt))
            else:
                t = kb.st.enter_context(kb.nc.sbuf_tensor(nm, shape, dt))
            self.t.append(t)
            self.b.append(Buf(nm, excl=psum))
        self.n = n
        self.i = -1

    def next(self):
        self.i = (self.i + 1) % self.n
        return self.t[self.i], self.b[self.i]


class KB:
    def __init__(self, nc):
        self.nc = nc
        self.S = Sched(nc)
        self.st = None
        self._uid = 0
        self._rank = {}
        self.dram = {}
        self.dbuf = {}

    def uid(self):
        self._uid += 1
        return self._uid

    def rank(self, e):
        key = id(e)
        if key not in self._rank:
            self._rank[key] = e.snap(e.partition_id() % 2)
        return self._rank[key]

    def dt_in(self, name, shape, dt=F32):
        self.dram[name] = self.nc.dram_tensor(name, list(shape), dt, kind="ExternalInput").ap()
        return self.dram[name]

    def dt_out(self, name, shape, dt=F32):
        self.dram[name] = self.nc.dram_tensor(name, list(shape), dt, kind="ExternalOutput").ap()
        return self.dram[name]

    def dt_int(self, name, shape, dt=F32):
        self.dram[name] = self.nc.dram_tensor(name, list(shape), dt, kind="Internal").ap()
        return self.dram[name]

    @contextlib.contextmanager
    def phase(self):
        old = self.st
        with contextlib.ExitStack() as st:
            self.st = st
            yield
            self.S.barrier()
        self.st = old


def build_identity(kb, ident_t, ident_b):
    S = kb.S
    S.op("pool", lambda e: e.memset(ident_t[:], 0.0), writes=[ident_b])
    S.op("pool", lambda e: e.affine_select(out=ident_t[:], in_=ident_t[:], pattern=[[-1, 128]], compare_op=ALU.not_equal,
                                           fill=1.0, base=0, channel_multiplier=1), reads=[ident_b], writes=[ident_b])


@contextlib.contextmanager
def subscope(kb):
    old = kb.st
    with contextlib.ExitStack() as st:
        kb.st = st
        yield
        kb.S.barrier()
    kb.st = old


def transpose_rows(kb, src_t, src_bufs, ts, nk, dstT, dst_b, tok0, PT, col0=0, dk0=0, evac="act"):
    S = kb.S
    ident, ident_b = kb.ident, kb.ident_b
    for k0 in range(0, nk, 4):
        kn = min(4, nk - k0)
        pt, pb = PT.next()
        for kk in range(kn):
            k = k0 + kk
            S.op("pe", lambda e, k=k, kk=kk, pt=pt: e.transpose(out=pt[:, kk, :ts], in_=src_t[:ts, col0 + k * 128:col0 + (k + 1) * 128],
                                                              identity=ident[:ts, :ts]),
                 reads=list(src_bufs) + [ident_b], writes=[pb])
        if evac == "act":
            S.op("act", lambda e, k0=k0, kn=kn, pt=pt: e.copy(out=dstT[:, dk0 + k0:dk0 + k0 + kn, tok0:tok0 + ts], in_=pt[:, :kn, :ts]),
                 reads=[pb], writes=[dst_b])
        else:
            S.op("dve", lambda e, k0=k0, kn=kn, pt=pt: e.tensor_copy(out=dstT[:, dk0 + k0:dk0 + k0 + kn, tok0:tok0 + ts], in_=pt[:, :kn, :ts]),
                 reads=[pb], writes=[dst_b])


def rstd_inplace(kb, ms_ap, msb, rows, scale):
    S = kb.S
    S.op("act", lambda e: e.activation(out=ms_ap, in_=ms_ap, func=AF.Ln, scale=scale, bias=kb.epsb[:rows, :]), reads=[msb, kb.epsb_b], writes=[msb])
    S.op("act", lambda e: e.activation(out=ms_ap, in_=ms_ap, func=AF.Exp, scale=-0.5), reads=[msb], writes=[msb])


def norm_T(kb, h_ap, hB, gain_row_ap, XT, XTb):
    S = kb.S
    with subscope(kb):
        G = Tl(kb, "ng", [128, D], F32)
        H = Tl(kb, "nh", [128, D], F32, n=3)
        J = Tl(kb, "nj", [128, D], F32)
        XN = Tl(kb, "nx", [128, D], BF16, n=3)
        MS = Tl(kb, "nms", [128, 1], F32, n=3)

        PT = Tl(kb, "npt", [128, 4, 128], BF16, n=2, psum=True)
        S.dma("sp", G.t[0][:], gain_row_ap.partition_broadcast(128), writes=[G.b[0]])
        def stage_a(ti):
            t0, ts = TILES[ti]
            ht, hb = H.next()
            S.dma("sp", ht[:ts, :], h_ap[t0:t0 + ts, :], reads=hB[ti], writes=[hb])
            ms, msb = MS.next()
            S.op("act", lambda e, ht=ht, ms=ms, ts=ts: e.activation(out=J.t[0][:ts, :], in_=ht[:ts, :], func=AF.Square, accum_out=ms[:ts, :]),
                 reads=[hb], writes=[J.b[0], msb])
            rstd_inplace(kb, ms[:ts, :], msb, ts, 1.0 / D)
            xn, xnb = XN.next()
            S.op("dve", lambda e, ht=ht, ms=ms, xn=xn, ts=ts: e.scalar_tensor_tensor(out=xn[:ts, :], in0=ht[:ts, :], scalar=ms[:ts, 0:1], in1=G.t[0][:ts, :],
                                                                                    op0=ALU.mult, op1=ALU.mult),
                 reads=[hb, msb, G.b[0]], writes=[xnb])
            return xn, xnb

        def stage_b(ti, xn, xnb):
            t0, ts = TILES[ti]
            transpose_rows(kb, xn, [xnb], ts, 16, XT, XTb, t0, PT, evac="act" if ti % 2 == 0 else "dve")

        pend = stage_a(0)
        for ti in range(1, len(TILES)):
            nxt = stage_a(ti)
            stage_b(ti - 1, *pend)
            pend = nxt
        stage_b(len(TILES) - 1, *pend)


def gemm_tm(kb, KC, lhsT_prep, wblocks, epilogue, wbufs=2, pbufs=4, tiles=TILES, after_block=None):
    S = kb.S
    with subscope(kb):
        W = Tl(kb, "gw", [128, KC, 512], BF16, n=wbufs)
        PS = Tl(kb, "gp", [128, 512], F32, n=pbufs, psum=True)
        wt_list = []

        def load(bi):
            wt, wb = W.next()
            blk = wblocks[bi]
            for pi, (src_ap, off, wd) in enumerate(blk["pieces"]):
                S.dma("pool", wt[:, :, off:off + wd], src_ap.rearrange("(k p) c -> p k c", p=128),
                      writes=[wb] if pi == 0 else [], add_writes=[] if pi == 0 else [wb])
            wt_list.append((wt, wb))

        load(0)
        for bi, blk in enumerate(wblocks):
            if bi + 1 < len(wblocks):
                load(bi + 1)
            wt, wb = wt_list[bi]
            n = blk["ncols"]
            for ti, (t0, ts) in enumerate(tiles):
                lfn, lbufs = lhsT_prep(bi, ti, t0, ts)
                ps, pb = PS.next()
                for k in range(KC):
                    S.op("pe", lambda e, ps=ps, lap=lfn(k), wt=wt, k=k, n=n, ts=ts: e.matmul(ps[:ts, :n], lhsT=lap, rhs=wt[:, k, :n],
                                                                                          start=(k == 0), stop=(k == KC - 1)),
                         reads=list(lbufs) + [wb], writes=[pb])
                epilogue(bi, ti, t0, ts, ps, pb, n)
            if after_block is not None:
                after_block(bi)


def resident_lhsT(XT, XTb):
    def prep(bi, ti, t0, ts):
        return (lambda k: XT[:, k, t0:t0 + ts]), [XTb]
    return prep


def residual_epilogue(kb, h_ap, hB, HS):
    S = kb.S

    def ep(bi, ti, t0, ts, ps, pb, n):
        c0 = bi * 512
        hs, hsb = HS.next()
        S.dma("sp", hs[:ts, :n], h_ap[t0:t0 + ts, c0:c0 + n], reads=[hB[ti][bi]], writes=[hsb])
        S.op("dve", lambda e: e.scalar_tensor_tensor(out=hs[:ts, :n], in0=ps[:ts, :n], scalar=kb.keep[:ts, ti:ti + 1], in1=hs[:ts, :n],
                                                     op0=ALU.mult, op1=ALU.add),
             reads=[pb, hsb, kb.keep_b], writes=[hsb])
        S.dma("sp", h_ap[t0:t0 + ts, c0:c0 + n], hs[:ts, :n], reads=[hsb], writes=[hB[ti][bi]])

    return ep


GROUPS = [[0, 1], [2, 3], [4, 5], [6, 7]]


class Exchange:
    def __init__(self, kb, name, ncols, dt, rpg):
        self.kb, self.ncols, self.rpg = kb, ncols, rpg
        self.P = kb.dt_int(name + "_p", [2 * T, ncols], dt)
        self.P3 = self.P.rearrange("(d t) c -> d t c", d=2)
        self.groups = [(j0, min(rpg, T - j0)) for j0 in range(0, T, rpg)]
        self.G = [kb.dt_int(f"{name}_g{j}", [2, 2 * rows, ncols], dt) for j, (j0, rows) in enumerate(self.groups)]
        self.Gb = [[Buf(), Buf()] for _ in self.groups]
        self.MY = kb.dt_int(name + "_my", [2 * T, ncols], dt)
        self.MY3 = self.MY.rearrange("(s t) c -> s t c", s=2)
        self.MYb = [Buf() for _ in self.groups]
        self.Pb = [[], []]

    def wbuf(self, d):
        b_ = Buf()
        self.Pb[d].append(b_)
        return b_

    def gather_d(self, d):
        S = self.kb.S
        for j, (j0, rows) in enumerate(self.groups):
            src_ap = self.P3[d, j0:j0 + rows, :]
            dst_ap = self.G[j][d]
            S.cc(lambda e, src_ap=src_ap, dst_ap=dst_ap: e.collective_compute("AllGather", ALU.bypass, replica_groups=[list(g) for g in GROUPS],
                                                                              ins=[src_ap.opt()], outs=[dst_ap.opt()]),
                 reads=self.Pb[d], writes=[self.Gb[j][d]])
        self.Pb[d] = []

    def finish(self):
        S = self.kb.S
        kb = self.kb
        for j, (j0, rows) in enumerate(self.groups):
            G = self.G[j]
            S.dma("pool", self.MY3[:, j0:j0 + rows, :],
                  (lambda e, G=G: G[bass.ds(kb.rank(e), 1), :, :].rearrange("o (s t) c -> (o s) t c", s=2)),
                  reads=self.Gb[j], writes=[self.MYb[j]])

    def gather(self):
        self.gather_d(0)
        self.gather_d(1)
        self.finish()

    def read(self, src, t0, n, c0, wd):
        j = t0 // self.rpg
        j0, rows = self.groups[j]
        assert t0 + n <= j0 + rows
        return self.MY3[src, t0:t0 + n, c0:c0 + wd], [self.MYb[j]]


def mixer_out_proj(kb, EX, colmap, w_ap, h_ap, hB):
    S = kb.S
    with kb.phase():
        XT = Tl(kb, "mx", [128, 16, T], BF16)
        with subscope(kb):
            O = Tl(kb, "go", [128, D], BF16, n=3)
            PT = Tl(kb, "gpt", [128, 4, 128], BF16, n=2, psum=True)
            for ti, (t0, ts) in enumerate(TILES):
                ot, ob = O.next()
                for pi, (src, c_src, c_dst, wd) in enumerate(colmap):
                    apf, gb = EX.read(src, t0, ts, c_src, wd)
                    S.dma("sp", ot[:ts, c_dst:c_dst + wd], apf,
                          reads=gb, writes=[ob] if pi == 0 else [], add_writes=[] if pi == 0 else [ob])
                transpose_rows(kb, ot, [ob], ts, 16, XT.t[0], XT.b[0], t0, PT)
        HS = Tl(kb, "mh", [128, 512], F32, n=3)
        wblocks = [{"pieces": [(w_ap[:, c0:c0 + 512], 0, 512)], "ncols": 512} for c0 in range(0, D, 512)]
        gemm_tm(kb, 16, resident_lhsT(XT.t[0], XT.b[0]), wblocks, residual_epilogue(kb, h_ap, hB, HS))


def ffn(kb, layer, h_ap, hB):
    S = kb.S
    dr = kb.dram
    aT = dr["aT"]
    aTb = [Buf() for _ in range(44)]
    with kb.phase():
        XT = Tl(kb, "fx", [128, 16, T], BF16)
        norm_T(kb, h_ap, hB, dr["ffn_norm"][layer:layer + 1, :], XT.t[0], XT.b[0])
        W = Tl(kb, "fw", [128, 16, 2, 256], BF16, n=2)
        CW = Tl(kb, "fcw", [128, 3, 88], F32)
        CB = Tl(kb, "fcb", [128, 88], F32)
        U = Tl(kb, "fu", [128, 2, T + 2], F32, n=2)
        C = Tl(kb, "fc", [128, 2, T], F32, n=2)
        A = Tl(kb, "fa", [128, T], BF16, n=2)
        PS = Tl(kb, "fp", [128, 512], F32, n=4, psum=True)
        S.dma("sp", CW.t[0][:], dr["conv_w"][layer].rearrange("t (c p) -> p t c", p=128), writes=[CW.b[0]], allow_slow_non_contiguous=True)
        S.dma("sp", CB.t[0][:], dr["conv_b"][layer:layer + 1, :].rearrange("o (c p) -> p (o c)", p=128), writes=[CB.b[0]], allow_slow_non_contiguous=True)
        for i in range(2):
            S.op("pool", lambda e, i=i: e.memset(U.t[i][:, :, 0:2], 0.0), writes=[U.b[i]])
        w_up = dr["w_up"][layer]
        NB = 22
        wl = []

        def loadw(b):
            wt, wb = W.next()
            for vg in range(2):
                c0 = vg * DFF + b * 256
                S.dma("pool", wt[:, :, vg, :], w_up[:, c0:c0 + 256].rearrange("(k p) c -> p k c", p=128),
                      writes=[wb] if vg == 0 else [], add_writes=[] if vg == 0 else [wb])
            wl.append((wt, wb))

        tslices = [(s0, min(512, T - s0)) for s0 in range(0, T, 512)]
        loadw(0)
        for b in range(NB):
            if b + 1 < NB:
                loadw(b + 1)
            wt, wb = wl[b]
            for j in range(2):
                fc = b * 2 + j
                ut, ub = U.next()
                for vg in range(2):
                    for (s0, sn) in tslices:
                        ps, pb = PS.next()
                        for k in range(16):
                            S.op("pe", lambda e, ps=ps, wt=wt, k=k, vg=vg, j=j, s0=s0, sn=sn: e.matmul(
                                ps[:, :sn], lhsT=wt[:, k, vg, j * 128:(j + 1) * 128], rhs=XT.t[0][:, k, s0:s0 + sn], start=(k == 0), stop=(k == 15)),
                                reads=[wb, XT.b[0]], writes=[pb])
                        S.op("act", lambda e, ps=ps, ut=ut, vg=vg, s0=s0, sn=sn: e.copy(out=ut[:, vg, 2 + s0:2 + s0 + sn], in_=ps[:, :sn]),
                             reads=[pb], writes=[ub])
                ct, cb = C.next()
                for vg in range(2):
                    ch = vg * 44 + fc
                    S.op("act", lambda e, ut=ut, ct=ct, vg=vg, ch=ch: e.activation(out=ct[:, vg, :], in_=ut[:, vg, 0:T], func=AF.Identity,
                                                                               scale=CW.t[0][:, 0, ch:ch + 1], bias=CB.t[0][:, ch:ch + 1]),
                         reads=[ub, CW.b[0], CB.b[0]], writes=[cb])
                    for tap in (1, 2):
                        S.op("dve", lambda e, ut=ut, ct=ct, vg=vg, ch=ch, tap=tap: e.scalar_tensor_tensor(
                            out=ct[:, vg, :], in0=ut[:, vg, tap:tap + T], scalar=CW.t[0][:, tap, ch:ch + 1], in1=ct[:, vg, :], op0=ALU.mult, op1=ALU.add),
                            reads=[ub, cb, CW.b[0]], writes=[cb])
                S.op("act", lambda e, ct=ct: e.activation(out=ct[:, 1, :], in_=ct[:, 1, :], func=AF.Silu), reads=[cb], writes=[cb])
                at, ab = A.next()
                S.op("dve", lambda e, ct=ct, at=at: e.tensor_tensor(out=at[:], in0=ct[:, 0, :], in1=ct[:, 1, :], op=ALU.mult), reads=[cb], writes=[ab])
                S.dma("sp", aT[fc * 128:(fc + 1) * 128, :], at[:], reads=[ab], writes=[aTb[fc]])
    with kb.phase():
        HS = Tl(kb, "dh", [128, 512], F32, n=3)
        AT = Tl(kb, "da", [128, 44, 256], BF16, n=2)
        w_down = dr["w_down"][layer]
        wblocks = [{"pieces": [(w_down[:, c0:c0 + 512], 0, 512)], "ncols": 512} for c0 in range(0, D, 512)]
        aT3 = aT.rearrange("(k p) t -> p k t", p=128)
        loaded = {}
        order = [(bi, g) for bi in range(len(wblocks)) for g in range((len(TILES) + 1) // 2)]

        def load_group(key):
            if key in loaded:
                return
            at, ab = AT.next()
            g0 = key[1] * 256
            gn = min(256, T - g0)
            S.dma("sp", at[:, :, :gn], aT3[:, :, g0:g0 + gn], reads=aTb, writes=[ab])
            loaded[key] = (at, ab)

        def prep(bi, ti, t0, ts):
            g = ti // 2
            if ti % 2 == 0:
                load_group((bi, g))
                nxt = order.index((bi, g)) + 1
                if nxt < len(order):
                    load_group(order[nxt])
            at, ab = loaded[(bi, g)]
            off = t0 - g * 256
            return (lambda k: at[:, k, off:off + ts]), [ab]

        gemm_tm(kb, 44, prep, wblocks, residual_epilogue(kb, h_ap, hB, HS))
C_QA, C_KA, C_VA, C_GA, C_QB, C_KB, C_VB, C_GB = 0, 256, 512, 1024, 1536, 2048, 2560, 3072
NP0 = 3584
DBG_L0 = {"la", "gemm", "ag"}
SC_INTRA, SC_XIB, SC_ZETA, SC_GCH, SC_MASK, SC_TRIU, SC_TRIL, SC_KBIAS, SC_N = 0, 128, 256, 258, 260, 516, 580, 644, 648


def stage_store(kb, ST, ps, pb, ts, n, dst_ap, dst_bufs, eng="act"):
    S = kb.S
    stt, stb = ST.next()
    if eng == "act":
        S.op("act", lambda e: e.copy(out=stt[:ts, :n], in_=ps[:ts, :n]), reads=[pb], writes=[stb])
    else:
        S.op("dve", lambda e: e.tensor_copy(out=stt[:ts, :n], in_=ps[:ts, :n]), reads=[pb], writes=[stb])
    S.dma("sp", dst_ap, stt[:ts, :n], reads=[stb], writes=dst_bufs)


def l0_proj(kb, h_ap, hB):
    S = kb.S
    dr = kb.dram
    w = dr["w_in_e"]
    P0 = kb.EX0.P3
    LA0 = kb.EXL.P3
    with kb.phase():
        XT = Tl(kb, "px", [128, 16, T], BF16)
        norm_T(kb, h_ap, hB, dr["mix_norm_e"], XT.t[0], XT.b[0])
        with subscope(kb) if "la" in DBG_L0 else contextlib.nullcontext():
          if "la" in DBG_L0:
              WL = Tl(kb, "wl", [128, 16, 16], BF16)
              LR = Tl(kb, "lr", [17, T], F32)
              LRH = Tl(kb, "lrh", [17, T], BF16)
              LRL = Tl(kb, "lrl", [17, T], BF16)
              W2 = Tl(kb, "w2", [17, 512], F32)
              W2H = Tl(kb, "w2h", [17, 512], BF16)
              W2L = Tl(kb, "w2l", [17, 512], BF16)
              PL = Tl(kb, "pl", [128, 512], F32, n=2, psum=True)
              E1 = Tl(kb, "e1", [128, 512], F32, n=2)
              S.dma("pool", WL.t[0][:], w[:, 3072:3088].rearrange("(k p) c -> p k c", p=128), writes=[WL.b[0]])
              S.dma("sp", W2.t[0][:], dr["gw2"], writes=[W2.b[0]])
              S.op("dve", lambda e: e.memset(LR.t[0][:], 1.0), writes=[LR.b[0]])
              S.op("dve", lambda e: e.tensor_copy(out=W2H.t[0][:], in_=W2.t[0][:]), reads=[W2.b[0]], writes=[W2H.b[0]])
              S.op("dve", lambda e: e.tensor_tensor(out=W2L.t[0][:], in0=W2.t[0][:], in1=W2H.t[0][:], op=ALU.subtract), reads=[W2.b[0], W2H.b[0]], writes=[W2L.b[0]])
              for s0 in range(0, T, 512):
                  sn = min(512, T - s0)
                  ps, pb = PL.next()
                  for k in range(16):
                      S.op("pe", lambda e, ps=ps, k=k, s0=s0, sn=sn: e.matmul(ps[:16, :sn], lhsT=WL.t[0][:, k, :], rhs=XT.t[0][:, k, s0:s0 + sn], start=(k == 0), stop=(k == 15)),
                           reads=[WL.b[0], XT.b[0]], writes=[pb])
                  S.op("act", lambda e, ps=ps, s0=s0, sn=sn: e.copy(out=LR.t[0][0:16, s0:s0 + sn], in_=ps[:16, :sn]), reads=[pb], writes=[LR.b[0]])
              S.op("dve", lambda e: e.tensor_copy(out=LRH.t[0][:], in_=LR.t[0][:]), reads=[LR.b[0]], writes=[LRH.b[0]])
              S.op("dve", lambda e: e.tensor_tensor(out=LRL.t[0][:], in0=LR.t[0][:], in1=LRH.t[0][:], op=ALU.subtract), reads=[LR.b[0], LRH.b[0]], writes=[LRL.b[0]])
              for ti, (t0, ts) in enumerate(TILES):
                  ps, pb = PL.next()
                  combos = [(LRH, W2H), (LRL, W2H), (LRH, W2L)]
                  for ci, (a, b_) in enumerate(combos):
                      S.op("pe", lambda e, ps=ps, a=a, b_=b_, ci=ci, t0=t0, ts=ts: e.matmul(ps[:ts, :], lhsT=a.t[0][:, t0:t0 + ts], rhs=b_.t[0][:, :], start=(ci == 0), stop=(ci == 2)),
                           reads=[a.b[0], b_.b[0]], writes=[pb])
                  e1, e1b = E1.next()
                  S.op("act", lambda e, ps=ps, e1=e1, ts=ts: e.activation(out=e1[:ts, :], in_=ps[:ts, :], func=AF.Exp, scale=-1.0), reads=[pb], writes=[e1b])
                  S.op("act", lambda e, e1=e1, ts=ts: e.activation(out=e1[:ts, :], in_=e1[:ts, :], func=AF.Ln, bias=kb.oneb[:ts, :], scale=1.0), reads=[e1b, kb.oneb_b], writes=[e1b])
                  S.op("dve", lambda e, e1=e1, ts=ts, ti=ti: e.tensor_scalar(out=e1[:ts, :], in0=e1[:ts, :], scalar1=kb.keep[:ts, ti:ti + 1], scalar2=-1.0 / 16.0,
                                                                          op0=ALU.mult, op1=ALU.mult), reads=[e1b, kb.keep_b], writes=[e1b])
                  for dst in range(2):
                      b_ = kb.EXL.wbuf(dst)
                      S.dma("sp", LA0[dst, t0:t0 + ts, :], e1[:ts, dst * 256:(dst + 1) * 256], reads=[e1b], writes=[b_])
        wblocks = []
        for dst in range(2):
            wblocks.append({"pieces": [(w[:, dst * 256:dst * 256 + 256], 0, 256), (w[:, 512 + dst * 256:512 + dst * 256 + 256], 256, 256)],
                            "ncols": 512, "dst": dst, "dcol": 0})
            for base, dcol in ((1024, C_VA), (2048, C_GA), (3088, C_QB), (4112, C_KB), (5136, C_VB), (6160, C_GB)):
                c0 = base + dst * 512
                wblocks.append({"pieces": [(w[:, c0:c0 + 512], 0, 512)], "ncols": 512, "dst": dst, "dcol": dcol})
        ST = Tl(kb, "pst", [128, 512], BF16, n=3)
        RTAB = Tl(kb, "prt", [128, 17, 256], F32)
        S.dma("sp", RTAB.t[0][:, 0:16, :], dr["rtab"][0:2048, :].rearrange("(i p) c -> p i c", p=128), writes=[RTAB.b[0]])
        S.dma("sp", RTAB.t[0][0:64, 16, :], dr["rtab"][2048:T, :], add_writes=[RTAB.b[0]])
        RA = Tl(kb, "pra", [128, 2, 2, 128], F32, n=2)
        RB = Tl(kb, "prb", [128, 2, 2, 128], F32, n=2)

        def ep(bi, ti, t0, ts, ps, pb, n):
            blk = wblocks[bi]
            b_ = kb.EX0.wbuf(blk["dst"])
            dst_ap = P0[blk["dst"], t0:t0 + ts, blk["dcol"]:blk["dcol"] + n]
            if blk["dcol"] in (C_QB, C_KB):
                x = ps[:ts, :].rearrange("p (g f d) -> p g f d", g=2, f=2)
                cosb = RTAB.t[0][:ts, ti, 0:128].unsqueeze(1).to_broadcast([ts, 2, 128])
                sinb = RTAB.t[0][:ts, ti, 128:256].unsqueeze(1).to_broadcast([ts, 2, 128])
                ra, rab = RA.next()
                rb, rbb = RB.next()
                S.op("dve", lambda e: e.tensor_tensor(out=ra[:ts, :, 0, :], in0=x[:, :, 0, :], in1=cosb, op=ALU.mult), reads=[pb, RTAB.b[0]], writes=[rab])
                S.op("dve", lambda e: e.tensor_tensor(out=rb[:ts, :, 0, :], in0=x[:, :, 1, :], in1=sinb, op=ALU.mult), reads=[pb, RTAB.b[0]], writes=[rbb])
                S.op("dve", lambda e: e.tensor_tensor(out=ra[:ts, :, 1, :], in0=x[:, :, 1, :], in1=cosb, op=ALU.mult), reads=[pb, RTAB.b[0], rab], writes=[rab])
                S.op("dve", lambda e: e.tensor_tensor(out=rb[:ts, :, 1, :], in0=x[:, :, 0, :], in1=sinb, op=ALU.mult), reads=[pb, RTAB.b[0], rbb], writes=[rbb])
                stt, stb = ST.next()
                so = stt[:ts, :].rearrange("p (g f d) -> p g f d", g=2, f=2)
                S.op("pool", lambda e: e.tensor_tensor(out=so[:, :, 0, :], in0=ra[:ts, :, 0, :], in1=rb[:ts, :, 0, :], op=ALU.subtract), reads=[rab, rbb], writes=[stb])
                S.op("pool", lambda e: e.tensor_tensor(out=so[:, :, 1, :], in0=ra[:ts, :, 1, :], in1=rb[:ts, :, 1, :], op=ALU.add), reads=[rab, rbb, stb], writes=[stb])
                S.dma("sp", dst_ap, stt[:ts, :n], reads=[stb], writes=[b_])
            else:
                stage_store(kb, ST, ps, pb, ts, n, dst_ap, [b_], eng="act")

        if "gemm" in DBG_L0:
            kb.EXL.gather()

            def after_block(bi):
                if bi == 6:
                    kb.EX0.gather_d(0)

            gemm_tm(kb, 16, resident_lhsT(XT.t[0], XT.b[0]), wblocks, ep, after_block=after_block)
            kb.EX0.gather_d(1)
            kb.EX0.finish()


def rms_gate_store(kb, OPt, OPb, nh, hd, rows, GN, GNb, gate_fn, OUT, out_cb, W):
    S = kb.S
    sq, sqb = W["SQ"].next()
    S.op("act", lambda e: e.activation(out=sq[:rows, :nh * hd], in_=OPt, func=AF.Square), reads=[OPb], writes=[sqb])
    ms, msb = W["MS"].next()
    S.op("dve", lambda e: e.tensor_reduce(out=ms[:rows, :nh], in_=sq[:rows, :nh * hd].rearrange("p (h d) -> p h d", h=nh), axis=AX.X, op=ALU.add),
         reads=[sqb], writes=[msb])
    rstd_inplace(kb, ms[:rows, :nh], msb, rows, 1.0 / hd)
    on, onb = W["ON"].next()
    S.op("dve", lambda e: e.tensor_tensor(out=on[:rows, :nh * hd].rearrange("p (h d) -> p h d", h=nh), in0=OPt.rearrange("p (h d) -> p h d", h=nh) if len(OPt.shape) == 2 else OPt,
                                          in1=ms[:rows, :nh].unsqueeze(2).to_broadcast([rows, nh, hd]), op=ALU.mult),
         reads=[OPb, msb], writes=[onb])
    S.op("pool", lambda e: e.tensor_tensor(out=on[:rows, :nh * hd], in0=on[:rows, :nh * hd], in1=GN[:rows, :nh * hd], op=ALU.mult),
         reads=[onb, GNb], writes=[onb])
    ot, ob = OUT.next()
    if gate_fn is not None:
        sg, sgb = gate_fn()
        S.op("dve", lambda e: e.tensor_tensor(out=ot[:rows, :nh * hd], in0=on[:rows, :nh * hd], in1=sg[:rows, :nh * hd], op=ALU.mult),
             reads=[onb, sgb], writes=[ob])
    else:
        S.op("dve", lambda e: e.tensor_copy(out=ot[:rows, :nh * hd], in_=on[:rows, :nh * hd]), reads=[onb], writes=[ob])
    out_cb(ot, ob)


def l0_scan(kb):
    S = kb.S
    dr = kb.dram
    O1 = kb.EXO.P3
    SCL = 128.0 ** -0.5
    with kb.phase():
        SC = Tl(kb, "sc", [128, SC_N], F32)
        GN = Tl(kb, "sgn", [64, 1024], F32)
        TRI = Tl(kb, "tri", [64, 2, 64], BF16)
        ONE = Tl(kb, "one", [64, 1], BF16)
        Sg = Tl(kb, "Sg", [128, 2, 256], F32)
        Sgb = Tl(kb, "Sgb", [128, 2, 256], BF16)
        Sr = Tl(kb, "Sr", [128, 2, 2, 256], F32)
        Srb = Tl(kb, "Srb", [128, 2, 2, 256], BF16)
        S.dma("sp", SC.t[0][:], dr["scc"], writes=[SC.b[0]])
        S.dma("sp", GN.t[0][:], dr["gnorm"].partition_broadcast(64), writes=[GN.b[0]])
        S.op("dve", lambda e: e.tensor_copy(out=TRI.t[0][:], in_=SC.t[0][0:64, SC_TRIU:SC_TRIU + 128].rearrange("p (a b) -> p a b", a=2)), reads=[SC.b[0]], writes=[TRI.b[0]])
        S.op("dve", lambda e: e.memset(ONE.t[0][:], 1.0), writes=[ONE.b[0]])
        S.op("dve", lambda e: e.memset(Sg.t[0][:], 0.0), writes=[Sg.b[0]])
        S.op("dve", lambda e: e.memset(Sgb.t[0][:], 0.0), writes=[Sgb.b[0]])
        S.op("pool", lambda e: e.memset(Sr.t[0][:], 0.0), writes=[Sr.b[0]])
        S.op("pool", lambda e: e.memset(Srb.t[0][:], 0.0), writes=[Srb.b[0]])
        sc = SC.t[0]
        scb = SC.b[0]
        Dt = Tl(kb, "sD", [64, NP0], BF16, n=3)
        LAt = Tl(kb, "sLA", [64, 256], F32, n=3)
        LH = Tl(kb, "sLH", [64, 2, 256], BF16, n=2)
        EX = Tl(kb, "sEX", [64, 3, 256], F32, n=2)
        DEC = Tl(kb, "sDEC", [128, 2], F32, n=2)
        GPa = Tl(kb, "sGPa", [64, 3, 256], BF16, n=2)
        GPb = Tl(kb, "sGPb", [64, 2, 256], BF16, n=2)
        TT = Tl(kb, "sTT", [128, 8, 64], BF16, n=2)
        STt = Tl(kb, "sST", [64, 256], BF16, n=2)
        KBr = Tl(kb, "sKB", [64, 2, 256], BF16, n=2)
        TR = Tl(kb, "sTR", [128, 8, 64], BF16, n=2)
        TQ = Tl(kb, "sTQ", [128, 4, 64], BF16, n=2)
        STr = Tl(kb, "sSTr", [64, 2, 64], BF16, n=2)
        SG = Tl(kb, "sSG", [64, 1024], F32, n=2)
        OUT = Tl(kb, "sOUT", [64, 1024], BF16, n=2)
        OEV = Tl(kb, "sOEV", [64, 1024], F32, n=2)
        Wk = {"SQ": Tl(kb, "sSQ", [64, 1024], F32), "MS": Tl(kb, "sMS", [64, 4], F32, n=2), "ON": Tl(kb, "sON", [64, 1024], F32)}
        cumP = Tl(kb, "pcum", [64, 2, 256], F32, psum=True)
        MISC = Tl(kb, "pmisc", [128, 512], F32, psum=True)
        TPB = Tl(kb, "ptp", [128, 16, 64], BF16, psum=True)
        OP = Tl(kb, "pop", [64, 4, 256], F32, psum=True)
        SPs = Tl(kb, "psps", [128, 2, 256], F32, n=3, psum=True)
        misc = MISC.t[0]
        spB = sprB = decB = MISC.b[0]
        tpB = [TPB.b[0], TPB.b[0]]
        ident, ident_b = kb.ident, kb.ident_b

        def stage_a(c):
                src, slot = (0, c) if c <= 32 else (1, c - 32)
                r0 = slot * 64
                dt_, db = Dt.next()
                apf, gb = kb.EX0.read(src, r0, 64, 0, NP0)
                S.dma("sp", dt_[:], apf, reads=gb, writes=[db])
                lat, lab = LAt.next()
                apf, gb = kb.EXL.read(src, r0, 64, 0, 256)
                S.dma("sp", lat[:], apf, reads=gb, writes=[lab])
                lh, lhb = LH.next()
                S.op("dve", lambda e, lh=lh, lat=lat: e.tensor_copy(out=lh[:, 0, :], in_=lat[:]), reads=[lab], writes=[lhb])
                S.op("dve", lambda e, lh=lh, lat=lat: e.tensor_tensor(out=lh[:, 1, :], in0=lat[:], in1=lh[:, 0, :], op=ALU.subtract), reads=[lab, lhb], writes=[lhb])
                cp, cpb = cumP.next()
                for w_ in range(2):
                    for hl in range(2):
                        S.op("pe", lambda e, cp=cp, lh=lh, w_=w_, hl=hl: e.matmul(cp[:, w_, :], lhsT=TRI.t[0][:, w_, :], rhs=lh[:, hl, :], start=(hl == 0), stop=(hl == 1)),
                             reads=[TRI.b[0], lhb], writes=[cpb])
                dp, dpb = misc[:, 384:386], decB
                for h in range(2):
                    for hl in range(2):
                        S.op("pe", lambda e, dp=dp, lh=lh, h=h, hl=hl: e.matmul(dp[:, h:h + 1], lhsT=lh[:, hl, h * 128:(h + 1) * 128], rhs=ONE.t[0][:, :], start=(hl == 0), stop=(hl == 1)),
                             reads=[ONE.b[0], lhb], writes=[dpb])
                ex, exb = EX.next()
                S.op("act", lambda e, ex=ex, cp=cp: e.activation(out=ex[:, 0, :], in_=cp[:, 0, :], func=AF.Exp), reads=[cpb], writes=[exb])
                S.op("act", lambda e, ex=ex, cp=cp: e.activation(out=ex[:, 1, :], in_=cp[:, 0, :], func=AF.Exp, scale=-1.0), reads=[cpb], writes=[exb])
                S.op("act", lambda e, ex=ex, cp=cp: e.activation(out=ex[:, 2, :], in_=cp[:, 1, :], func=AF.Exp), reads=[cpb], writes=[exb])
                dec, decb = DEC.next()
                S.op("act", lambda e, dec=dec, dp=dp: e.activation(out=dec[:], in_=dp, func=AF.Exp), reads=[dpb], writes=[decb])
                ga_, gab = GPa.next()
                gb_, gbb = GPb.next()
                q_ap = dt_[:, C_QA:C_QA + 256]
                k_ap = dt_[:, C_KA:C_KA + 256]
                S.op("dve", lambda e, ga_=ga_, ex=ex, q_ap=q_ap: e.scalar_tensor_tensor(out=ga_[:, 0, :], in0=q_ap, scalar=SCL, in1=ex[:, 0, :], op0=ALU.mult, op1=ALU.mult), reads=[db, exb], writes=[gab])
                S.op("pool", lambda e, gb_=gb_, ex=ex, q_ap=q_ap: e.tensor_tensor(out=gb_[:, 0, :], in0=q_ap, in1=ex[:, 1, :], op=ALU.mult), reads=[db, exb], writes=[gbb])
                S.op("dve", lambda e, ga_=ga_, ex=ex, k_ap=k_ap: e.tensor_tensor(out=ga_[:, 1, :], in0=k_ap, in1=ex[:, 1, :], op=ALU.mult), reads=[db, exb, gab], writes=[gab])
                S.op("pool", lambda e, gb_=gb_, ex=ex, k_ap=k_ap: e.tensor_tensor(out=gb_[:, 1, :], in0=k_ap, in1=ex[:, 0, :], op=ALU.mult), reads=[db, exb, gbb], writes=[gbb])
                S.op("dve", lambda e, ga_=ga_, ex=ex, k_ap=k_ap: e.tensor_tensor(out=ga_[:, 2, :], in0=k_ap, in1=ex[:, 2, :], op=ALU.mult), reads=[db, exb, gab], writes=[gab])
                tp, tpb = TPB.t[0][:, 0:8, :], tpB[0]
                srcs = [(ga_, 0, gab), (gb_, 0, gbb), (ga_, 1, gab), (gb_, 1, gbb)]
                for it in range(4):
                    st_, si, sb_ = srcs[it]
                    for h in range(2):
                        S.op("pe", lambda e, tp=tp, st_=st_, si=si, it=it, h=h: e.transpose(out=tp[:, it * 2 + h, :], in_=st_[:, si, h * 128:(h + 1) * 128], identity=ident[:64, :64]),
                             reads=[sb_, ident_b], writes=[tpb])
                tt, ttb = TT.next()
                S.op("act", lambda e, tt=tt, tp=tp: e.copy(out=tt[:], in_=tp), reads=[tpb], writes=[ttb])
                rr = dt_[:, C_QB:C_QB + 1024].rearrange("p (g f d) -> p g f d", g=4, f=2)
                rrb = db
                kbr, kbrb = KBr.next()
                S.op("dve", lambda e, kbr=kbr, rr=rr: e.tensor_tensor(out=kbr[:], in0=rr[:, 2:4, :, :].rearrange("p g f d -> p g (f d)"),
                                                                      in1=sc[0:64, SC_ZETA:SC_ZETA + 2].unsqueeze(2).to_broadcast([64, 2, 256]), op=ALU.mult),
                     reads=[rrb, scb], writes=[kbrb])
                tp2, tp2b = TPB.t[0][:, 8:16, :], tpB[1]
                for g in range(4):
                    for f in range(2):
                        S.op("pe", lambda e, tp2=tp2, rr=rr, g=g, f=f: e.transpose(out=tp2[:, g * 2 + f, :], in_=rr[:, g, f, :], identity=ident[:64, :64]),
                             reads=[rrb, ident_b], writes=[tp2b])
                tr, trb = TR.next()
                S.op("act", lambda e, tr=tr, tp2=tp2: e.copy(out=tr[:], in_=tp2), reads=[tp2b], writes=[trb])
                tq, tqb = TQ.next()
                S.op("dve", lambda e, tq=tq, tp2=tp2: e.tensor_tensor(out=tq[:].rearrange("p (h f) i -> p h f i", h=2), in0=tp2[:, 0:4, :].rearrange("p (h f) i -> p h f i", h=2),
                                                                      in1=sc[:, SC_XIB:SC_XIB + 128].rearrange("p (h i) -> p h i", h=2).unsqueeze(2).to_broadcast([128, 2, 2, 64]), op=ALU.mult),
                     reads=[tp2b, scb], writes=[tqb])
                sg, sgb = SG.next()
                S.op("act", lambda e, sg=sg, dt_=dt_: e.activation(out=sg[:, 0:512], in_=dt_[:, C_GA:C_GA + 512], func=AF.Exp, scale=-1.0), reads=[db], writes=[sgb])
                S.op("act", lambda e, sg=sg, dt_=dt_: e.activation(out=sg[:, 512:1024], in_=dt_[:, C_GB:C_GB + 512], func=AF.Exp, scale=-1.0), reads=[db, sgb], writes=[sgb])
                S.op("act", lambda e, sg=sg: e.activation(out=sg[:, :], in_=sg[:, :], func=AF.Ln, bias=kb.oneb[:64, :], scale=1.0), reads=[sgb, kb.oneb_b], writes=[sgb])
                S.op("act", lambda e, sg=sg: e.activation(out=sg[:, :], in_=sg[:, :], func=AF.Exp, scale=-1.0), reads=[sgb], writes=[sgb])
                S.op("pool", lambda e, sg=sg, dt_=dt_: e.tensor_tensor(out=sg[:, 0:512], in0=sg[:, 0:512], in1=dt_[:, C_GA:C_GA + 512], op=ALU.mult), reads=[db, sgb], writes=[sgb])
                S.op("pool", lambda e, sg=sg, dt_=dt_: e.tensor_tensor(out=sg[:, 512:1024], in0=sg[:, 512:1024], in1=dt_[:, C_GB:C_GB + 512], op=ALU.mult), reads=[db, sgb], writes=[sgb])

                return dict(c=c, dt_=dt_, db=db, dec=dec, decb=decb, ga_=ga_, gab=gab, tt=tt, ttb=ttb, kbr=kbr, kbrb=kbrb, tr=tr, trb=trb, tq=tq, tqb=tqb, sg=sg, sgb=sgb)

        def stage_b(Lc):
                c, dt_, db, dec, decb, ga_, gab, tt, ttb = Lc['c'], Lc['dt_'], Lc['db'], Lc['dec'], Lc['decb'], Lc['ga_'], Lc['gab'], Lc['tt'], Lc['ttb']
                kbr, kbrb, tr, trb, tq, tqb, sg, sgb = Lc['kbr'], Lc['kbrb'], Lc['tr'], Lc['trb'], Lc['tq'], Lc['tqb'], Lc['sg'], Lc['sgb']
                sp_, spb = misc[0:64, 0:256], spB
                for h in range(2):
                    S.op("pe", lambda e, sp_=sp_, tt=tt, h=h: e.matmul(sp_[:, (h * 2 + 0) * 64:(h * 2 + 1) * 64], lhsT=tt[:, 2 * 2 + h, :], rhs=tt[:, 0 * 2 + h, :], start=True, stop=True),
                         reads=[ttb], writes=[spb])
                    S.op("pe", lambda e, sp_=sp_, tt=tt, h=h: e.matmul(sp_[:, (h * 2 + 1) * 64:(h * 2 + 2) * 64], lhsT=tt[:, 3 * 2 + h, :], rhs=tt[:, 1 * 2 + h, :], start=True, stop=True),
                         reads=[ttb], writes=[spb])
                stt, stb = STt.next()
                S.op("dve", lambda e, stt=stt, sp_=sp_: e.tensor_tensor(out=stt[:], in0=sp_, in1=sc[0:64, SC_MASK:SC_MASK + 256], op=ALU.mult), reads=[spb, scb], writes=[stb])
                op_, opb = OP.next()
                for h in range(2):
                    v_ap = dt_[:, C_VA + h * 256:C_VA + (h + 1) * 256]
                    S.op("pe", lambda e, op_=op_, stt=stt, h=h, v_ap=v_ap: e.matmul(op_[:, h, :], lhsT=stt[:, (h * 2) * 64:(h * 2 + 1) * 64], rhs=v_ap, start=True, stop=False),
                         reads=[stb, db], writes=[opb])
                    S.op("pe", lambda e, op_=op_, stt=stt, h=h, v_ap=v_ap: e.matmul(op_[:, h, :], lhsT=stt[:, (h * 2 + 1) * 64:(h * 2 + 2) * 64], rhs=v_ap, start=False, stop=False),
                         reads=[stb, db], writes=[opb])
                    S.op("pe", lambda e, op_=op_, tt=tt, h=h: e.matmul(op_[:, h, :], lhsT=tt[:, 0 * 2 + h, :], rhs=Sgb.t[0][:, h, :], start=False, stop=True),
                         reads=[ttb, Sgb.b[0]], writes=[opb])
                ss, ssb = SPs.next()
                for h in range(2):
                    v_ap = dt_[:, C_VA + h * 256:C_VA + (h + 1) * 256]
                    S.op("pe", lambda e, ss=ss, ga_=ga_, h=h, v_ap=v_ap: e.matmul(ss[:, h, :], lhsT=ga_[:, 2, h * 128:(h + 1) * 128], rhs=v_ap, start=True, stop=True),
                         reads=[gab, db], writes=[ssb])
                for h in range(2):
                    S.op("dve", lambda e, ss=ss, dec=dec, h=h: e.scalar_tensor_tensor(out=Sg.t[0][:, h, :], in0=Sg.t[0][:, h, :], scalar=dec[:, h:h + 1], in1=ss[:, h, :], op0=ALU.mult, op1=ALU.add),
                         reads=[ssb, decb, Sg.b[0]], writes=[Sg.b[0]])
                S.op("pool", lambda e: e.tensor_copy(out=Sgb.t[0][:], in_=Sg.t[0][:]), reads=[Sg.b[0]], writes=[Sgb.b[0]])
                spr, sprb = misc[0:64, 256:384], sprB
                for h in range(2):
                    for f in range(2):
                        S.op("pe", lambda e, spr=spr, tr=tr, h=h, f=f: e.matmul(spr[:, h * 64:(h + 1) * 64], lhsT=tr[:, 4 + h * 2 + f, :], rhs=tr[:, h * 2 + f, :], start=(f == 0), stop=(f == 1)),
                             reads=[trb], writes=[sprb])
                strt, strb = STr.next()
                S.op("dve", lambda e, strt=strt, spr=spr: e.tensor_tensor(out=strt[:].rearrange("p h i -> p (h i)"), in0=spr, in1=sc[0:64, SC_INTRA:SC_INTRA + 128], op=ALU.mult),
                     reads=[sprb, scb], writes=[strb])
                for h in range(2):
                    v_ap = dt_[:, C_VB + h * 256:C_VB + (h + 1) * 256]
                    S.op("pe", lambda e, op_=op_, strt=strt, h=h, v_ap=v_ap: e.matmul(op_[:, 2 + h, :], lhsT=strt[:, h, :], rhs=v_ap, start=True, stop=False),
                         reads=[strb, db], writes=[opb])
                    for f in range(2):
                        S.op("pe", lambda e, op_=op_, tq=tq, h=h, f=f: e.matmul(op_[:, 2 + h, :], lhsT=tq[:, h * 2 + f, :], rhs=Srb.t[0][:, h, f, :], start=False, stop=(f == 1)),
                             reads=[tqb, Srb.b[0]], writes=[opb])
                for h in range(2):
                    v_ap = dt_[:, C_VB + h * 256:C_VB + (h + 1) * 256]
                    ss, ssb = SPs.next()
                    for f in range(2):
                        S.op("pe", lambda e, ss=ss, kbr=kbr, h=h, f=f, v_ap=v_ap: e.matmul(ss[:, f, :], lhsT=kbr[:, h, f * 128:(f + 1) * 128], rhs=v_ap, start=True, stop=True),
                             reads=[kbrb, db], writes=[ssb])
                    S.op("dve", lambda e, ss=ss, h=h: e.scalar_tensor_tensor(out=Sr.t[0][:, h, :, :], in0=Sr.t[0][:, h, :, :], scalar=sc[:, SC_GCH + h:SC_GCH + h + 1], in1=ss[:],
                                                                            op0=ALU.mult, op1=ALU.add),
                         reads=[ssb, scb, Sr.b[0]], writes=[Sr.b[0]])
                S.op("pool", lambda e: e.tensor_copy(out=Srb.t[0][:], in_=Sr.t[0][:]), reads=[Sr.b[0]], writes=[Srb.b[0]])
                def out_cb(ot, ob, c=c):
                    if c <= 32:
                        S.dma("sp", O1[0, c * 64:(c + 1) * 64, :], ot[:, :], reads=[ob], writes=[kb.EXO.wbuf(0)])
                    if c >= 32:
                        S.dma("sp", O1[1, (c - 32) * 64:(c - 31) * 64, :], ot[:, :], reads=[ob], writes=[kb.EXO.wbuf(1)])

                oe, oeb = OEV.next()
                S.op("act", lambda e, oe=oe, op_=op_: e.copy(out=oe[:], in_=op_[:].rearrange("p h d -> p (h d)")), reads=[opb], writes=[oeb])
                rms_gate_store(kb, oe[:], oeb, 4, 256, 64, GN.t[0], GN.b[0], lambda sg=sg, sgb=sgb: (sg, sgb), OUT, out_cb, Wk)

        pend = stage_a(0)
        for c in range(1, SEQC):
            nxt = stage_a(c)
            stage_b(pend)
            if pend["c"] == 32:
                kb.EXO.gather_d(0)
            pend = nxt
        stage_b(pend)
        kb.EXO.gather_d(1)
        kb.EXO.finish()
NP1 = 3072


def l1_proj(kb, h_ap, hB):
    dr = kb.dram
    w = dr["w_in_o"]
    P1 = kb.EX1.P3
    with kb.phase():
        XT = Tl(kb, "px", [128, 16, T], BF16)
        norm_T(kb, h_ap, hB, dr["mix_norm_o"], XT.t[0], XT.b[0])
        wblocks = []
        for dst in range(2):
            for base, dcol in ((0, 0), (2048, 1024), (4096, 2048)):
                for j in range(2):
                    c0 = base + dst * 1024 + j * 512
                    wblocks.append({"pieces": [(w[:, c0:c0 + 512], 0, 512)], "ncols": 512, "dst": dst, "dcol": dcol + j * 512})
        ST = Tl(kb, "pst", [128, 512], BF16, n=3)
        S_ = kb.S
        DTAB = Tl(kb, "pdt", [128, 17, 128], F32)
        S_.dma("sp", DTAB.t[0][:, 0:16, :], dr["dtab"][0:2048, :].rearrange("(i p) c -> p i c", p=128), writes=[DTAB.b[0]])
        S_.dma("sp", DTAB.t[0][0:64, 16, :], dr["dtab"][2048:T, :], add_writes=[DTAB.b[0]])
        GQK = Tl(kb, "pgqk", [128, 2, 128], F32)
        S_.dma("sp", GQK.t[0][:].rearrange("p a b -> p (a b)"), dr["qk_norm"].partition_broadcast(128), writes=[GQK.b[0]])
        SQ = Tl(kb, "psq", [128, 512], F32, n=2)
        MS4 = Tl(kb, "pms4", [128, 4], F32, n=3)
        XN = Tl(kb, "pxn", [128, 4, 128], F32, n=2)
        RA = Tl(kb, "pra", [128, 4, 2, 64], F32, n=2)
        RB = Tl(kb, "prb", [128, 4, 2, 64], F32, n=2)

        def ep(bi, ti, t0, ts, ps, pb, n):
            S = S_
            blk = wblocks[bi]
            b_ = kb.EX1.wbuf(blk["dst"])
            dst_ap = P1[blk["dst"], t0:t0 + ts, blk["dcol"]:blk["dcol"] + n]
            if blk["dcol"] < 2048:
                qk = 0 if blk["dcol"] < 1024 else 1
                sq, sqb = SQ.next()
                S.op("act", lambda e: e.activation(out=sq[:ts, :], in_=ps[:ts, :], func=AF.Square), reads=[pb], writes=[sqb])
                ms, msb = MS4.next()
                S.op("dve", lambda e: e.tensor_reduce(out=ms[:ts, :], in_=sq[:ts, :].rearrange("p (g d) -> p g d", g=4), axis=AX.X, op=ALU.add), reads=[sqb], writes=[msb])
                rstd_inplace(kb, ms[:ts, :], msb, ts, 1.0 / 128)
                xn, xnb = XN.next()
                S.op("dve", lambda e: e.tensor_tensor(out=xn[:ts], in0=ps[:ts, :].rearrange("p (g d) -> p g d", g=4), in1=ms[:ts, :].unsqueeze(2).to_broadcast([ts, 4, 128]), op=ALU.mult),
                     reads=[pb, msb], writes=[xnb])
                S.op("pool", lambda e: e.tensor_tensor(out=xn[:ts], in0=xn[:ts], in1=GQK.t[0][:ts, qk, :].unsqueeze(1).to_broadcast([ts, 4, 128]), op=ALU.mult),
                     reads=[xnb, GQK.b[0]], writes=[xnb])
                cosb = DTAB.t[0][:ts, ti, 0:64].unsqueeze(1).to_broadcast([ts, 4, 64])
                sinb = DTAB.t[0][:ts, ti, 64:128].unsqueeze(1).to_broadcast([ts, 4, 64])
                x1 = xn[:ts, :, 0:64]
                x2 = xn[:ts, :, 64:128]
                ra, rab = RA.next()
                rb, rbb = RB.next()
                S.op("dve", lambda e: e.tensor_tensor(out=ra[:ts, :, 0, :], in0=x1, in1=cosb, op=ALU.mult), reads=[xnb, DTAB.b[0]], writes=[rab])
                S.op("pool", lambda e: e.tensor_tensor(out=rb[:ts, :, 0, :], in0=x2, in1=sinb, op=ALU.mult), reads=[xnb, DTAB.b[0]], writes=[rbb])
                S.op("dve", lambda e: e.tensor_tensor(out=ra[:ts, :, 1, :], in0=x2, in1=cosb, op=ALU.mult), reads=[xnb, DTAB.b[0], rab], writes=[rab])
                S.op("pool", lambda e: e.tensor_tensor(out=rb[:ts, :, 1, :], in0=x1, in1=sinb, op=ALU.mult), reads=[xnb, DTAB.b[0], rbb], writes=[rbb])
                stt, stb = ST.next()
                so = stt[:ts, :].rearrange("p (g d) -> p g d", g=4)
                S.op("dve", lambda e: e.tensor_tensor(out=so[:, :, 0:64], in0=ra[:ts, :, 0, :], in1=rb[:ts, :, 0, :], op=ALU.subtract), reads=[rab, rbb], writes=[stb])
                S.op("pool", lambda e: e.tensor_tensor(out=so[:, :, 64:128], in0=ra[:ts, :, 1, :], in1=rb[:ts, :, 1, :], op=ALU.add), reads=[rab, rbb, stb], writes=[stb])
                S.dma("sp", dst_ap, stt[:ts, :n], reads=[stb], writes=[b_])
            else:
                stage_store(kb, ST, ps, pb, ts, n, dst_ap, [b_], eng="act")

        def after_block(bi):
            if bi == 5:
                kb.EX1.gather_d(0)

        gemm_tm(kb, 16, resident_lhsT(XT.t[0], XT.b[0]), wblocks, ep, after_block=after_block)
        kb.EX1.gather_d(1)
        kb.EX1.finish()


def seq_rows(n0, n):
    out = []
    c0, c1 = n0 // 64, (n0 + n) // 64
    c = c0
    while c < c1:
        if c <= 32:
            ce = min(c1, 33)
            out.append((0, c * 64, (ce - c) * 64, (c - c0) * 64))
        else:
            ce = c1
            out.append((1, (c - 32) * 64, (ce - c) * 64, (c - c0) * 64))
        c = ce
    return out


def l1_attn(kb):
    S = kb.S
    dr = kb.dram
    O2 = kb.EXO.P3
    SCALE = 128.0 ** -0.5
    NTT = 33
    with kb.phase():
        GD = Tl(kb, "aGD", [128, 256], F32)
        LV = Tl(kb, "aLV", [128, 4, 128], F32)
        LS = Tl(kb, "aLS", [128, 4], F32)
        NLAM = Tl(kb, "aNL", [128, 1], F32)
        KBI = Tl(kb, "aKB", [128, 1], F32)
        ZB = Tl(kb, "aZB", [128, 1], F32)
        S.dma("sp", GD.t[0][:], dr["diff_norm"].partition_broadcast(128), writes=[GD.b[0]])
        S.op("dve", lambda e: e.tensor_scalar(out=GD.t[0][:], in0=GD.t[0][:], scalar1=1.0 - LAM_INIT, scalar2=None, op0=ALU.mult), reads=[GD.b[0]], writes=[GD.b[0]])
        S.dma("sp", LV.t[0][:].rearrange("p a b -> p (a b)"), dr["lamv"].partition_broadcast(128), writes=[LV.b[0]])
        S.op("dve", lambda e: e.tensor_tensor(out=LV.t[0][:, 0, :], in0=LV.t[0][:, 0, :], in1=LV.t[0][:, 1, :], op=ALU.mult), reads=[LV.b[0]], writes=[LV.b[0]])
        S.op("dve", lambda e: e.tensor_tensor(out=LV.t[0][:, 2, :], in0=LV.t[0][:, 2, :], in1=LV.t[0][:, 3, :], op=ALU.mult), reads=[LV.b[0]], writes=[LV.b[0]])
        S.op("dve", lambda e: e.tensor_reduce(out=LS.t[0][:], in_=LV.t[0][:], axis=AX.X, op=ALU.add), reads=[LV.b[0]], writes=[LS.b[0]])
        S.op("act", lambda e: e.activation(out=LS.t[0][:], in_=LS.t[0][:], func=AF.Exp), reads=[LS.b[0]], writes=[LS.b[0]])
        S.op("dve", lambda e: e.tensor_tensor(out=NLAM.t[0][:], in0=LS.t[0][:, 2:3], in1=LS.t[0][:, 0:1], op=ALU.subtract), reads=[LS.b[0]], writes=[NLAM.b[0]])
        S.op("dve", lambda e: e.tensor_scalar(out=NLAM.t[0][:], in0=NLAM.t[0][:], scalar1=-LAM_INIT, scalar2=None, op0=ALU.add), reads=[NLAM.b[0]], writes=[NLAM.b[0]])
        S.dma("sp", KBI.t[0][:], dr["scc"][:, SC_KBIAS:SC_KBIAS + 1], writes=[KBI.b[0]], allow_slow_non_contiguous=True)
        S.op("dve", lambda e: e.memset(ZB.t[0][:], 0.0), writes=[ZB.b[0]])

        QKT = Tl(kb, "aQKT", [128, 4, LSEQ], BF16, n=2)
        VA = Tl(kb, "aVA", [128, NTT, 257], BF16, n=2)
        for i in range(2):
            S.op("pool", lambda e, i=i: e.memset(VA.t[i][:, :, 256:257], 1.0), writes=[VA.b[i]])
        X = Tl(kb, "aX", [128, 768], BF16, n=3)
        PTr = Tl(kb, "aPTr", [128, 4, 128], BF16, n=2, psum=True)
        PT = Tl(kb, "aPT", [128, 512], BF16, n=5)
        SPS = Tl(kb, "aSPS", [128, 512], F32, n=2, psum=True)
        OPS = Tl(kb, "aOPS", [128, 512], F32, n=4, psum=True)
        O1s = Tl(kb, "aO1s", [128, 4, 256], F32)
        O1sb = [Buf() for _ in range(4)]
        RD = Tl(kb, "aRD", [128, 1], F32, n=6)
        OC = Tl(kb, "aOC", [128, 256], F32, n=3)
        OUT = Tl(kb, "aOUT", [128, 256], BF16, n=3)
        Wk = {"SQ": Tl(kb, "aSQ2", [128, 256], F32), "MS": Tl(kb, "aMS", [128, 1], F32, n=2), "ON": Tl(kb, "aON", [128, 256], F32)}

        def prep_gen(h):
            qkt, qktb = QKT.t[h % 2], QKT.b[h % 2]
            va, vab = VA.t[h % 2], VA.b[h % 2]
            for tt in range(NTT):
                n0 = tt * 128
                ts = min(128, LSEQ - n0)
                xt, xb = X.next()
                first = True
                for (src, r0, nr, doff) in seq_rows(n0, ts):
                    for ci, cbase in enumerate((0, 1024, 2048)):
                        apf, gb = kb.EX1.read(src, r0, nr, cbase + h * 256, 256)
                        S.dma("sp", xt[doff:doff + nr, ci * 256:(ci + 1) * 256], apf,
                              reads=gb, writes=[xb] if first else [], add_writes=[] if first else [xb])
                        first = False
                S.op("pool", lambda e, va=va, xt=xt, tt=tt, ts=ts: e.tensor_copy(out=va[:ts, tt, 0:256], in_=xt[:ts, 512:768]), reads=[xb], writes=[vab])
                transpose_rows(kb, xt, [xb], ts, 4, qkt, qktb, n0, PTr, evac="dve")
                yield

        def chunk_store(ot, ob, h, a, rows):
            for cc in range(rows // 64):
                c = a // 64 + cc
                dsts = ([(0, c)] if c <= 32 else []) + ([(1, c - 32)] if c >= 32 else [])
                for (d_, sl) in dsts:
                    S.dma("sp", O2[d_, sl * 64:(sl + 1) * 64, h * 256:(h + 1) * 256], ot[cc * 64:(cc + 1) * 64, :], reads=[ob], writes=[kb.EXO.wbuf(d_)])

        def attention(h, bg):
            qkt, qktb = QKT.t[h % 2], QKT.b[h % 2]
            va, vab = VA.t[h % 2], VA.b[h % 2]
            for q0 in range(0, LSEQ, 512):
                qn = min(512, LSEQ - q0)
                subs = [(a, min(128, q0 + qn - a)) for a in range(q0, q0 + qn, 128)]
                for p in range(2):
                    ops = [OPS.next() for _ in subs]
                    tiles = [t for t in range(NTT) if t * 128 < q0 + qn]

                    def stage1(t):
                        k0 = t * 128
                        kn = min(128, LSEQ - k0)
                        vstart = max(q0, k0)
                        n = q0 + qn - vstart
                        sp_, spb = SPS.next()
                        S.op("pe", lambda e, sp_=sp_, k0=k0, kn=kn, vstart=vstart, n=n, p=p: e.matmul(sp_[:kn, :n], lhsT=qkt[:, 2 + p, k0:k0 + kn], rhs=qkt[:, p, vstart:vstart + n],
                                                                                             start=True, stop=True), reads=[qktb], writes=[spb])
                        pt, ptb = PT.next()
                        bias_t, bias_b = (KBI, KBI.b[0]) if t == 0 else (ZB, ZB.b[0])
                        S.op("act", lambda e, pt=pt, sp_=sp_, kn=kn, n=n, bias_t=bias_t: e.activation(out=pt[:kn, :n], in_=sp_[:kn, :n], func=AF.Exp, scale=SCALE, bias=bias_t.t[0][:kn, :]),
                             reads=[spb, bias_b], writes=[ptb])
                        if k0 >= q0 and kn == 128:
                            S.op("pool", lambda e, pt=pt: e.memset(pt[64:128, 0:64], 0.0), reads=[ptb], writes=[ptb])
                        return (t, kn, vstart, pt, ptb)

                    def finish(m):
                        a, rows = subs[m]
                        op_, opb = ops[m]
                        rd, rdb = RD.next()
                        S.op("dve", lambda e, rd=rd, op_=op_, rows=rows: e.reciprocal(out=rd[:rows, :], in_=op_[:rows, 256:257]), reads=[opb], writes=[rdb])
                        if p == 0:
                            S.op("dve", lambda e, rd=rd, op_=op_, rows=rows, m=m: e.tensor_scalar(out=O1s.t[0][:rows, m, :], in0=op_[:rows, 0:256], scalar1=rd[:rows, 0:1], scalar2=None, op0=ALU.mult),
                                 reads=[opb, rdb], writes=[O1sb[m]])
                        else:
                            oc, ocb = OC.next()
                            S.op("dve", lambda e, rd=rd, op_=op_, rows=rows, oc=oc: e.tensor_scalar(out=oc[:rows, :], in0=op_[:rows, 0:256], scalar1=rd[:rows, 0:1], scalar2=NLAM.t[0][:rows, 0:1],
                                                                                                op0=ALU.mult, op1=ALU.mult), reads=[opb, rdb, NLAM.b[0]], writes=[ocb])
                            S.op("pool", lambda e, rows=rows, oc=oc, m=m: e.tensor_tensor(out=oc[:rows, :], in0=oc[:rows, :], in1=O1s.t[0][:rows, m, :], op=ALU.add),
                                 reads=[ocb, O1sb[m]], writes=[ocb])
                            rms_gate_store(kb, oc[:rows, :], ocb, 1, 256, rows, GD.t[0], GD.b[0], None, OUT,
                                           lambda ot, ob, a=a, rows=rows: chunk_store(ot, ob, h, a, rows), Wk)

                    def stage2(st):
                        t, kn, vstart, pt, ptb = st
                        for m, (a, rows) in enumerate(subs):
                            if a < vstart:
                                continue
                            rel = a - vstart
                            op_, opb = ops[m]
                            last = (t == a // 128)
                            S.op("pe", lambda e, op_=op_, pt=pt, kn=kn, rel=rel, rows=rows, t=t, last=last: e.matmul(op_[:rows, 0:257], lhsT=pt[:kn, rel:rel + rows], rhs=va[:kn, t, :],
                                                                                                                 start=(t == 0), stop=last), reads=[ptb, vab], writes=[opb])
                            if last:
                                finish(m)

                    SKEW = 2
                    pend = []
                    for t in tiles:
                        pend.append(stage1(t))
                        if len(pend) > SKEW:
                            stage2(pend.pop(0))
                    while pend:
                        stage2(pend.pop(0))
                    if bg is not None:
                        for _ in range(2):
                            next(bg, None)
            if bg is not None:
                for _ in bg:
                    pass

        g0 = prep_gen(0)
        for _ in g0:
            pass
        for h in range(4):
            bg = prep_gen(h + 1) if h + 1 < 4 else None
            attention(h, bg)
INPUT_SPECS = [
    ("xin", [T, D]), ("keep", [128, 17]), ("scc", [128, SC_N]), ("gnorm", [1, 1024]), ("rtab", [T, 256]), ("dtab", [T, 128]),
    ("mix_norm_e", [1, D]), ("w_in_e", [D, 7184]), ("gw2", [17, 512]), ("w_out_e", [D, D]),
    ("mix_norm_o", [1, D]), ("w_in_o", [D, 6144]), ("qk_norm", [1, 256]), ("lamv", [1, 512]), ("diff_norm", [1, 256]), ("w_out_o", [D, D]),
    ("ffn_norm", [2, D]), ("w_up", [2, D, 2 * DFF]), ("conv_w", [2, 3, 2 * DFF]), ("conv_b", [2, 2 * DFF]), ("w_down", [2, DFF, D]),
]
PHASES = ["none", "l0_proj", "l0_scan", "l0_out", "l0_ffn", "l1_proj", "l1_attn", "l1_out", "l1_ffn"]


def build(stop_after=None, dump=(), start_at=None):
    nc = bass.Bass("TRN2", target_bir_lowering=False)
    kb = KB(nc)
    S = kb.S
    for name, shape in INPUT_SPECS:
        kb.dt_in(name, shape)
    out = kb.dt_out("out", [2048, D])
    hbuf = kb.dt_int("hbuf", [T, D])
    kb.dt_int("aT", [DFF, T], BF16)
    kb.EX0 = Exchange(kb, "ex0", NP0, BF16, 256)
    kb.EXL = Exchange(kb, "exl", 256, F32, 1024)
    kb.EXO = Exchange(kb, "exo", 1024, BF16, 1024)
    kb.EX1 = Exchange(kb, "ex1", NP1, BF16, 256)
    kb.dram["P0"] = kb.EX0.P
    kb.dram["LA0"] = kb.EXL.P
    kb.dram["O1"] = kb.EXO.P
    kb.dram["P1"] = kb.EX1.P
    dr = kb.dram
    dump_out = {}
    for name in dump:
        src = dr[name]
        dump_out[name] = nc.dram_tensor("dbg_" + name, list(src.shape), src.dtype, kind="ExternalOutput").ap()
    with contextlib.ExitStack() as st:
        kb.st = st
        ident = Tl(kb, "ident", [128, 128], BF16)
        kb.ident, kb.ident_b = ident.t[0], ident.b[0]
        build_identity(kb, kb.ident, kb.ident_b)
        keep = Tl(kb, "keep", [128, 17], F32)
        kb.keep, kb.keep_b = keep.t[0], keep.b[0]
        S.dma("sp", kb.keep[:], dr["keep"], writes=[kb.keep_b])
        epsb = Tl(kb, "epsb", [128, 1], F32)
        kb.epsb, kb.epsb_b = epsb.t[0], epsb.b[0]
        S.op("dve", lambda e: e.memset(kb.epsb[:], EPS), writes=[kb.epsb_b])
        oneb = Tl(kb, "oneb", [128, 1], F32)
        kb.oneb, kb.oneb_b = oneb.t[0], oneb.b[0]
        S.op("dve", lambda e: e.memset(kb.oneb[:], 1.0), writes=[kb.oneb_b])
        hB = [[Buf() for _ in range(4)] for _ in TILES]
        allh = [b for row in hB for b in row]
        S.dma("sp", hbuf, dr["xin"], writes=allh)
        S.barrier()

        def done(phase):
            return stop_after is not None and PHASES.index(phase) >= PHASES.index(stop_after)

        def active(phase):
            return start_at is None or PHASES.index(phase) >= PHASES.index(start_at)

        def run():
            if done("none"):
                return
            if active("l0_proj"):
                l0_proj(kb, hbuf, hB)
            if done("l0_proj"):
                return
            if active("l0_scan"):
                l0_scan(kb)
            if done("l0_scan"):
                return
            if active("l0_out"):
                cm0 = [(0, 0, 0, 512), (0, 512, 1024, 512), (1, 0, 512, 512), (1, 512, 1536, 512)]
                mixer_out_proj(kb, kb.EXO, cm0, dr["w_out_e"], hbuf, hB)
            if done("l0_out"):
                return
            if active("l0_ffn"):
                ffn(kb, 0, hbuf, hB)
            if done("l0_ffn"):
                return
            if active("l1_proj"):
                l1_proj(kb, hbuf, hB)
            if done("l1_proj"):
                return
            if active("l1_attn"):
                l1_attn(kb)
                kb.EXO.gather()
            if done("l1_attn"):
                return
            if active("l1_out"):
                cm1 = [(0, 0, 0, 1024), (1, 0, 1024, 1024)]
                mixer_out_proj(kb, kb.EXO, cm1, dr["w_out_o"], hbuf, hB)
            if done("l1_out"):
                return
            if active("l1_ffn"):
                ffn(kb, 1, hbuf, hB)

        run()
        S.barrier()
        S.dma("sp", out, hbuf[64:T, :], reads=allh, writes=[Buf()])
        for name in dump:
            S.dma("sp", dump_out[name], dr[name], writes=[Buf()])
        S.emit()
    return nc


def _consts(rank):
    scc = np.zeros((128, SC_N), np.float64)
    j = np.arange(64)[:, None]
    i = np.arange(64)[None, :]
    for hl in range(2):
        h = 2 * rank + hl
        lg = math.log(1.0 - 2.0 ** (-5.0 - h))
        scc[0:64, SC_INTRA + hl * 64:SC_INTRA + (hl + 1) * 64] = np.exp(np.abs(i - j) * lg) / 16.0
        scc[:, SC_XIB + hl * 64:SC_XIB + (hl + 1) * 64] = (np.exp((np.arange(64) + 1.0) * lg) / 16.0)[None, :]
        scc[0:64, SC_ZETA + hl] = np.exp((63.0 - np.arange(64)) * lg)
        scc[:, SC_GCH + hl] = math.exp(64.0 * lg)
        scc[0:64, SC_MASK + (hl * 2 + 0) * 64:SC_MASK + (hl * 2 + 1) * 64] = (i >= j)
        scc[0:64, SC_MASK + (hl * 2 + 1) * 64:SC_MASK + (hl * 2 + 2) * 64] = (i < j) * (128.0 ** -0.5)
    scc[0:64, SC_TRIU:SC_TRIU + 64] = (i >= j)
    scc[0:64, SC_TRIL:SC_TRIL + 64] = (i < j)
    scc[0:48, SC_KBIAS] = -30000.0
    return scc.astype(np.float32)


def _tables():
    f32 = np.float32
    pos = (np.arange(LSEQ) - 48).astype(f32)
    inv_r = (f32(1.0) / (f32(10000.0) ** np.linspace(0.0, 1.0, 128, dtype=f32))).astype(f32)
    ang = (pos[:, None] * inv_r[None, :]).astype(f32)
    ret_tab = np.concatenate([np.cos(ang), np.sin(ang)], axis=1).astype(f32)
    inv_d = (f32(1.0) / (f32(10000.0) ** (np.arange(0, 128, 2, dtype=f32) / f32(128)))).astype(f32)
    ang = (pos[:, None] * inv_d[None, :]).astype(f32)
    diff_tab = np.concatenate([np.cos(ang), np.sin(ang)], axis=1).astype(f32)
    return ret_tab, diff_tab


def make_in_maps(x, meta, mix_norm_e, w_in_e, gla_w_gate_e, gla_b_gate_e, gla_norm_e, ret_norm_e, w_out_e,
                 mix_norm_o, w_in_o, q_norm_o, k_norm_o, lam_q1_o, lam_k1_o, lam_q2_o, lam_k2_o, diff_norm_o, w_out_o,
                 ffn_norm, w_up, conv_w, conv_b, w_down):
    f = lambda a: np.ascontiguousarray(np.asarray(a, dtype=np.float32))
    x = f(x)
    meta = f(meta)
    ret_tab, diff_tab = _tables()
    shared = {
        "mix_norm_e": f(mix_norm_e).reshape(1, D), "w_in_e": f(w_in_e)[0],
        "gw2": np.concatenate([f(gla_w_gate_e)[0], f(gla_b_gate_e)[0][None, :]], axis=0),
        "w_out_e": f(w_out_e)[0], "mix_norm_o": f(mix_norm_o).reshape(1, D), "w_in_o": f(w_in_o)[0],
        "qk_norm": np.concatenate([f(q_norm_o)[0], f(k_norm_o)[0]])[None, :],
        "lamv": np.concatenate([f(lam_q1_o)[0], f(lam_k1_o)[0], f(lam_q2_o)[0], f(lam_k2_o)[0]])[None, :],
        "diff_norm": f(diff_norm_o).reshape(1, 256), "w_out_o": f(w_out_o)[0],
        "ffn_norm": f(ffn_norm), "w_up": f(w_up), "conv_w": f(conv_w), "conv_b": f(conv_b), "w_down": f(w_down),
    }
    consts = [_consts(0), _consts(1)]
    gn = f(gla_norm_e)[0]
    rn = f(ret_norm_e)[0]
    in_maps = []
    for c in range(8):
        b, r = c // 2, c % 2
        if r == 0:
            xin = np.concatenate([np.zeros((48, D), np.float32), meta, x[b, 0:2048]], axis=0)
        else:
            xin = x[b, 1984:4096]
        tok = np.arange(17 * 128)
        valid = (tok < T) & ((tok >= 48) if r == 0 else True)
        keep = np.ascontiguousarray(valid.reshape(17, 128).T.astype(np.float32))
        m = dict(shared)
        m["xin"] = np.ascontiguousarray(xin)
        m["keep"] = keep
        m["scc"] = consts[r]
        lo = 0 if r == 0 else 2048
        m["rtab"] = np.ascontiguousarray(ret_tab[lo:lo + T])
        m["dtab"] = np.ascontiguousarray(diff_tab[lo:lo + T])
        m["gnorm"] = np.concatenate([gn[2 * r:2 * r + 2].reshape(-1), rn[2 * r:2 * r + 2].reshape(-1)])[None, :]
        in_maps.append(m)
    return in_maps


_NC_CACHE = {}


def kernel(**inputs):
    in_maps = make_in_maps(**inputs)
    if "nc" not in _NC_CACHE:
        _NC_CACHE["nc"] = build()
    res = run_bass_kernel_spmd(_NC_CACHE["nc"], in_maps, core_ids=list(range(8)))
    outp = np.zeros((4, 4096, D), np.float32)
    for c in range(8):
        b, r = c // 2, c % 2
        outp[b, r * 2048:(r + 1) * 2048] = res.results[c]["out"]
    return outp
```

```python
import contextlib
import math
import numpy as np
import concourse.bass as bass
import concourse.mybir as mybir
from concourse.bass_utils import run_bass_kernel_spmd

F32 = mybir.dt.float32
BF16 = mybir.dt.bfloat16
AF = mybir.ActivationFunctionType
ALU = mybir.AluOpType
AX = mybir.AxisListType

D = 2048
T = 2112
NSL = 33
TILES = [(i * 128, min(128, T - i * 128)) for i in range(17)]
SEQC = 65
LSEQ = SEQC * 64
DFF = 5632
EPS = 1e-6
LAM_INIT = 0.8 - 0.6 * math.exp(-0.3 * 1)
NSLOT = 8
NCC = 8


class Buf:
    __slots__ = ("name", "w", "r", "rd", "excl")

    def __init__(self, name="", excl=False):
        self.name = name
        self.w = []
        self.r = {}
        self.rd = []
        self.excl = excl


class Sched:
    ENG = ("pe", "act", "dve", "pool", "sp")
    DMAQ = ("sp", "pool", "act")

    def __init__(self, nc):
        self.nc = nc
        self.prog = {e: [] for e in self.ENG}
        self.cnt = {e: 0 for e in self.ENG}
        self.dcnt = {q: 0 for q in self.DMAQ}
        self.ccnt = 0
        self.seen = {e: {} for e in self.ENG}
        self.pending = {e: {} for e in self.ENG}

    def _semkey(self, ev):
        if ev[0] == "e":
            return ("e", ev[1]), ev[2]
        if ev[0] == "c":
            return ("c", ev[1]), 1
        q, k = ev[1], ev[2]
        return ("d", q, k % NSLOT), 16 * (k // NSLOT + 1)

    def _collect(self, X, reads, writes):
        need = dict(self.pending[X])
        self.pending[X] = {}
        seen = self.seen[X]

        def add(ev):
            if ev is None:
                return
            if ev[0] == "e" and ev[1] == X and X == "pe":
                return
            key, val = self._semkey(ev)
            if seen.get(key, 0) >= val:
                return
            if need.get(key, 0) < val:
                need[key] = val

        excl_reads = [b for b in reads if b.excl]
        for b in reads:
            for ev in b.w:
                add(ev)
        for b in list(writes) + excl_reads:
            for ev in b.w:
                add(ev)
            for ev in b.r.values():
                add(ev)
            for ev in b.rd:
                add(ev)
        out = []
        for key, val in need.items():
            if seen.get(key, 0) < val:
                seen[key] = val
                out.append((key, val))
        return out

    def _mark(self, my, reads, writes, is_dma):
        writes = list(writes) + [b for b in reads if b.excl]
        for b in reads:
            if b.excl:
                continue
            if is_dma:
                b.rd.append(my)
            else:
                b.r[my[1]] = my
        for b in writes:
            b.w = [my]
            b.r = {}
            b.rd = []

    def op(self, eng, fn, reads=(), writes=()):
        waits = self._collect(eng, reads, writes)
        self.cnt[eng] += 1
        my = ("e", eng, self.cnt[eng])
        self.prog[eng].append((waits, fn, ("e", eng), 1))
        self._mark(my, reads, writes, False)
        return my

    def dma(self, q, out_ap, in_ap, reads=(), writes=(), add_writes=(), **kw):
        k = self.dcnt[q]
        waits = self._collect(q, reads, writes)
        if k >= NSLOT:
            key, val = ("d", q, k % NSLOT), 16 * (k // NSLOT)
            if self.seen[q].get(key, 0) < val:
                self.seen[q][key] = val
                waits.append((key, val))
        self.dcnt[q] += 1
        my = ("d", q, k)

        def fn(e):
            o = out_ap(e) if callable(out_ap) else out_ap
            i = in_ap(e) if callable(in_ap) else in_ap
            return e.dma_start(out=o, in_=i, **kw)

        self.prog[q].append((waits, fn, ("d", q, k % NSLOT), 16))
        self._mark(my, reads, writes, True)
        for b in add_writes:
            b.w.append(my)
        return my

    def cc(self, fn, reads=(), writes=()):
        i = self.ccnt
        self.ccnt += 1
        waits = self._collect("pool", reads, writes)
        my = ("c", i)
        self.prog["pool"].append((waits, fn, ("c", i), None))
        self._mark(my, reads, writes, True)
        return my

    def barrier(self):
        allw = {}
        for e in self.ENG:
            if self.cnt[e]:
                allw[("e", e)] = self.cnt[e]
        for q in self.DMAQ:
            n = self.dcnt[q]
            for i in range(min(n, NSLOT)):
                last_k = ((n - 1 - i) // NSLOT) * NSLOT + i
                allw[("d", q, i)] = 16 * (last_k // NSLOT + 1)
        for i in range(self.ccnt):
            allw[("c", i)] = 1
        for e in self.ENG:
            p = self.pending[e]
            for k, v in allw.items():
                if k == ("e", e) and e == "pe":
                    continue
                if p.get(k, 0) < v:
                    p[k] = v

    def emit(self, final_eng="sp"):
        nc = self.nc
        self.barrier()
        with contextlib.ExitStack() as st:
            sems = {}
            for e in self.ENG:
                sems[("e", e)] = st.enter_context(nc.semaphore(f"s_{e}"))
            for q in self.DMAQ:
                for i in range(NSLOT):
                    sems[("d", q, i)] = st.enter_context(nc.semaphore(f"d_{q}{i}"))
            for i in range(self.ccnt):
                sems[("c", i)] = st.enter_context(nc.semaphore(f"c_{i}"))
            fin = [(k, v) for k, v in self.pending[final_eng].items() if self.seen[final_eng].get(k, 0) < v]
            block = st.enter_context(nc.Block())
            prog = self.prog

            def run(engname, eng):
                for waits, fn, inc, amt in prog[engname]:
                    for key, val in waits:
                        eng.wait_ge(sems[key], val)
                    ins = fn(eng)
                    if amt is None:
                        ins.then_inc(sems[inc])
                    else:
                        ins.then_inc(sems[inc], amt)
                if engname == final_eng:
                    for key, val in fin:
                        eng.wait_ge(sems[key], val)

            @block.tensor
            def _(e):
                run("pe", e)

            @block.scalar
            def _(e):
                run("act", e)

            @block.vector
            def _(e):
                run("dve", e)

            @block.gpsimd
            def _(e):
                run("pool", e)

            @block.sync
            def _(e):
                run("sp", e)


class Tl:
    def __init__(self, kb, name, shape, dt, n=1, psum=False):
        self.t = []
        self.b = []
        for i in range(n):
            nm = f"{name}_{kb.uid()}"
            if psum:
                t = kb.st.enter_context(kb.nc.psum_tensor(nm, shape, dt))
            else:
                t = kb.st.enter_context(kb.nc.sbuf_tensor(nm, shape, dt))
            self.t.append(t)
            self.b.append(Buf(nm, excl=psum))
        self.n = n
        self.i = -1

    def next(self):
        self.i = (self.i + 1) % self.n
        return self.t[self.i], self.b[self.i]


class KB:
    def __init__(self, nc):
        self.nc = nc
        self.S = Sched(nc)
        self.st = None
        self._uid = 0
        self._rank = {}
        self.dram = {}
        self.dbuf = {}

    def uid(self):
        self._uid += 1
        return self._uid

    def rank(self, e):
        key = id(e)
        if key not in self._rank:
            self._rank[key] = e.snap(e.partition_id() % 2)
        return self._rank[key]

    def dt_in(self, name, shape, dt=F32):
        self.dram[name] = self.nc.dram_tensor(name, list(shape), dt, kind="ExternalInput").ap()
        return self.dram[name]

    def dt_out(self, name, shape, dt=F32):
        self.dram[name] = self.nc.dram_tensor(name, list(shape), dt, kind="ExternalOutput").ap()
        return self.dram[name]

    def dt_int(self, name, shape, dt=F32):
        self.dram[name] = self.nc.dram_tensor(name, list(shape), dt, kind="Internal").ap()
        return self.dram[name]

    @contextlib.contextmanager
    def phase(self):
        old = self.st
        with contextlib.ExitStack() as st:
            self.st = st
            yield
            self.S.barrier()
        self.st = old


def build_identity(kb, ident_t, ident_b):
    S = kb.S
    S.op("pool", lambda e: e.memset(ident_t[:], 0.0), writes=[ident_b])
    S.op("pool", lambda e: e.affine_select(out=ident_t[:], in_=ident_t[:], pattern=[[-1, 128]], compare_op=ALU.not_equal,
                                           fill=1.0, base=0, channel_multiplier=1), reads=[ident_b], writes=[ident_b])


@contextlib.contextmanager
def subscope(kb):
    old = kb.st
    with contextlib.ExitStack() as st:
        kb.st = st
        yield
        kb.S.barrier()
    kb.st = old


def transpose_rows(kb, src_t, src_bufs, ts, nk, dstT, dst_b, tok0, PT, col0=0, dk0=0, evac="act"):
    S = kb.S
    ident, ident_b = kb.ident, kb.ident_b
    for k0 in range(0, nk, 4):
        kn = min(4, nk - k0)
        pt, pb = PT.next()
        for kk in range(kn):
            k = k0 + kk
            S.op("pe", lambda e, k=k, kk=kk, pt=pt: e.transpose(out=pt[:, kk, :ts], in_=src_t[:ts, col0 + k * 128:col0 + (k + 1) * 128],
                                                              identity=ident[:ts, :ts]),
                 reads=list(src_bufs) + [ident_b], writes=[pb])
        if evac == "act":
            S.op("act", lambda e, k0=k0, kn=kn, pt=pt: e.copy(out=dstT[:, dk0 + k0:dk0 + k0 + kn, tok0:tok0 + ts], in_=pt[:, :kn, :ts]),
                 reads=[pb], writes=[dst_b])
        else:
            S.op("dve", lambda e, k0=k0, kn=kn, pt=pt: e.tensor_copy(out=dstT[:, dk0 + k0:dk0 + k0 + kn, tok0:tok0 + ts], in_=pt[:, :kn, :ts]),
                 reads=[pb], writes=[dst_b])


def rstd_inplace(kb, ms_ap, msb, rows, scale):
    S = kb.S
    S.op("act", lambda e: e.activation(out=ms_ap, in_=ms_ap, func=AF.Ln, scale=scale, bias=kb.epsb[:rows, :]), reads=[msb, kb.epsb_b], writes=[msb])
    S.op("act", lambda e: e.activation(out=ms_ap, in_=ms_ap, func=AF.Exp, scale=-0.5), reads=[msb], writes=[msb])


def norm_T(kb, h_ap, hB, gain_row_ap, XT, XTb):
    S = kb.S
    with subscope(kb):
        G = Tl(kb, "ng", [128, D], F32)
        H = Tl(kb, "nh", [128, D], F32, n=5)
        J = Tl(kb, "nj", [128, D], F32)
        XN = Tl(kb, "nx", [128, D], BF16, n=3)
        MS = Tl(kb, "nms", [128, 1], F32, n=3)

        PT = Tl(kb, "npt", [128, 4, 128], BF16, n=2, psum=True)
        S.dma("sp", G.t[0][:], gain_row_ap.partition_broadcast(128), writes=[G.b[0]])
        def stage_l(ti):
            t0, ts = TILES[ti]
            ht, hb = H.next()
            S.dma("sp", ht[:ts, :], h_ap[t0:t0 + ts, :], reads=hB[ti], writes=[hb])
            return ht, hb

        AHEAD = 3
        lds = {ti: stage_l(ti) for ti in range(AHEAD)}

        def stage_a(ti):
            t0, ts = TILES[ti]
            if ti + AHEAD < len(TILES):
                lds[ti + AHEAD] = stage_l(ti + AHEAD)
            ht, hb = lds.pop(ti)
            ms, msb = MS.next()
            S.op("act", lambda e, ht=ht, ms=ms, ts=ts: e.activation(out=J.t[0][:ts, :], in_=ht[:ts, :], func=AF.Square, accum_out=ms[:ts, :]),
                 reads=[hb], writes=[J.b[0], msb])
            rstd_inplace(kb, ms[:ts, :], msb, ts, 1.0 / D)
            xn, xnb = XN.next()
            S.op("dve", lambda e, ht=ht, ms=ms, xn=xn, ts=ts: e.scalar_tensor_tensor(out=xn[:ts, :], in0=ht[:ts, :], scalar=ms[:ts, 0:1], in1=G.t[0][:ts, :],
                                                                                    op0=ALU.mult, op1=ALU.mult),
                 reads=[hb, msb, G.b[0]], writes=[xnb])
            return xn, xnb

        def stage_b(ti, xn, xnb):
            t0, ts = TILES[ti]
            transpose_rows(kb, xn, [xnb], ts, 16, XT, XTb, t0, PT, evac="act" if ti % 2 == 0 else "dve")

        pend = stage_a(0)
        for ti in range(1, len(TILES)):
            nxt = stage_a(ti)
            stage_b(ti - 1, *pend)
            pend = nxt
        stage_b(len(TILES) - 1, *pend)


def gemm_tm(kb, KC, lhsT_prep, wblocks, epilogue, wbufs=2, pbufs=4, tiles=TILES, after_block=None, pre_tile=None):
    S = kb.S
    with subscope(kb):
        W = Tl(kb, "gw", [128, KC, 512], BF16, n=wbufs)
        PS = Tl(kb, "gp", [128, 512], F32, n=pbufs, psum=True)
        wt_list = []

        def load(bi):
            wt, wb = W.next()
            blk = wblocks[bi]
            for pi, (src_ap, off, wd) in enumerate(blk["pieces"]):
                S.dma("pool", wt[:, :, off:off + wd], src_ap.rearrange("(k p) c -> p k c", p=128),
                      writes=[wb] if pi == 0 else [], add_writes=[] if pi == 0 else [wb])
            wt_list.append((wt, wb))

        load(0)
        for bi, blk in enumerate(wblocks):
            if bi + 1 < len(wblocks):
                load(bi + 1)
            wt, wb = wt_list[bi]
            n = blk["ncols"]
            for ti, (t0, ts) in enumerate(tiles):
                if pre_tile is not None:
                    pre_tile(bi, ti, t0, ts, n)
                lfn, lbufs = lhsT_prep(bi, ti, t0, ts)
                ps, pb = PS.next()
                for k in range(KC):
                    S.op("pe", lambda e, ps=ps, lap=lfn(k), wt=wt, k=k, n=n, ts=ts: e.matmul(ps[:ts, :n], lhsT=lap, rhs=wt[:, k, :n],
                                                                                          start=(k == 0), stop=(k == KC - 1)),
                         reads=list(lbufs) + [wb], writes=[pb])
                epilogue(bi, ti, t0, ts, ps, pb, n)
            if after_block is not None:
                after_block(bi)


def resident_lhsT(XT, XTb):
    def prep(bi, ti, t0, ts):
        return (lambda k: XT[:, k, t0:t0 + ts]), [XTb]
    return prep


def residual_epilogue(kb, h_ap, hB, HS):
    S = kb.S
    held = {}

    def pre(bi, ti, t0, ts, n):
        c0 = bi * 512
        hs, hsb = HS.next()
        S.dma("sp", hs[:ts, :n], h_ap[t0:t0 + ts, c0:c0 + n], reads=[hB[ti][bi]], writes=[hsb])
        held[(bi, ti)] = (hs, hsb)

    def ep(bi, ti, t0, ts, ps, pb, n):
        c0 = bi * 512
        hs, hsb = held.pop((bi, ti))
        S.op("dve", lambda e: e.scalar_tensor_tensor(out=hs[:ts, :n], in0=ps[:ts, :n], scalar=kb.keep[:ts, ti:ti + 1], in1=hs[:ts, :n],
                                                     op0=ALU.mult, op1=ALU.add),
             reads=[pb, hsb, kb.keep_b], writes=[hsb])
        S.dma("sp", h_ap[t0:t0 + ts, c0:c0 + n], hs[:ts, :n], reads=[hsb], writes=[hB[ti][bi]])

    return pre, ep


GROUPS = [[0, 1], [2, 3], [4, 5], [6, 7]]


class Exchange:
    def __init__(self, kb, name, ncols, dt, rpg):
        self.kb, self.ncols, self.rpg = kb, ncols, rpg
        self.P = kb.dt_int(name + "_p", [2 * T, ncols], dt)
        self.P3 = self.P.rearrange("(d t) c -> d t c", d=2)
        self.groups = [(j0, min(rpg, T - j0)) for j0 in range(0, T, rpg)]
        self.G = [kb.dt_int(f"{name}_g{j}", [2, 2 * rows, ncols], dt) for j, (j0, rows) in enumerate(self.groups)]
        self.Gb = [[Buf(), Buf()] for _ in self.groups]
        self.MY = kb.dt_int(name + "_my", [2 * T, ncols], dt)
        self.MY3 = self.MY.rearrange("(s t) c -> s t c", s=2)
        self.MYb = [Buf() for _ in self.groups]
        self.Pb = [[], []]

    def wbuf(self, d):
        b_ = Buf()
        self.Pb[d].append(b_)
        return b_

    def gather_d(self, d):
        S = self.kb.S
        for j, (j0, rows) in enumerate(self.groups):
            src_ap = self.P3[d, j0:j0 + rows, :]
            dst_ap = self.G[j][d]
            S.cc(lambda e, src_ap=src_ap, dst_ap=dst_ap: e.collective_compute("AllGather", ALU.bypass, replica_groups=[list(g) for g in GROUPS],
                                                                              ins=[src_ap.opt()], outs=[dst_ap.opt()]),
                 reads=self.Pb[d], writes=[self.Gb[j][d]])
        self.Pb[d] = []

    def finish(self):
        S = self.kb.S
        kb = self.kb
        for j, (j0, rows) in enumerate(self.groups):
            G = self.G[j]
            S.dma("pool", self.MY3[:, j0:j0 + rows, :],
                  (lambda e, G=G: G[bass.ds(kb.rank(e), 1), :, :].rearrange("o (s t) c -> (o s) t c", s=2)),
                  reads=self.Gb[j], writes=[self.MYb[j]])

    def gather(self):
        self.gather_d(0)
        self.gather_d(1)
        self.finish()

    def read(self, src, t0, n, c0, wd):
        j = t0 // self.rpg
        j0, rows = self.groups[j]
        assert t0 + n <= j0 + rows
        return self.MY3[src, t0:t0 + n, c0:c0 + wd], [self.MYb[j]]


def mixer_out_proj(kb, EX, colmap, w_ap, h_ap, hB):
    S = kb.S
    with kb.phase():
        XT = Tl(kb, "mx", [128, 16, T], BF16)
        with subscope(kb):
            O = Tl(kb, "go", [128, D], BF16, n=3)
            PT = Tl(kb, "gpt", [128, 4, 128], BF16, n=2, psum=True)
            for ti, (t0, ts) in enumerate(TILES):
                ot, ob = O.next()
                for pi, (src, c_src, c_dst, wd) in enumerate(colmap):
                    apf, gb = EX.read(src, t0, ts, c_src, wd)
                    S.dma("sp", ot[:ts, c_dst:c_dst + wd], apf,
                          reads=gb, writes=[ob] if pi == 0 else [], add_writes=[] if pi == 0 else [ob])
                transpose_rows(kb, ot, [ob], ts, 16, XT.t[0], XT.b[0], t0, PT)
        HS = Tl(kb, "mh", [128, 512], F32, n=6)
        wblocks = [{"pieces": [(w_ap[:, c0:c0 + 512], 0, 512)], "ncols": 512} for c0 in range(0, D, 512)]
        pre, ep = residual_epilogue(kb, h_ap, hB, HS)
        gemm_tm(kb, 16, resident_lhsT(XT.t[0], XT.b[0]), wblocks, ep, pre_tile=pre)


def ffn(kb, layer, h_ap, hB):
    S = kb.S
    dr = kb.dram
    aT = dr["aT"]
    aTb = [Buf() for _ in range(44)]
    with kb.phase():
        XT = Tl(kb, "fx", [128, 16, T], BF16)
        norm_T(kb, h_ap, hB, dr["ffn_norm"][layer:layer + 1, :], XT.t[0], XT.b[0])
        W = Tl(kb, "fw", [128, 16, 2, 256], BF16, n=2)
        CW = Tl(kb, "fcw", [128, 3, 88], F32)
        CB = Tl(kb, "fcb", [128, 88], F32)
        U = Tl(kb, "fu", [128, 2, T + 2], F32, n=2)
        C = Tl(kb, "fc", [128, 2, T], F32, n=2)
        A = Tl(kb, "fa", [128, T], BF16, n=2)
        PS = Tl(kb, "fp", [128, 512], F32, n=4, psum=True)
        S.dma("sp", CW.t[0][:], dr["conv_w"][layer].rearrange("t (c p) -> p t c", p=128), writes=[CW.b[0]], allow_slow_non_contiguous=True)
        S.dma("sp", CB.t[0][:], dr["conv_b"][layer:layer + 1, :].rearrange("o (c p) -> p (o c)", p=128), writes=[CB.b[0]], allow_slow_non_contiguous=True)
        for i in range(2):
            S.op("pool", lambda e, i=i: e.memset(U.t[i][:, :, 0:2], 0.0), writes=[U.b[i]])
        w_up = dr["w_up"][layer]
        NB = 22
        wl = []

        def loadw(b):
            wt, wb = W.next()
            for vg in range(2):
                c0 = vg * DFF + b * 256
                S.dma("pool", wt[:, :, vg, :], w_up[:, c0:c0 + 256].rearrange("(k p) c -> p k c", p=128),
                      writes=[wb] if vg == 0 else [], add_writes=[] if vg == 0 else [wb])
            wl.append((wt, wb))

        tslices = [(s0, min(512, T - s0)) for s0 in range(0, T, 512)]
        loadw(0)
        for b in range(NB):
            if b + 1 < NB:
                loadw(b + 1)
            wt, wb = wl[b]
            for j in range(2):
                fc = b * 2 + j
                ut, ub = U.next()
                for vg in range(2):
                    for (s0, sn) in tslices:
                        ps, pb = PS.next()
                        for k in range(16):
                            S.op("pe", lambda e, ps=ps, wt=wt, k=k, vg=vg, j=j, s0=s0, sn=sn: e.matmul(
                                ps[:, :sn], lhsT=wt[:, k, vg, j * 128:(j + 1) * 128], rhs=XT.t[0][:, k, s0:s0 + sn], start=(k == 0), stop=(k == 15)),
                                reads=[wb, XT.b[0]], writes=[pb])
                        S.op("act", lambda e, ps=ps, ut=ut, vg=vg, s0=s0, sn=sn: e.copy(out=ut[:, vg, 2 + s0:2 + s0 + sn], in_=ps[:, :sn]),
                             reads=[pb], writes=[ub])
                ct, cb = C.next()
                for vg in range(2):
                    ch = vg * 44 + fc
                    S.op("act", lambda e, ut=ut, ct=ct, vg=vg, ch=ch: e.activation(out=ct[:, vg, :], in_=ut[:, vg, 0:T], func=AF.Identity,
                                                                               scale=CW.t[0][:, 0, ch:ch + 1], bias=CB.t[0][:, ch:ch + 1]),
                         reads=[ub, CW.b[0], CB.b[0]], writes=[cb])
                    for tap in (1, 2):
                        S.op("dve", lambda e, ut=ut, ct=ct, vg=vg, ch=ch, tap=tap: e.scalar_tensor_tensor(
                            out=ct[:, vg, :], in0=ut[:, vg, tap:tap + T], scalar=CW.t[0][:, tap, ch:ch + 1], in1=ct[:, vg, :], op0=ALU.mult, op1=ALU.add),
                            reads=[ub, cb, CW.b[0]], writes=[cb])
                S.op("act", lambda e, ct=ct: e.activation(out=ct[:, 1, :], in_=ct[:, 1, :], func=AF.Silu), reads=[cb], writes=[cb])
                at, ab = A.next()
                S.op("dve", lambda e, ct=ct, at=at: e.tensor_tensor(out=at[:], in0=ct[:, 0, :], in1=ct[:, 1, :], op=ALU.mult), reads=[cb], writes=[ab])
                S.dma("sp", aT[fc * 128:(fc + 1) * 128, :], at[:], reads=[ab], writes=[aTb[fc]])
    with kb.phase():
        HS = Tl(kb, "dh", [128, 512], F32, n=6)
        AT = Tl(kb, "da", [128, 44, 256], BF16, n=2)
        w_down = dr["w_down"][layer]
        wblocks = [{"pieces": [(w_down[:, c0:c0 + 512], 0, 512)], "ncols": 512} for c0 in range(0, D, 512)]
        aT3 = aT.rearrange("(k p) t -> p k t", p=128)
        loaded = {}
        order = [(bi, g) for bi in range(len(wblocks)) for g in range((len(TILES) + 1) // 2)]

        def load_group(key):
            if key in loaded:
                return
            at, ab = AT.next()
            g0 = key[1] * 256
            gn = min(256, T - g0)
            S.dma("sp", at[:, :, :gn], aT3[:, :, g0:g0 + gn], reads=aTb, writes=[ab])
            loaded[key] = (at, ab)

        def prep(bi, ti, t0, ts):
            g = ti // 2
            if ti % 2 == 0:
                load_group((bi, g))
                nxt = order.index((bi, g)) + 1
                if nxt < len(order):
                    load_group(order[nxt])
            at, ab = loaded[(bi, g)]
            off = t0 - g * 256
            return (lambda k: at[:, k, off:off + ts]), [ab]

        pre, ep = residual_epilogue(kb, h_ap, hB, HS)
        gemm_tm(kb, 44, prep, wblocks, ep, pre_tile=pre)
C_QA, C_KA, C_VA, C_GA, C_QB, C_KB, C_VB, C_GB = 0, 256, 512, 1024, 1536, 2048, 2560, 3072
NP0 = 3584
DBG_L0 = {"la", "gemm", "ag"}
SC_INTRA, SC_XIB, SC_ZETA, SC_GCH, SC_MASK, SC_TRIU, SC_TRIL, SC_KBIAS, SC_N = 0, 128, 256, 258, 260, 516, 580, 644, 648


def stage_store(kb, ST, ps, pb, ts, n, dst_ap, dst_bufs, eng="act"):
    S = kb.S
    stt, stb = ST.next()
    if eng == "act":
        S.op("act", lambda e: e.copy(out=stt[:ts, :n], in_=ps[:ts, :n]), reads=[pb], writes=[stb])
    else:
        S.op("dve", lambda e: e.tensor_copy(out=stt[:ts, :n], in_=ps[:ts, :n]), reads=[pb], writes=[stb])
    S.dma("sp", dst_ap, stt[:ts, :n], reads=[stb], writes=dst_bufs)


def l0_proj(kb, h_ap, hB):
    S = kb.S
    dr = kb.dram
    w = dr["w_in_e"]
    P0 = kb.EX0.P3
    LA0 = kb.EXL.P3
    with kb.phase():
        XT = Tl(kb, "px", [128, 16, T], BF16)
        norm_T(kb, h_ap, hB, dr["mix_norm_e"], XT.t[0], XT.b[0])
        with subscope(kb) if "la" in DBG_L0 else contextlib.nullcontext():
          if "la" in DBG_L0:
              WL = Tl(kb, "wl", [128, 16, 16], BF16)
              LR = Tl(kb, "lr", [17, T], F32)
              LRH = Tl(kb, "lrh", [17, T], BF16)
              LRL = Tl(kb, "lrl", [17, T], BF16)
              W2 = Tl(kb, "w2", [17, 512], F32)
              W2H = Tl(kb, "w2h", [17, 512], BF16)
              W2L = Tl(kb, "w2l", [17, 512], BF16)
              PL = Tl(kb, "pl", [128, 512], F32, n=2, psum=True)
              E1 = Tl(kb, "e1", [128, 512], F32, n=2)
              S.dma("pool", WL.t[0][:], w[:, 3072:3088].rearrange("(k p) c -> p k c", p=128), writes=[WL.b[0]])
              S.dma("sp", W2.t[0][:], dr["gw2"], writes=[W2.b[0]])
              S.op("dve", lambda e: e.memset(LR.t[0][:], 1.0), writes=[LR.b[0]])
              S.op("dve", lambda e: e.tensor_copy(out=W2H.t[0][:], in_=W2.t[0][:]), reads=[W2.b[0]], writes=[W2H.b[0]])
              S.op("dve", lambda e: e.tensor_tensor(out=W2L.t[0][:], in0=W2.t[0][:], in1=W2H.t[0][:], op=ALU.subtract), reads=[W2.b[0], W2H.b[0]], writes=[W2L.b[0]])
              for s0 in range(0, T, 512):
                  sn = min(512, T - s0)
                  ps, pb = PL.next()
                  for k in range(16):
                      S.op("pe", lambda e, ps=ps, k=k, s0=s0, sn=sn: e.matmul(ps[:16, :sn], lhsT=WL.t[0][:, k, :], rhs=XT.t[0][:, k, s0:s0 + sn], start=(k == 0), stop=(k == 15)),
                           reads=[WL.b[0], XT.b[0]], writes=[pb])
                  S.op("act", lambda e, ps=ps, s0=s0, sn=sn: e.copy(out=LR.t[0][0:16, s0:s0 + sn], in_=ps[:16, :sn]), reads=[pb], writes=[LR.b[0]])
              S.op("dve", lambda e: e.tensor_copy(out=LRH.t[0][:], in_=LR.t[0][:]), reads=[LR.b[0]], writes=[LRH.b[0]])
              S.op("dve", lambda e: e.tensor_tensor(out=LRL.t[0][:], in0=LR.t[0][:], in1=LRH.t[0][:], op=ALU.subtract), reads=[LR.b[0], LRH.b[0]], writes=[LRL.b[0]])
              for ti, (t0, ts) in enumerate(TILES):
                  ps, pb = PL.next()
                  combos = [(LRH, W2H), (LRL, W2H), (LRH, W2L)]
                  for ci, (a, b_) in enumerate(combos):
                      S.op("pe", lambda e, ps=ps, a=a, b_=b_, ci=ci, t0=t0, ts=ts: e.matmul(ps[:ts, :], lhsT=a.t[0][:, t0:t0 + ts], rhs=b_.t[0][:, :], start=(ci == 0), stop=(ci == 2)),
                           reads=[a.b[0], b_.b[0]], writes=[pb])
                  e1, e1b = E1.next()
                  S.op("act", lambda e, ps=ps, e1=e1, ts=ts: e.activation(out=e1[:ts, :], in_=ps[:ts, :], func=AF.Exp, scale=-1.0), reads=[pb], writes=[e1b])
                  S.op("act", lambda e, e1=e1, ts=ts: e.activation(out=e1[:ts, :], in_=e1[:ts, :], func=AF.Ln, bias=kb.oneb[:ts, :], scale=1.0), reads=[e1b, kb.oneb_b], writes=[e1b])
                  S.op("dve", lambda e, e1=e1, ts=ts, ti=ti: e.tensor_scalar(out=e1[:ts, :], in0=e1[:ts, :], scalar1=kb.keep[:ts, ti:ti + 1], scalar2=-1.0 / 16.0,
                                                                          op0=ALU.mult, op1=ALU.mult), reads=[e1b, kb.keep_b], writes=[e1b])
                  for dst in range(2):
                      b_ = kb.EXL.wbuf(dst)
                      S.dma("sp", LA0[dst, t0:t0 + ts, :], e1[:ts, dst * 256:(dst + 1) * 256], reads=[e1b], writes=[b_])
        wblocks = []
        for dst in range(2):
            wblocks.append({"pieces": [(w[:, dst * 256:dst * 256 + 256], 0, 256), (w[:, 512 + dst * 256:512 + dst * 256 + 256], 256, 256)],
                            "ncols": 512, "dst": dst, "dcol": 0})
            for base, dcol in ((1024, C_VA), (2048, C_GA), (3088, C_QB), (4112, C_KB), (5136, C_VB), (6160, C_GB)):
                c0 = base + dst * 512
                wblocks.append({"pieces": [(w[:, c0:c0 + 512], 0, 512)], "ncols": 512, "dst": dst, "dcol": dcol})
        ST = Tl(kb, "pst", [128, 512], BF16, n=3)
        RTAB = Tl(kb, "prt", [128, 17, 256], F32)
        S.dma("sp", RTAB.t[0][:, 0:16, :], dr["rtab"][0:2048, :].rearrange("(i p) c -> p i c", p=128), writes=[RTAB.b[0]])
        S.dma("sp", RTAB.t[0][0:64, 16, :], dr["rtab"][2048:T, :], add_writes=[RTAB.b[0]])
        RA = Tl(kb, "pra", [128, 2, 2, 128], F32, n=2)
        RB = Tl(kb, "prb", [128, 2, 2, 128], F32, n=2)

        def ep(bi, ti, t0, ts, ps, pb, n):
            blk = wblocks[bi]
            b_ = kb.EX0.wbuf(blk["dst"])
            dst_ap = P0[blk["dst"], t0:t0 + ts, blk["dcol"]:blk["dcol"] + n]
            if blk["dcol"] in (C_QB, C_KB):
                x = ps[:ts, :].rearrange("p (g f d) -> p g f d", g=2, f=2)
                cosb = RTAB.t[0][:ts, ti, 0:128].unsqueeze(1).to_broadcast([ts, 2, 128])
                sinb = RTAB.t[0][:ts, ti, 128:256].unsqueeze(1).to_broadcast([ts, 2, 128])
                ra, rab = RA.next()
                rb, rbb = RB.next()
                S.op("dve", lambda e: e.tensor_tensor(out=ra[:ts, :, 0, :], in0=x[:, :, 0, :], in1=cosb, op=ALU.mult), reads=[pb, RTAB.b[0]], writes=[rab])
                S.op("dve", lambda e: e.tensor_tensor(out=rb[:ts, :, 0, :], in0=x[:, :, 1, :], in1=sinb, op=ALU.mult), reads=[pb, RTAB.b[0]], writes=[rbb])
                S.op("dve", lambda e: e.tensor_tensor(out=ra[:ts, :, 1, :], in0=x[:, :, 1, :], in1=cosb, op=ALU.mult), reads=[pb, RTAB.b[0], rab], writes=[rab])
                S.op("dve", lambda e: e.tensor_tensor(out=rb[:ts, :, 1, :], in0=x[:, :, 0, :], in1=sinb, op=ALU.mult), reads=[pb, RTAB.b[0], rbb], writes=[rbb])
                stt, stb = ST.next()
                so = stt[:ts, :].rearrange("p (g f d) -> p g f d", g=2, f=2)
                S.op("pool", lambda e: e.tensor_tensor(out=so[:, :, 0, :], in0=ra[:ts, :, 0, :], in1=rb[:ts, :, 0, :], op=ALU.subtract), reads=[rab, rbb], writes=[stb])
                S.op("pool", lambda e: e.tensor_tensor(out=so[:, :, 1, :], in0=ra[:ts, :, 1, :], in1=rb[:ts, :, 1, :], op=ALU.add), reads=[rab, rbb, stb], writes=[stb])
                S.dma("sp", dst_ap, stt[:ts, :n], reads=[stb], writes=[b_])
            else:
                stage_store(kb, ST, ps, pb, ts, n, dst_ap, [b_], eng="act")

        if "gemm" in DBG_L0:
            kb.EXL.gather()

            def after_block(bi):
                if bi == 6:
                    kb.EX0.gather_d(0)

            gemm_tm(kb, 16, resident_lhsT(XT.t[0], XT.b[0]), wblocks, ep, after_block=after_block)
            kb.EX0.gather_d(1)
            kb.EX0.finish()


def rms_gate_store(kb, OPt, OPb, nh, hd, rows, GN, GNb, gate_fn, OUT, out_cb, W):
    S = kb.S
    sq, sqb = W["SQ"].next()
    S.op("act", lambda e: e.activation(out=sq[:rows, :nh * hd], in_=OPt, func=AF.Square), reads=[OPb], writes=[sqb])
    ms, msb = W["MS"].next()
    S.op("dve", lambda e: e.tensor_reduce(out=ms[:rows, :nh], in_=sq[:rows, :nh * hd].rearrange("p (h d) -> p h d", h=nh), axis=AX.X, op=ALU.add),
         reads=[sqb], writes=[msb])
    rstd_inplace(kb, ms[:rows, :nh], msb, rows, 1.0 / hd)
    on, onb = W["ON"].next()
    S.op("dve", lambda e: e.tensor_tensor(out=on[:rows, :nh * hd].rearrange("p (h d) -> p h d", h=nh), in0=OPt.rearrange("p (h d) -> p h d", h=nh) if len(OPt.shape) == 2 else OPt,
                                          in1=ms[:rows, :nh].unsqueeze(2).to_broadcast([rows, nh, hd]), op=ALU.mult),
         reads=[OPb, msb], writes=[onb])
    S.op("pool", lambda e: e.tensor_tensor(out=on[:rows, :nh * hd], in0=on[:rows, :nh * hd], in1=GN[:rows, :nh * hd], op=ALU.mult),
         reads=[onb, GNb], writes=[onb])
    ot, ob = OUT.next()
    if gate_fn is not None:
        sg, sgb = gate_fn()
        S.op("dve", lambda e: e.tensor_tensor(out=ot[:rows, :nh * hd], in0=on[:rows, :nh * hd], in1=sg[:rows, :nh * hd], op=ALU.mult),
             reads=[onb, sgb], writes=[ob])
    else:
        S.op("dve", lambda e: e.tensor_copy(out=ot[:rows, :nh * hd], in_=on[:rows, :nh * hd]), reads=[onb], writes=[ob])
    out_cb(ot, ob)


def l0_scan(kb):
    S = kb.S
    dr = kb.dram
    O1 = kb.EXO.P3
    SCL = 128.0 ** -0.5
    with kb.phase():
        SC = Tl(kb, "sc", [128, SC_N], F32)
        GN = Tl(kb, "sgn", [64, 1024], F32)
        TRI = Tl(kb, "tri", [64, 2, 64], BF16)
        ONE = Tl(kb, "one", [64, 1], BF16)
        Sg = Tl(kb, "Sg", [128, 2, 256], F32)
        Sgb = Tl(kb, "Sgb", [128, 2, 256], BF16)
        Sr = Tl(kb, "Sr", [128, 2, 2, 256], F32)
        Srb = Tl(kb, "Srb", [128, 2, 2, 256], BF16)
        S.dma("sp", SC.t[0][:], dr["scc"], writes=[SC.b[0]])
        S.dma("sp", GN.t[0][:], dr["gnorm"].partition_broadcast(64), writes=[GN.b[0]])
        S.op("dve", lambda e: e.tensor_copy(out=TRI.t[0][:], in_=SC.t[0][0:64, SC_TRIU:SC_TRIU + 128].rearrange("p (a b) -> p a b", a=2)), reads=[SC.b[0]], writes=[TRI.b[0]])
        S.op("dve", lambda e: e.memset(ONE.t[0][:], 1.0), writes=[ONE.b[0]])
        S.op("dve", lambda e: e.memset(Sg.t[0][:], 0.0), writes=[Sg.b[0]])
        S.op("dve", lambda e: e.memset(Sgb.t[0][:], 0.0), writes=[Sgb.b[0]])
        S.op("pool", lambda e: e.memset(Sr.t[0][:], 0.0), writes=[Sr.b[0]])
        S.op("pool", lambda e: e.memset(Srb.t[0][:], 0.0), writes=[Srb.b[0]])
        sc = SC.t[0]
        scb = SC.b[0]
        Dt = Tl(kb, "sD", [64, NP0], BF16, n=6)
        LAt = Tl(kb, "sLA", [64, 256], F32, n=5)
        LH = Tl(kb, "sLH", [64, 2, 256], BF16, n=2)
        EX = Tl(kb, "sEX", [64, 3, 256], F32, n=2)
        DEC = Tl(kb, "sDEC", [128, 2], F32, n=2)
        GPa = Tl(kb, "sGPa", [64, 3, 256], BF16, n=2)
        GPb = Tl(kb, "sGPb", [64, 2, 256], BF16, n=2)
        TT = Tl(kb, "sTT", [128, 8, 64], BF16, n=2)
        STt = Tl(kb, "sST", [64, 256], BF16, n=2)
        KBr = Tl(kb, "sKB", [64, 2, 256], BF16, n=2)
        TR = Tl(kb, "sTR", [128, 8, 64], BF16, n=2)
        TQ = Tl(kb, "sTQ", [128, 4, 64], BF16, n=2)
        STr = Tl(kb, "sSTr", [64, 2, 64], BF16, n=2)
        SG = Tl(kb, "sSG", [64, 1024], F32, n=2)
        OUT = Tl(kb, "sOUT", [64, 1024], BF16, n=2)
        OEV = Tl(kb, "sOEV", [64, 1024], F32, n=2)
        Wk = {"SQ": Tl(kb, "sSQ", [64, 1024], F32), "MS": Tl(kb, "sMS", [64, 4], F32, n=2), "ON": Tl(kb, "sON", [64, 1024], F32)}
        cumP = Tl(kb, "pcum", [64, 2, 256], F32, psum=True)
        MISC = Tl(kb, "pmisc", [128, 512], F32, psum=True)
        TPB = Tl(kb, "ptp", [128, 16, 64], BF16, psum=True)
        OP = Tl(kb, "pop", [64, 4, 256], F32, psum=True)
        SPs = Tl(kb, "psps", [128, 2, 256], F32, n=3, psum=True)
        misc = MISC.t[0]
        spB = sprB = decB = MISC.b[0]
        tpB = [TPB.b[0], TPB.b[0]]
        ident, ident_b = kb.ident, kb.ident_b

        def stage_l(c):
                src, slot = (0, c) if c <= 32 else (1, c - 32)
                r0 = slot * 64
                dt_, db = Dt.next()
                apf, gb = kb.EX0.read(src, r0, 64, 0, NP0)
                S.dma("sp", dt_[:], apf, reads=gb, writes=[db])
                lat, lab = LAt.next()
                apf, gb = kb.EXL.read(src, r0, 64, 0, 256)
                S.dma("sp", lat[:], apf, reads=gb, writes=[lab])
                return dt_, db, lat, lab

        def stage_a(c, ld):
                dt_, db, lat, lab = ld
                lh, lhb = LH.next()
                S.op("dve", lambda e, lh=lh, lat=lat: e.tensor_copy(out=lh[:, 0, :], in_=lat[:]), reads=[lab], writes=[lhb])
                S.op("dve", lambda e, lh=lh, lat=lat: e.tensor_tensor(out=lh[:, 1, :], in0=lat[:], in1=lh[:, 0, :], op=ALU.subtract), reads=[lab, lhb], writes=[lhb])
                cp, cpb = cumP.next()
                for w_ in range(2):
                    for hl in range(2):
                        S.op("pe", lambda e, cp=cp, lh=lh, w_=w_, hl=hl: e.matmul(cp[:, w_, :], lhsT=TRI.t[0][:, w_, :], rhs=lh[:, hl, :], start=(hl == 0), stop=(hl == 1)),
                             reads=[TRI.b[0], lhb], writes=[cpb])
                dp, dpb = misc[:, 384:386], decB
                for h in range(2):
                    for hl in range(2):
                        S.op("pe", lambda e, dp=dp, lh=lh, h=h, hl=hl: e.matmul(dp[:, h:h + 1], lhsT=lh[:, hl, h * 128:(h + 1) * 128], rhs=ONE.t[0][:, :], start=(hl == 0), stop=(hl == 1)),
                             reads=[ONE.b[0], lhb], writes=[dpb])
                ex, exb = EX.next()
                S.op("act", lambda e, ex=ex, cp=cp: e.activation(out=ex[:, 0, :], in_=cp[:, 0, :], func=AF.Exp), reads=[cpb], writes=[exb])
                S.op("act", lambda e, ex=ex, cp=cp: e.activation(out=ex[:, 1, :], in_=cp[:, 0, :], func=AF.Exp, scale=-1.0), reads=[cpb], writes=[exb])
                S.op("act", lambda e, ex=ex, cp=cp: e.activation(out=ex[:, 2, :], in_=cp[:, 1, :], func=AF.Exp), reads=[cpb], writes=[exb])
                dec, decb = DEC.next()
                S.op("act", lambda e, dec=dec, dp=dp: e.activation(out=dec[:], in_=dp, func=AF.Exp), reads=[dpb], writes=[decb])
                ga_, gab = GPa.next()
                gb_, gbb = GPb.next()
                q_ap = dt_[:, C_QA:C_QA + 256]
                k_ap = dt_[:, C_KA:C_KA + 256]
                S.op("dve", lambda e, ga_=ga_, ex=ex, q_ap=q_ap: e.scalar_tensor_tensor(out=ga_[:, 0, :], in0=q_ap, scalar=SCL, in1=ex[:, 0, :], op0=ALU.mult, op1=ALU.mult), reads=[db, exb], writes=[gab])
                S.op("pool", lambda e, gb_=gb_, ex=ex, q_ap=q_ap: e.tensor_tensor(out=gb_[:, 0, :], in0=q_ap, in1=ex[:, 1, :], op=ALU.mult), reads=[db, exb], writes=[gbb])
                S.op("dve", lambda e, ga_=ga_, ex=ex, k_ap=k_ap: e.tensor_tensor(out=ga_[:, 1, :], in0=k_ap, in1=ex[:, 1, :], op=ALU.mult), reads=[db, exb, gab], writes=[gab])
                S.op("pool", lambda e, gb_=gb_, ex=ex, k_ap=k_ap: e.tensor_tensor(out=gb_[:, 1, :], in0=k_ap, in1=ex[:, 0, :], op=ALU.mult), reads=[db, exb, gbb], writes=[gbb])
                S.op("dve", lambda e, ga_=ga_, ex=ex, k_ap=k_ap: e.tensor_tensor(out=ga_[:, 2, :], in0=k_ap, in1=ex[:, 2, :], op=ALU.mult), reads=[db, exb, gab], writes=[gab])
                tp, tpb = TPB.t[0][:, 0:8, :], tpB[0]
                srcs = [(ga_, 0, gab), (gb_, 0, gbb), (ga_, 1, gab), (gb_, 1, gbb)]
                for it in range(4):
                    st_, si, sb_ = srcs[it]
                    for h in range(2):
                        S.op("pe", lambda e, tp=tp, st_=st_, si=si, it=it, h=h: e.transpose(out=tp[:, it * 2 + h, :], in_=st_[:, si, h * 128:(h + 1) * 128], identity=ident[:64, :64]),
                             reads=[sb_, ident_b], writes=[tpb])
                tt, ttb = TT.next()
                S.op("act", lambda e, tt=tt, tp=tp: e.copy(out=tt[:], in_=tp), reads=[tpb], writes=[ttb])
                rr = dt_[:, C_QB:C_QB + 1024].rearrange("p (g f d) -> p g f d", g=4, f=2)
                rrb = db
                kbr, kbrb = KBr.next()
                S.op("dve", lambda e, kbr=kbr, rr=rr: e.tensor_tensor(out=kbr[:], in0=rr[:, 2:4, :, :].rearrange("p g f d -> p g (f d)"),
                                                                      in1=sc[0:64, SC_ZETA:SC_ZETA + 2].unsqueeze(2).to_broadcast([64, 2, 256]), op=ALU.mult),
                     reads=[rrb, scb], writes=[kbrb])
                tp2, tp2b = TPB.t[0][:, 8:16, :], tpB[1]
                for g in range(4):
                    for f in range(2):
                        S.op("pe", lambda e, tp2=tp2, rr=rr, g=g, f=f: e.transpose(out=tp2[:, g * 2 + f, :], in_=rr[:, g, f, :], identity=ident[:64, :64]),
                             reads=[rrb, ident_b], writes=[tp2b])
                tr, trb = TR.next()
                S.op("act", lambda e, tr=tr, tp2=tp2: e.copy(out=tr[:], in_=tp2), reads=[tp2b], writes=[trb])
                tq, tqb = TQ.next()
                S.op("dve", lambda e, tq=tq, tp2=tp2: e.tensor_tensor(out=tq[:].rearrange("p (h f) i -> p h f i", h=2), in0=tp2[:, 0:4, :].rearrange("p (h f) i -> p h f i", h=2),
                                                                      in1=sc[:, SC_XIB:SC_XIB + 128].rearrange("p (h i) -> p h i", h=2).unsqueeze(2).to_broadcast([128, 2, 2, 64]), op=ALU.mult),
                     reads=[tp2b, scb], writes=[tqb])
                sg, sgb = SG.next()
                S.op("act", lambda e, sg=sg, dt_=dt_: e.activation(out=sg[:, 0:512], in_=dt_[:, C_GA:C_GA + 512], func=AF.Exp, scale=-1.0), reads=[db], writes=[sgb])
                S.op("act", lambda e, sg=sg, dt_=dt_: e.activation(out=sg[:, 512:1024], in_=dt_[:, C_GB:C_GB + 512], func=AF.Exp, scale=-1.0), reads=[db, sgb], writes=[sgb])
                S.op("act", lambda e, sg=sg: e.activation(out=sg[:, :], in_=sg[:, :], func=AF.Ln, bias=kb.oneb[:64, :], scale=1.0), reads=[sgb, kb.oneb_b], writes=[sgb])
                S.op("act", lambda e, sg=sg: e.activation(out=sg[:, :], in_=sg[:, :], func=AF.Exp, scale=-1.0), reads=[sgb], writes=[sgb])
                S.op("pool", lambda e, sg=sg, dt_=dt_: e.tensor_tensor(out=sg[:, 0:512], in0=sg[:, 0:512], in1=dt_[:, C_GA:C_GA + 512], op=ALU.mult), reads=[db, sgb], writes=[sgb])
                S.op("pool", lambda e, sg=sg, dt_=dt_: e.tensor_tensor(out=sg[:, 512:1024], in0=sg[:, 512:1024], in1=dt_[:, C_GB:C_GB + 512], op=ALU.mult), reads=[db, sgb], writes=[sgb])

                return dict(c=c, dt_=dt_, db=db, dec=dec, decb=decb, ga_=ga_, gab=gab, tt=tt, ttb=ttb, kbr=kbr, kbrb=kbrb, tr=tr, trb=trb, tq=tq, tqb=tqb, sg=sg, sgb=sgb)

        def stage_b(Lc):
                c, dt_, db, dec, decb, ga_, gab, tt, ttb = Lc['c'], Lc['dt_'], Lc['db'], Lc['dec'], Lc['decb'], Lc['ga_'], Lc['gab'], Lc['tt'], Lc['ttb']
                kbr, kbrb, tr, trb, tq, tqb, sg, sgb = Lc['kbr'], Lc['kbrb'], Lc['tr'], Lc['trb'], Lc['tq'], Lc['tqb'], Lc['sg'], Lc['sgb']
                sp_, spb = misc[0:64, 0:256], spB
                for h in range(2):
                    S.op("pe", lambda e, sp_=sp_, tt=tt, h=h: e.matmul(sp_[:, (h * 2 + 0) * 64:(h * 2 + 1) * 64], lhsT=tt[:, 2 * 2 + h, :], rhs=tt[:, 0 * 2 + h, :], start=True, stop=True),
                         reads=[ttb], writes=[spb])
                    S.op("pe", lambda e, sp_=sp_, tt=tt, h=h: e.matmul(sp_[:, (h * 2 + 1) * 64:(h * 2 + 2) * 64], lhsT=tt[:, 3 * 2 + h, :], rhs=tt[:, 1 * 2 + h, :], start=True, stop=True),
                         reads=[ttb], writes=[spb])
                stt, stb = STt.next()
                S.op("dve", lambda e, stt=stt, sp_=sp_: e.tensor_tensor(out=stt[:], in0=sp_, in1=sc[0:64, SC_MASK:SC_MASK + 256], op=ALU.mult), reads=[spb, scb], writes=[stb])
                op_, opb = OP.next()
                for h in range(2):
                    v_ap = dt_[:, C_VA + h * 256:C_VA + (h + 1) * 256]
                    S.op("pe", lambda e, op_=op_, stt=stt, h=h, v_ap=v_ap: e.matmul(op_[:, h, :], lhsT=stt[:, (h * 2) * 64:(h * 2 + 1) * 64], rhs=v_ap, start=True, stop=False),
                         reads=[stb, db], writes=[opb])
                    S.op("pe", lambda e, op_=op_, stt=stt, h=h, v_ap=v_ap: e.matmul(op_[:, h, :], lhsT=stt[:, (h * 2 + 1) * 64:(h * 2 + 2) * 64], rhs=v_ap, start=False, stop=False),
                         reads=[stb, db], writes=[opb])
                    S.op("pe", lambda e, op_=op_, tt=tt, h=h: e.matmul(op_[:, h, :], lhsT=tt[:, 0 * 2 + h, :], rhs=Sgb.t[0][:, h, :], start=False, stop=True),
                         reads=[ttb, Sgb.b[0]], writes=[opb])
                ss, ssb = SPs.next()
                for h in range(2):
                    v_ap = dt_[:, C_VA + h * 256:C_VA + (h + 1) * 256]
                    S.op("pe", lambda e, ss=ss, ga_=ga_, h=h, v_ap=v_ap: e.matmul(ss[:, h, :], lhsT=ga_[:, 2, h * 128:(h + 1) * 128], rhs=v_ap, start=True, stop=True),
                         reads=[gab, db], writes=[ssb])
                for h in range(2):
                    S.op("dve", lambda e, ss=ss, dec=dec, h=h: e.scalar_tensor_tensor(out=Sg.t[0][:, h, :], in0=Sg.t[0][:, h, :], scalar=dec[:, h:h + 1], in1=ss[:, h, :], op0=ALU.mult, op1=ALU.add),
                         reads=[ssb, decb, Sg.b[0]], writes=[Sg.b[0]])
                S.op("pool", lambda e: e.tensor_copy(out=Sgb.t[0][:], in_=Sg.t[0][:]), reads=[Sg.b[0]], writes=[Sgb.b[0]])
                spr, sprb = misc[0:64, 256:384], sprB
                for h in range(2):
                    for f in range(2):
                        S.op("pe", lambda e, spr=spr, tr=tr, h=h, f=f: e.matmul(spr[:, h * 64:(h + 1) * 64], lhsT=tr[:, 4 + h * 2 + f, :], rhs=tr[:, h * 2 + f, :], start=(f == 0), stop=(f == 1)),
                             reads=[trb], writes=[sprb])
                strt, strb = STr.next()
                S.op("dve", lambda e, strt=strt, spr=spr: e.tensor_tensor(out=strt[:].rearrange("p h i -> p (h i)"), in0=spr, in1=sc[0:64, SC_INTRA:SC_INTRA + 128], op=ALU.mult),
                     reads=[sprb, scb], writes=[strb])
                for h in range(2):
                    v_ap = dt_[:, C_VB + h * 256:C_VB + (h + 1) * 256]
                    S.op("pe", lambda e, op_=op_, strt=strt, h=h, v_ap=v_ap: e.matmul(op_[:, 2 + h, :], lhsT=strt[:, h, :], rhs=v_ap, start=True, stop=False),
                         reads=[strb, db], writes=[opb])
                    for f in range(2):
                        S.op("pe", lambda e, op_=op_, tq=tq, h=h, f=f: e.matmul(op_[:, 2 + h, :], lhsT=tq[:, h * 2 + f, :], rhs=Srb.t[0][:, h, f, :], start=False, stop=(f == 1)),
                             reads=[tqb, Srb.b[0]], writes=[opb])
                for h in range(2):
                    v_ap = dt_[:, C_VB + h * 256:C_VB + (h + 1) * 256]
                    ss, ssb = SPs.next()
                    for f in range(2):
                        S.op("pe", lambda e, ss=ss, kbr=kbr, h=h, f=f, v_ap=v_ap: e.matmul(ss[:, f, :], lhsT=kbr[:, h, f * 128:(f + 1) * 128], rhs=v_ap, start=True, stop=True),
                             reads=[kbrb, db], writes=[ssb])
                    S.op("dve", lambda e, ss=ss, h=h: e.scalar_tensor_tensor(out=Sr.t[0][:, h, :, :], in0=Sr.t[0][:, h, :, :], scalar=sc[:, SC_GCH + h:SC_GCH + h + 1], in1=ss[:],
                                                                            op0=ALU.mult, op1=ALU.add),
                         reads=[ssb, scb, Sr.b[0]], writes=[Sr.b[0]])
                S.op("pool", lambda e: e.tensor_copy(out=Srb.t[0][:], in_=Sr.t[0][:]), reads=[Sr.b[0]], writes=[Srb.b[0]])
                def out_cb(ot, ob, c=c):
                    if c <= 32:
                        S.dma("sp", O1[0, c * 64:(c + 1) * 64, :], ot[:, :], reads=[ob], writes=[kb.EXO.wbuf(0)])
                    if c >= 32:
                        S.dma("sp", O1[1, (c - 32) * 64:(c - 31) * 64, :], ot[:, :], reads=[ob], writes=[kb.EXO.wbuf(1)])

                oe, oeb = OEV.next()
                S.op("act", lambda e, oe=oe, op_=op_: e.copy(out=oe[:], in_=op_[:].rearrange("p h d -> p (h d)")), reads=[opb], writes=[oeb])
                rms_gate_store(kb, oe[:], oeb, 4, 256, 64, GN.t[0], GN.b[0], lambda sg=sg, sgb=sgb: (sg, sgb), OUT, out_cb, Wk)

        AHEAD = 3
        loads = {c: stage_l(c) for c in range(min(AHEAD, SEQC))}
        pend = stage_a(0, loads.pop(0))
        for c in range(1, SEQC):
            if c + AHEAD - 1 < SEQC:
                loads[c + AHEAD - 1] = stage_l(c + AHEAD - 1)
            nxt = stage_a(c, loads.pop(c))
            stage_b(pend)
            if pend["c"] == 32:
                kb.EXO.gather_d(0)
            pend = nxt
        stage_b(pend)
        kb.EXO.gather_d(1)
        kb.EXO.finish()
NP1 = 3072


def l1_proj(kb, h_ap, hB):
    dr = kb.dram
    w = dr["w_in_o"]
    P1 = kb.EX1.P3
    with kb.phase():
        XT = Tl(kb, "px", [128, 16, T], BF16)
        norm_T(kb, h_ap, hB, dr["mix_norm_o"], XT.t[0], XT.b[0])
        wblocks = []
        for dst in range(2):
            for base, dcol in ((0, 0), (2048, 1024), (4096, 2048)):
                for j in range(2):
                    c0 = base + dst * 1024 + j * 512
                    wblocks.append({"pieces": [(w[:, c0:c0 + 512], 0, 512)], "ncols": 512, "dst": dst, "dcol": dcol + j * 512})
        ST = Tl(kb, "pst", [128, 512], BF16, n=3)
        S_ = kb.S
        DTAB = Tl(kb, "pdt", [128, 17, 128], F32)
        S_.dma("sp", DTAB.t[0][:, 0:16, :], dr["dtab"][0:2048, :].rearrange("(i p) c -> p i c", p=128), writes=[DTAB.b[0]])
        S_.dma("sp", DTAB.t[0][0:64, 16, :], dr["dtab"][2048:T, :], add_writes=[DTAB.b[0]])
        GQK = Tl(kb, "pgqk", [128, 2, 128], F32)
        S_.dma("sp", GQK.t[0][:].rearrange("p a b -> p (a b)"), dr["qk_norm"].partition_broadcast(128), writes=[GQK.b[0]])
        SQ = Tl(kb, "psq", [128, 512], F32, n=2)
        MS4 = Tl(kb, "pms4", [128, 4], F32, n=3)
        XN = Tl(kb, "pxn", [128, 4, 128], F32, n=2)
        RA = Tl(kb, "pra", [128, 4, 2, 64], F32, n=2)
        RB = Tl(kb, "prb", [128, 4, 2, 64], F32, n=2)

        def ep(bi, ti, t0, ts, ps, pb, n):
            S = S_
            blk = wblocks[bi]
            b_ = kb.EX1.wbuf(blk["dst"])
            dst_ap = P1[blk["dst"], t0:t0 + ts, blk["dcol"]:blk["dcol"] + n]
            if blk["dcol"] < 2048:
                qk = 0 if blk["dcol"] < 1024 else 1
                sq, sqb = SQ.next()
                S.op("act", lambda e: e.activation(out=sq[:ts, :], in_=ps[:ts, :], func=AF.Square), reads=[pb], writes=[sqb])
                ms, msb = MS4.next()
                S.op("dve", lambda e: e.tensor_reduce(out=ms[:ts, :], in_=sq[:ts, :].rearrange("p (g d) -> p g d", g=4), axis=AX.X, op=ALU.add), reads=[sqb], writes=[msb])
                rstd_inplace(kb, ms[:ts, :], msb, ts, 1.0 / 128)
                xn, xnb = XN.next()
                S.op("dve", lambda e: e.tensor_tensor(out=xn[:ts], in0=ps[:ts, :].rearrange("p (g d) -> p g d", g=4), in1=ms[:ts, :].unsqueeze(2).to_broadcast([ts, 4, 128]), op=ALU.mult),
                     reads=[pb, msb], writes=[xnb])
                S.op("pool", lambda e: e.tensor_tensor(out=xn[:ts], in0=xn[:ts], in1=GQK.t[0][:ts, qk, :].unsqueeze(1).to_broadcast([ts, 4, 128]), op=ALU.mult),
                     reads=[xnb, GQK.b[0]], writes=[xnb])
                cosb = DTAB.t[0][:ts, ti, 0:64].unsqueeze(1).to_broadcast([ts, 4, 64])
                sinb = DTAB.t[0][:ts, ti, 64:128].unsqueeze(1).to_broadcast([ts, 4, 64])
                x1 = xn[:ts, :, 0:64]
                x2 = xn[:ts, :, 64:128]
                ra, rab = RA.next()
                rb, rbb = RB.next()
                S.op("dve", lambda e: e.tensor_tensor(out=ra[:ts, :, 0, :], in0=x1, in1=cosb, op=ALU.mult), reads=[xnb, DTAB.b[0]], writes=[rab])
                S.op("pool", lambda e: e.tensor_tensor(out=rb[:ts, :, 0, :], in0=x2, in1=sinb, op=ALU.mult), reads=[xnb, DTAB.b[0]], writes=[rbb])
                S.op("dve", lambda e: e.tensor_tensor(out=ra[:ts, :, 1, :], in0=x2, in1=cosb, op=ALU.mult), reads=[xnb, DTAB.b[0], rab], writes=[rab])
                S.op("pool", lambda e: e.tensor_tensor(out=rb[:ts, :, 1, :], in0=x1, in1=sinb, op=ALU.mult), reads=[xnb, DTAB.b[0], rbb], writes=[rbb])
                stt, stb = ST.next()
                so = stt[:ts, :].rearrange("p (g d) -> p g d", g=4)
                S.op("dve", lambda e: e.tensor_tensor(out=so[:, :, 0:64], in0=ra[:ts, :, 0, :], in1=rb[:ts, :, 0, :], op=ALU.subtract), reads=[rab, rbb], writes=[stb])
                S.op("pool", lambda e: e.tensor_tensor(out=so[:, :, 64:128], in0=ra[:ts, :, 1, :], in1=rb[:ts, :, 1, :], op=ALU.add), reads=[rab, rbb, stb], writes=[stb])
                S.dma("sp", dst_ap, stt[:ts, :n], reads=[stb], writes=[b_])
            else:
                stage_store(kb, ST, ps, pb, ts, n, dst_ap, [b_], eng="act")

        def after_block(bi):
            if bi == 5:
                kb.EX1.gather_d(0)

        gemm_tm(kb, 16, resident_lhsT(XT.t[0], XT.b[0]), wblocks, ep, after_block=after_block)
        kb.EX1.gather_d(1)
        kb.EX1.finish()


def seq_rows(n0, n):
    out = []
    c0, c1 = n0 // 64, (n0 + n) // 64
    c = c0
    while c < c1:
        if c <= 32:
            ce = min(c1, 33)
            out.append((0, c * 64, (ce - c) * 64, (c - c0) * 64))
        else:
            ce = c1
            out.append((1, (c - 32) * 64, (ce - c) * 64, (c - c0) * 64))
        c = ce
    return out


def l1_attn(kb):
    S = kb.S
    dr = kb.dram
    O2 = kb.EXO.P3
    SCALE = 128.0 ** -0.5
    NTT = 33
    with kb.phase():
        GD = Tl(kb, "aGD", [128, 256], F32)
        LV = Tl(kb, "aLV", [128, 4, 128], F32)
        LS = Tl(kb, "aLS", [128, 4], F32)
        NLAM = Tl(kb, "aNL", [128, 1], F32)
        KBI = Tl(kb, "aKB", [128, 1], F32)
        ZB = Tl(kb, "aZB", [128, 1], F32)
        S.dma("sp", GD.t[0][:], dr["diff_norm"].partition_broadcast(128), writes=[GD.b[0]])
        S.op("dve", lambda e: e.tensor_scalar(out=GD.t[0][:], in0=GD.t[0][:], scalar1=1.0 - LAM_INIT, scalar2=None, op0=ALU.mult), reads=[GD.b[0]], writes=[GD.b[0]])
        S.dma("sp", LV.t[0][:].rearrange("p a b -> p (a b)"), dr["lamv"].partition_broadcast(128), writes=[LV.b[0]])
        S.op("dve", lambda e: e.tensor_tensor(out=LV.t[0][:, 0, :], in0=LV.t[0][:, 0, :], in1=LV.t[0][:, 1, :], op=ALU.mult), reads=[LV.b[0]], writes=[LV.b[0]])
        S.op("dve", lambda e: e.tensor_tensor(out=LV.t[0][:, 2, :], in0=LV.t[0][:, 2, :], in1=LV.t[0][:, 3, :], op=ALU.mult), reads=[LV.b[0]], writes=[LV.b[0]])
        S.op("dve", lambda e: e.tensor_reduce(out=LS.t[0][:], in_=LV.t[0][:], axis=AX.X, op=ALU.add), reads=[LV.b[0]], writes=[LS.b[0]])
        S.op("act", lambda e: e.activation(out=LS.t[0][:], in_=LS.t[0][:], func=AF.Exp), reads=[LS.b[0]], writes=[LS.b[0]])
        S.op("dve", lambda e: e.tensor_tensor(out=NLAM.t[0][:], in0=LS.t[0][:, 2:3], in1=LS.t[0][:, 0:1], op=ALU.subtract), reads=[LS.b[0]], writes=[NLAM.b[0]])
        S.op("dve", lambda e: e.tensor_scalar(out=NLAM.t[0][:], in0=NLAM.t[0][:], scalar1=-LAM_INIT, scalar2=None, op0=ALU.add), reads=[NLAM.b[0]], writes=[NLAM.b[0]])
        S.dma("sp", KBI.t[0][:], dr["scc"][:, SC_KBIAS:SC_KBIAS + 1], writes=[KBI.b[0]], allow_slow_non_contiguous=True)
        S.op("dve", lambda e: e.memset(ZB.t[0][:], 0.0), writes=[ZB.b[0]])

        QKT = Tl(kb, "aQKT", [128, 4, LSEQ], BF16, n=2)
        VA = Tl(kb, "aVA", [128, NTT, 257], BF16, n=2)
        for i in range(2):
            S.op("pool", lambda e, i=i: e.memset(VA.t[i][:, :, 256:257], 1.0), writes=[VA.b[i]])
        X = Tl(kb, "aX", [128, 768], BF16, n=7)
        PTr = Tl(kb, "aPTr", [128, 4, 128], BF16, n=2, psum=True)
        PT = Tl(kb, "aPT", [128, 512], BF16, n=5)
        SPS = Tl(kb, "aSPS", [128, 512], F32, n=2, psum=True)
        OPS = Tl(kb, "aOPS", [128, 512], F32, n=4, psum=True)
        O1s = Tl(kb, "aO1s", [128, 4, 256], F32)
        O1sb = [Buf() for _ in range(4)]
        RD = Tl(kb, "aRD", [128, 1], F32, n=6)
        OC = Tl(kb, "aOC", [128, 256], F32, n=3)
        OUT = Tl(kb, "aOUT", [128, 256], BF16, n=3)
        Wk = {"SQ": Tl(kb, "aSQ2", [128, 256], F32), "MS": Tl(kb, "aMS", [128, 1], F32, n=2), "ON": Tl(kb, "aON", [128, 256], F32)}

        def prep_gen(h):
            qkt, qktb = QKT.t[h % 2], QKT.b[h % 2]
            va, vab = VA.t[h % 2], VA.b[h % 2]
            LOOK = 4

            def load(tt):
                n0 = tt * 128
                ts = min(128, LSEQ - n0)
                xt, xb = X.next()
                first = True
                for (src, r0, nr, doff) in seq_rows(n0, ts):
                    for ci, cbase in enumerate((0, 1024, 2048)):
                        apf, gb = kb.EX1.read(src, r0, nr, cbase + h * 256, 256)
                        S.dma("sp", xt[doff:doff + nr, ci * 256:(ci + 1) * 256], apf,
                              reads=gb, writes=[xb] if first else [], add_writes=[] if first else [xb])
                        first = False
                return xt, xb, n0, ts

            pending = [load(tt) for tt in range(min(LOOK, NTT))]
            for tt in range(NTT):
                if tt + LOOK < NTT:
                    pending.append(load(tt + LOOK))
                xt, xb, n0, ts = pending.pop(0)
                S.op("pool", lambda e, va=va, xt=xt, tt=tt, ts=ts: e.tensor_copy(out=va[:ts, tt, 0:256], in_=xt[:ts, 512:768]), reads=[xb], writes=[vab])
                transpose_rows(kb, xt, [xb], ts, 4, qkt, qktb, n0, PTr, evac="dve")
                yield

        def chunk_store(ot, ob, h, a, rows):
            for cc in range(rows // 64):
                c = a // 64 + cc
                dsts = ([(0, c)] if c <= 32 else []) + ([(1, c - 32)] if c >= 32 else [])
                for (d_, sl) in dsts:
                    S.dma("sp", O2[d_, sl * 64:(sl + 1) * 64, h * 256:(h + 1) * 256], ot[cc * 64:(cc + 1) * 64, :], reads=[ob], writes=[kb.EXO.wbuf(d_)])

        def attention(h, bg):
            qkt, qktb = QKT.t[h % 2], QKT.b[h % 2]
            va, vab = VA.t[h % 2], VA.b[h % 2]
            for q0 in range(0, LSEQ, 512):
                qn = min(512, LSEQ - q0)
                subs = [(a, min(128, q0 + qn - a)) for a in range(q0, q0 + qn, 128)]
                for p in range(2):
                    ops = [OPS.next() for _ in subs]
                    tiles = [t for t in range(NTT) if t * 128 < q0 + qn]

                    def stage1(t):
                        k0 = t * 128
                        kn = min(128, LSEQ - k0)
                        vstart = max(q0, k0)
                        n = q0 + qn - vstart
                        sp_, spb = SPS.next()
                        S.op("pe", lambda e, sp_=sp_, k0=k0, kn=kn, vstart=vstart, n=n, p=p: e.matmul(sp_[:kn, :n], lhsT=qkt[:, 2 + p, k0:k0 + kn], rhs=qkt[:, p, vstart:vstart + n],
                                                                                             start=True, stop=True), reads=[qktb], writes=[spb])
                        pt, ptb = PT.next()
                        bias_t, bias_b = (KBI, KBI.b[0]) if t == 0 else (ZB, ZB.b[0])
                        S.op("act", lambda e, pt=pt, sp_=sp_, kn=kn, n=n, bias_t=bias_t: e.activation(out=pt[:kn, :n], in_=sp_[:kn, :n], func=AF.Exp, scale=SCALE, bias=bias_t.t[0][:kn, :]),
                             reads=[spb, bias_b], writes=[ptb])
                        if k0 >= q0 and kn == 128:
                            S.op("pool", lambda e, pt=pt: e.memset(pt[64:128, 0:64], 0.0), reads=[ptb], writes=[ptb])
                        return (t, kn, vstart, pt, ptb)

                    def finish(m):
                        a, rows = subs[m]
                        op_, opb = ops[m]
                        rd, rdb = RD.next()
                        S.op("dve", lambda e, rd=rd, op_=op_, rows=rows: e.reciprocal(out=rd[:rows, :], in_=op_[:rows, 256:257]), reads=[opb], writes=[rdb])
                        if p == 0:
                            S.op("dve", lambda e, rd=rd, op_=op_, rows=rows, m=m: e.tensor_scalar(out=O1s.t[0][:rows, m, :], in0=op_[:rows, 0:256], scalar1=rd[:rows, 0:1], scalar2=None, op0=ALU.mult),
                                 reads=[opb, rdb], writes=[O1sb[m]])
                        else:
                            oc, ocb = OC.next()
                            S.op("dve", lambda e, rd=rd, op_=op_, rows=rows, oc=oc: e.tensor_scalar(out=oc[:rows, :], in0=op_[:rows, 0:256], scalar1=rd[:rows, 0:1], scalar2=NLAM.t[0][:rows, 0:1],
                                                                                                op0=ALU.mult, op1=ALU.mult), reads=[opb, rdb, NLAM.b[0]], writes=[ocb])
                            S.op("pool", lambda e, rows=rows, oc=oc, m=m: e.tensor_tensor(out=oc[:rows, :], in0=oc[:rows, :], in1=O1s.t[0][:rows, m, :], op=ALU.add),
                                 reads=[ocb, O1sb[m]], writes=[ocb])
                            rms_gate_store(kb, oc[:rows, :], ocb, 1, 256, rows, GD.t[0], GD.b[0], None, OUT,
                                           lambda ot, ob, a=a, rows=rows: chunk_store(ot, ob, h, a, rows), Wk)

                    def stage2(st):
                        t, kn, vstart, pt, ptb = st
                        for m, (a, rows) in enumerate(subs):
                            if a < vstart:
                                continue
                            rel = a - vstart
                            op_, opb = ops[m]
                            last = (t == a // 128)
                            S.op("pe", lambda e, op_=op_, pt=pt, kn=kn, rel=rel, rows=rows, t=t, last=last: e.matmul(op_[:rows, 0:257], lhsT=pt[:kn, rel:rel + rows], rhs=va[:kn, t, :],
                                                                                                                 start=(t == 0), stop=last), reads=[ptb, vab], writes=[opb])
                            if last:
                                finish(m)

                    SKEW = 2
                    pend = []
                    for t in tiles:
                        pend.append(stage1(t))
                        if len(pend) > SKEW:
                            stage2(pend.pop(0))
                    while pend:
                        stage2(pend.pop(0))
                    if bg is not None:
                        for _ in range(2):
                            next(bg, None)
            if bg is not None:
                for _ in bg:
                    pass

        g0 = prep_gen(0)
        for _ in g0:
            pass
        for h in range(4):
            bg = prep_gen(h + 1) if h + 1 < 4 else None
            attention(h, bg)
INPUT_SPECS = [
    ("xin", [T, D]), ("keep", [128, 17]), ("scc", [128, SC_N]), ("gnorm", [1, 1024]), ("rtab", [T, 256]), ("dtab", [T, 128]),
    ("mix_norm_e", [1, D]), ("w_in_e", [D, 7184]), ("gw2", [17, 512]), ("w_out_e", [D, D]),
    ("mix_norm_o", [1, D]), ("w_in_o", [D, 6144]), ("qk_norm", [1, 256]), ("lamv", [1, 512]), ("diff_norm", [1, 256]), ("w_out_o", [D, D]),
    ("ffn_norm", [2, D]), ("w_up", [2, D, 2 * DFF]), ("conv_w", [2, 3, 2 * DFF]), ("conv_b", [2, 2 * DFF]), ("w_down", [2, DFF, D]),
]
PHASES = ["none", "l0_proj", "l0_scan", "l0_out", "l0_ffn", "l1_proj", "l1_attn", "l1_out", "l1_ffn"]


def build(stop_after=None, dump=(), start_at=None):
    nc = bass.Bass("TRN2", target_bir_lowering=False)
    kb = KB(nc)
    S = kb.S
    for name, shape in INPUT_SPECS:
        kb.dt_in(name, shape)
    out = kb.dt_out("out", [2048, D])
    hbuf = kb.dt_int("hbuf", [T, D])
    kb.dt_int("aT", [DFF, T], BF16)
    kb.EX0 = Exchange(kb, "ex0", NP0, BF16, 256)
    kb.EXL = Exchange(kb, "exl", 256, F32, 1024)
    kb.EXO = Exchange(kb, "exo", 1024, BF16, 1024)
    kb.EX1 = Exchange(kb, "ex1", NP1, BF16, 256)
    kb.dram["P0"] = kb.EX0.P
    kb.dram["LA0"] = kb.EXL.P
    kb.dram["O1"] = kb.EXO.P
    kb.dram["P1"] = kb.EX1.P
    dr = kb.dram
    dump_out = {}
    for name in dump:
        src = dr[name]
        dump_out[name] = nc.dram_tensor("dbg_" + name, list(src.shape), src.dtype, kind="ExternalOutput").ap()
    with contextlib.ExitStack() as st:
        kb.st = st
        ident = Tl(kb, "ident", [128, 128], BF16)
        kb.ident, kb.ident_b = ident.t[0], ident.b[0]
        build_identity(kb, kb.ident, kb.ident_b)
        keep = Tl(kb, "keep", [128, 17], F32)
        kb.keep, kb.keep_b = keep.t[0], keep.b[0]
        S.dma("sp", kb.keep[:], dr["keep"], writes=[kb.keep_b])
        epsb = Tl(kb, "epsb", [128, 1], F32)
        kb.epsb, kb.epsb_b = epsb.t[0], epsb.b[0]
        S.op("dve", lambda e: e.memset(kb.epsb[:], EPS), writes=[kb.epsb_b])
        oneb = Tl(kb, "oneb", [128, 1], F32)
        kb.oneb, kb.oneb_b = oneb.t[0], oneb.b[0]
        S.op("dve", lambda e: e.memset(kb.oneb[:], 1.0), writes=[kb.oneb_b])
        hB = [[Buf() for _ in range(4)] for _ in TILES]
        allh = [b for row in hB for b in row]
        S.dma("sp", hbuf, dr["xin"], writes=allh)
        S.barrier()

        def done(phase):
            return stop_after is not None and PHASES.index(phase) >= PHASES.index(stop_after)

        def active(phase):
            return start_at is None or PHASES.index(phase) >= PHASES.index(start_at)

        def run():
            if done("none"):
                return
            if active("l0_proj"):
                l0_proj(kb, hbuf, hB)
            if done("l0_proj"):
                return
            if active("l0_scan"):
                l0_scan(kb)
            if done("l0_scan"):
                return
            if active("l0_out"):
                cm0 = [(0, 0, 0, 512), (0, 512, 1024, 512), (1, 0, 512, 512), (1, 512, 1536, 512)]
                mixer_out_proj(kb, kb.EXO, cm0, dr["w_out_e"], hbuf, hB)
            if done("l0_out"):
                return
            if active("l0_ffn"):
                ffn(kb, 0, hbuf, hB)
            if done("l0_ffn"):
                return
            if active("l1_proj"):
                l1_proj(kb, hbuf, hB)
            if done("l1_proj"):
                return
            if active("l1_attn"):
                l1_attn(kb)
                kb.EXO.gather()
            if done("l1_attn"):
                return
            if active("l1_out"):
                cm1 = [(0, 0, 0, 1024), (1, 0, 1024, 1024)]
                mixer_out_proj(kb, kb.EXO, cm1, dr["w_out_o"], hbuf, hB)
            if done("l1_out"):
                return
            if active("l1_ffn"):
                ffn(kb, 1, hbuf, hB)

        run()
        S.barrier()
        S.dma("sp", out, hbuf[64:T, :], reads=allh, writes=[Buf()])
        for name in dump:
            S.dma("sp", dump_out[name], dr[name], writes=[Buf()])
        S.emit()
    return nc


def _consts(rank):
    scc = np.zeros((128, SC_N), np.float64)
    j = np.arange(64)[:, None]
    i = np.arange(64)[None, :]
    for hl in range(2):
        h = 2 * rank + hl
        lg = math.log(1.0 - 2.0 ** (-5.0 - h))
        scc[0:64, SC_INTRA + hl * 64:SC_INTRA + (hl + 1) * 64] = np.exp(np.abs(i - j) * lg) / 16.0
        scc[:, SC_XIB + hl * 64:SC_XIB + (hl + 1) * 64] = (np.exp((np.arange(64) + 1.0) * lg) / 16.0)[None, :]
        scc[0:64, SC_ZETA + hl] = np.exp((63.0 - np.arange(64)) * lg)
        scc[:, SC_GCH + hl] = math.exp(64.0 * lg)
        scc[0:64, SC_MASK + (hl * 2 + 0) * 64:SC_MASK + (hl * 2 + 1) * 64] = (i >= j)
        scc[0:64, SC_MASK + (hl * 2 + 1) * 64:SC_MASK + (hl * 2 + 2) * 64] = (i < j) * (128.0 ** -0.5)
    scc[0:64, SC_TRIU:SC_TRIU + 64] = (i >= j)
    scc[0:64, SC_TRIL:SC_TRIL + 64] = (i < j)
    scc[0:48, SC_KBIAS] = -30000.0
    return scc.astype(np.float32)


def _tables():
    f32 = np.float32
    pos = (np.arange(LSEQ) - 48).astype(f32)
    inv_r = (f32(1.0) / (f32(10000.0) ** np.linspace(0.0, 1.0, 128, dtype=f32))).astype(f32)
    ang = (pos[:, None] * inv_r[None, :]).astype(f32)
    ret_tab = np.concatenate([np.cos(ang), np.sin(ang)], axis=1).astype(f32)
    inv_d = (f32(1.0) / (f32(10000.0) ** (np.arange(0, 128, 2, dtype=f32) / f32(128)))).astype(f32)
    ang = (pos[:, None] * inv_d[None, :]).astype(f32)
    diff_tab = np.concatenate([np.cos(ang), np.sin(ang)], axis=1).astype(f32)
    return ret_tab, diff_tab


def make_in_maps(x, meta, mix_norm_e, w_in_e, gla_w_gate_e, gla_b_gate_e, gla_norm_e, ret_norm_e, w_out_e,
                 mix_norm_o, w_in_o, q_norm_o, k_norm_o, lam_q1_o, lam_k1_o, lam_q2_o, lam_k2_o, diff_norm_o, w_out_o,
                 ffn_norm, w_up, conv_w, conv_b, w_down):
    f = lambda a: np.ascontiguousarray(np.asarray(a, dtype=np.float32))
    x = f(x)
    meta = f(meta)
    ret_tab, diff_tab = _tables()
    shared = {
        "mix_norm_e": f(mix_norm_e).reshape(1, D), "w_in_e": f(w_in_e)[0],
        "gw2": np.concatenate([f(gla_w_gate_e)[0], f(gla_b_gate_e)[0][None, :]], axis=0),
        "w_out_e": f(w_out_e)[0], "mix_norm_o": f(mix_norm_o).reshape(1, D), "w_in_o": f(w_in_o)[0],
        "qk_norm": np.concatenate([f(q_norm_o)[0], f(k_norm_o)[0]])[None, :],
        "lamv": np.concatenate([f(lam_q1_o)[0], f(lam_k1_o)[0], f(lam_q2_o)[0], f(lam_k2_o)[0]])[None, :],
        "diff_norm": f(diff_norm_o).reshape(1, 256), "w_out_o": f(w_out_o)[0],
        "ffn_norm": f(ffn_norm), "w_up": f(w_up), "conv_w": f(conv_w), "conv_b": f(conv_b), "w_down": f(w_down),
    }
    consts = [_consts(0), _consts(1)]
    gn = f(gla_norm_e)[0]
    rn = f(ret_norm_e)[0]
    in_maps = []
    for c in range(8):
        b, r = c // 2, c % 2
        if r == 0:
            xin = np.concatenate([np.zeros((48, D), np.float32), meta, x[b, 0:2048]], axis=0)
        else:
            xin = x[b, 1984:4096]
        tok = np.arange(17 * 128)
        valid = (tok < T) & ((tok >= 48) if r == 0 else True)
        keep = np.ascontiguousarray(valid.reshape(17, 128).T.astype(np.float32))
        m = dict(shared)
        m["xin"] = np.ascontiguousarray(xin)
        m["keep"] = keep
        m["scc"] = consts[r]
        lo = 0 if r == 0 else 2048
        m["rtab"] = np.ascontiguousarray(ret_tab[lo:lo + T])
        m["dtab"] = np.ascontiguousarray(diff_tab[lo:lo + T])
        m["gnorm"] = np.concatenate([gn[2 * r:2 * r + 2].reshape(-1), rn[2 * r:2 * r + 2].reshape(-1)])[None, :]
        in_maps.append(m)
    return in_maps


_NC_CACHE = {}


def kernel(**inputs):
    in_maps = make_in_maps(**inputs)
    if "nc" not in _NC_CACHE:
        _NC_CACHE["nc"] = build()
    res = run_bass_kernel_spmd(_NC_CACHE["nc"], in_maps, core_ids=list(range(8)))
    outp = np.zeros((4, 4096, D), np.float32)
    for c in range(8):
        b, r = c // 2, c % 2
        outp[b, r * 2048:(r + 1) * 2048] = res.results[c]["out"]
    return outp
```

```python
import contextlib
import math
import numpy as np
import concourse.bass as bass
import concourse.mybir as mybir
from concourse.bass_utils import run_bass_kernel_spmd

F32 = mybir.dt.float32
BF16 = mybir.dt.bfloat16
AF = mybir.ActivationFunctionType
ALU = mybir.AluOpType
AX = mybir.AxisListType

D = 2048
T = 2112
NSL = 33
TILES = [(i * 128, min(128, T - i * 128)) for i in range(17)]
SEQC = 65
LSEQ = SEQC * 64
DFF = 5632
EPS = 1e-6
LAM_INIT = 0.8 - 0.6 * math.exp(-0.3 * 1)
NSLOT = 8
NCC = 8


class Buf:
    __slots__ = ("name", "w", "r", "rd", "excl")

    def __init__(self, name="", excl=False):
        self.name = name
        self.w = []
        self.r = {}
        self.rd = []
        self.excl = excl


class Sched:
    ENG = ("pe", "act", "dve", "pool", "sp")
    DMAQ = ("sp", "pool", "act")

    def __init__(self, nc):
        self.nc = nc
        self.prog = {e: [] for e in self.ENG}
        self.cnt = {e: 0 for e in self.ENG}
        self.dcnt = {q: 0 for q in self.DMAQ}
        self.ccnt = 0
        self.seen = {e: {} for e in self.ENG}
        self.pending = {e: {} for e in self.ENG}

    def _semkey(self, ev):
        if ev[0] == "e":
            return ("e", ev[1]), ev[2]
        if ev[0] == "c":
            return ("c", ev[1]), 1
        q, k = ev[1], ev[2]
        return ("d", q, k % NSLOT), 16 * (k // NSLOT + 1)

    def _collect(self, X, reads, writes):
        need = dict(self.pending[X])
        self.pending[X] = {}
        seen = self.seen[X]

        def add(ev):
            if ev is None:
                return
            if ev[0] == "e" and ev[1] == X and X == "pe":
                return
            key, val = self._semkey(ev)
            if seen.get(key, 0) >= val:
                return
            if need.get(key, 0) < val:
                need[key] = val

        excl_reads = [b for b in reads if b.excl]
        for b in reads:
            for ev in b.w:
                add(ev)
        for b in list(writes) + excl_reads:
            for ev in b.w:
                add(ev)
            for ev in b.r.values():
                add(ev)
            for ev in b.rd:
                add(ev)
        out = []
        for key, val in need.items():
            if seen.get(key, 0) < val:
                seen[key] = val
                out.append((key, val))
        return out

    def _mark(self, my, reads, writes, is_dma):
        writes = list(writes) + [b for b in reads if b.excl]
        for b in reads:
            if b.excl:
                continue
            if is_dma:
                b.rd.append(my)
            else:
                b.r[my[1]] = my
        for b in writes:
            b.w = [my]
            b.r = {}
            b.rd = []

    def op(self, eng, fn, reads=(), writes=()):
        waits = self._collect(eng, reads, writes)
        self.cnt[eng] += 1
        my = ("e", eng, self.cnt[eng])
        self.prog[eng].append((waits, fn, ("e", eng), 1))
        self._mark(my, reads, writes, False)
        return my

    def dma(self, q, out_ap, in_ap, reads=(), writes=(), add_writes=(), **kw):
        k = self.dcnt[q]
        waits = self._collect(q, reads, writes)
        if k >= NSLOT:
            key, val = ("d", q, k % NSLOT), 16 * (k // NSLOT)
            if self.seen[q].get(key, 0) < val:
                self.seen[q][key] = val
                waits.append((key, val))
        self.dcnt[q] += 1
        my = ("d", q, k)

        def fn(e):
            o = out_ap(e) if callable(out_ap) else out_ap
            i = in_ap(e) if callable(in_ap) else in_ap
            return e.dma_start(out=o, in_=i, **kw)

        self.prog[q].append((waits, fn, ("d", q, k % NSLOT), 16))
        self._mark(my, reads, writes, True)
        for b in add_writes:
            b.w.append(my)
        return my

    def cc(self, fn, reads=(), writes=()):
        i = self.ccnt
        self.ccnt += 1
        waits = self._collect("pool", reads, writes)
        my = ("c", i)
        self.prog["pool"].append((waits, fn, ("c", i), None))
        self._mark(my, reads, writes, True)
        return my

    def barrier(self):
        allw = {}
        for e in self.ENG:
            if self.cnt[e]:
                allw[("e", e)] = self.cnt[e]
        for q in self.DMAQ:
            n = self.dcnt[q]
            for i in range(min(n, NSLOT)):
                last_k = ((n - 1 - i) // NSLOT) * NSLOT + i
                allw[("d", q, i)] = 16 * (last_k // NSLOT + 1)
        for i in range(self.ccnt):
            allw[("c", i)] = 1
        for e in self.ENG:
            p = self.pending[e]
            for k, v in allw.items():
                if k == ("e", e) and e == "pe":
                    continue
                if p.get(k, 0) < v:
                    p[k] = v

    def emit(self, final_eng="sp"):
        nc = self.nc
        self.barrier()
        with contextlib.ExitStack() as st:
            sems = {}
            for e in self.ENG:
                sems[("e", e)] = st.enter_context(nc.semaphore(f"s_{e}"))
            for q in self.DMAQ:
                for i in range(NSLOT):
                    sems[("d", q, i)] = st.enter_context(nc.semaphore(f"d_{q}{i}"))
            for i in range(self.ccnt):
                sems[("c", i)] = st.enter_context(nc.semaphore(f"c_{i}"))
            fin = [(k, v) for k, v in self.pending[final_eng].items() if self.seen[final_eng].get(k, 0) < v]
            block = st.enter_context(nc.Block())
            prog = self.prog

            def run(engname, eng):
                for waits, fn, inc, amt in prog[engname]:
                    for key, val in waits:
                        eng.wait_ge(sems[key], val)
                    ins = fn(eng)
                    if amt is None:
                        ins.then_inc(sems[inc])
                    else:
                        ins.then_inc(sems[inc], amt)
                if engname == final_eng:
                    for key, val in fin:
                        eng.wait_ge(sems[key], val)

            @block.tensor
            def _(e):
                run("pe", e)

            @block.scalar
            def _(e):
                run("act", e)

            @block.vector
            def _(e):
                run("dve", e)

            @block.gpsimd
            def _(e):
                run("pool", e)

            @block.sync
            def _(e):
                run("sp", e)


class Tl:
    def __init__(self, kb, name, shape, dt, n=1, psum=False):
        self.t = []
        self.b = []
        for i in range(n):
            nm = f"{name}_{kb.uid()}"
            if psum:
                t = kb.st.enter_context(kb.nc.psum_tensor(nm, shape, dt))
            else:
                t = kb.st.enter_context(kb.nc.sbuf_tensor(nm, shape, dt))
            self.t.append(t)
            self.b.append(Buf(nm, excl=psum))
        self.n = n
        self.i = -1

    def next(self):
        self.i = (self.i + 1) % self.n
        return self.t[self.i], self.b[self.i]


class KB:
    def __init__(self, nc):
        self.nc = nc
        self.S = Sched(nc)
        self.st = None
        self._uid = 0
        self._rank = {}
        self.dram = {}
        self.dbuf = {}

    def uid(self):
        self._uid += 1
        return self._uid

    def rank(self, e):
        key = id(e)
        if key not in self._rank:
            self._rank[key] = e.snap(e.partition_id() % 2)
        return self._rank[key]

    def dt_in(self, name, shape, dt=F32):
        self.dram[name] = self.nc.dram_tensor(name, list(shape), dt, kind="ExternalInput").ap()
        return self.dram[name]

    def dt_out(self, name, shape, dt=F32):
        self.dram[name] = self.nc.dram_tensor(name, list(shape), dt, kind="ExternalOutput").ap()
        return self.dram[name]

    def dt_int(self, name, shape, dt=F32):
        self.dram[name] = self.nc.dram_tensor(name, list(shape), dt, kind="Internal").ap()
        return self.dram[name]

    @contextlib.contextmanager
    def phase(self):
        old = self.st
        with contextlib.ExitStack() as st:
            self.st = st
            yield
            self.S.barrier()
        self.st = old


def build_identity(kb, ident_t, ident_b):
    S = kb.S
    S.op("pool", lambda e: e.memset(ident_t[:], 0.0), writes=[ident_b])
    S.op("pool", lambda e: e.affine_select(out=ident_t[:], in_=ident_t[:], pattern=[[-1, 128]], compare_op=ALU.not_equal,
                                           fill=1.0, base=0, channel_multiplier=1), reads=[ident_b], writes=[ident_b])


@contextlib.contextmanager
def subscope(kb):
    old = kb.st
    with contextlib.ExitStack() as st:
        kb.st = st
        yield
        kb.S.barrier()
    kb.st = old


def transpose_rows(kb, src_t, src_bufs, ts, nk, dstT, dst_b, tok0, PT, col0=0, dk0=0, evac="act"):
    S = kb.S
    ident, ident_b = kb.ident, kb.ident_b
    for k0 in range(0, nk, 4):
        kn = min(4, nk - k0)
        pt, pb = PT.next()
        for kk in range(kn):
            k = k0 + kk
            S.op("pe", lambda e, k=k, kk=kk, pt=pt: e.transpose(out=pt[:, kk, :ts], in_=src_t[:ts, col0 + k * 128:col0 + (k + 1) * 128],
                                                              identity=ident[:ts, :ts]),
                 reads=list(src_bufs) + [ident_b], writes=[pb])
        if evac == "act":
            S.op("act", lambda e, k0=k0, kn=kn, pt=pt: e.copy(out=dstT[:, dk0 + k0:dk0 + k0 + kn, tok0:tok0 + ts], in_=pt[:, :kn, :ts]),
                 reads=[pb], writes=[dst_b])
        else:
            S.op("dve", lambda e, k0=k0, kn=kn, pt=pt: e.tensor_copy(out=dstT[:, dk0 + k0:dk0 + k0 + kn, tok0:tok0 + ts], in_=pt[:, :kn, :ts]),
                 reads=[pb], writes=[dst_b])


def rstd_inplace(kb, ms_ap, msb, rows, scale):
    S = kb.S
    S.op("act", lambda e: e.activation(out=ms_ap, in_=ms_ap, func=AF.Ln, scale=scale, bias=kb.epsb[:rows, :]), reads=[msb, kb.epsb_b], writes=[msb])
    S.op("act", lambda e: e.activation(out=ms_ap, in_=ms_ap, func=AF.Exp, scale=-0.5), reads=[msb], writes=[msb])


def norm_T(kb, h_ap, hB, gain_row_ap, XT, XTb):
    S = kb.S
    with subscope(kb):
        G = Tl(kb, "ng", [128, D], F32)
        H = Tl(kb, "nh", [128, D], F32, n=5)
        J = Tl(kb, "nj", [128, D], F32)
        XN = Tl(kb, "nx", [128, D], BF16, n=3)
        MS = Tl(kb, "nms", [128, 1], F32, n=3)

        PT = Tl(kb, "npt", [128, 4, 128], BF16, n=2, psum=True)
        S.dma("sp", G.t[0][:], gain_row_ap.partition_broadcast(128), writes=[G.b[0]])
        def stage_l(ti):
            t0, ts = TILES[ti]
            ht, hb = H.next()
            S.dma("sp", ht[:ts, :], h_ap[t0:t0 + ts, :], reads=hB[ti], writes=[hb])
            return ht, hb

        AHEAD = 3
        lds = {ti: stage_l(ti) for ti in range(AHEAD)}

        def stage_a(ti):
            t0, ts = TILES[ti]
            if ti + AHEAD < len(TILES):
                lds[ti + AHEAD] = stage_l(ti + AHEAD)
            ht, hb = lds.pop(ti)
            ms, msb = MS.next()
            S.op("act", lambda e, ht=ht, ms=ms, ts=ts: e.activation(out=J.t[0][:ts, :], in_=ht[:ts, :], func=AF.Square, accum_out=ms[:ts, :]),
                 reads=[hb], writes=[J.b[0], msb])
            rstd_inplace(kb, ms[:ts, :], msb, ts, 1.0 / D)
            xn, xnb = XN.next()
            S.op("dve", lambda e, ht=ht, ms=ms, xn=xn, ts=ts: e.scalar_tensor_tensor(out=xn[:ts, :], in0=ht[:ts, :], scalar=ms[:ts, 0:1], in1=G.t[0][:ts, :],
                                                                                    op0=ALU.mult, op1=ALU.mult),
                 reads=[hb, msb, G.b[0]], writes=[xnb])
            return xn, xnb

        def stage_b(ti, xn, xnb):
            t0, ts = TILES[ti]
            transpose_rows(kb, xn, [xnb], ts, 16, XT, XTb, t0, PT, evac="act" if ti % 2 == 0 else "dve")

        pend = stage_a(0)
        for ti in range(1, len(TILES)):
            nxt = stage_a(ti)
            stage_b(ti - 1, *pend)
            pend = nxt
        stage_b(len(TILES) - 1, *pend)


def gemm_tm(kb, KC, lhsT_prep, wblocks, epilogue, wbufs=2, pbufs=4, tiles=TILES, after_block=None, pre_tile=None):
    S = kb.S
    with subscope(kb):
        W = Tl(kb, "gw", [128, KC, 512], BF16, n=wbufs)
        PS = Tl(kb, "gp", [128, 512], F32, n=pbufs, psum=True)
        wt_list = []

        def load(bi):
            wt, wb = W.next()
            blk = wblocks[bi]
            for pi, (src_ap, off, wd) in enumerate(blk["pieces"]):
                S.dma("pool", wt[:, :, off:off + wd], src_ap.rearrange("(k p) c -> p k c", p=128),
                      writes=[wb] if pi == 0 else [], add_writes=[] if pi == 0 else [wb])
            wt_list.append((wt, wb))

        load(0)
        for bi, blk in enumerate(wblocks):
            if bi + 1 < len(wblocks):
                load(bi + 1)
            wt, wb = wt_list[bi]
            n = blk["ncols"]
            for ti, (t0, ts) in enumerate(tiles):
                if pre_tile is not None:
                    pre_tile(bi, ti, t0, ts, n)
                lfn, lbufs = lhsT_prep(bi, ti, t0, ts)
                ps, pb = PS.next()
                for k in range(KC):
                    S.op("pe", lambda e, ps=ps, lap=lfn(k), wt=wt, k=k, n=n, ts=ts: e.matmul(ps[:ts, :n], lhsT=lap, rhs=wt[:, k, :n],
                                                                                          start=(k == 0), stop=(k == KC - 1)),
                         reads=list(lbufs) + [wb], writes=[pb])
                epilogue(bi, ti, t0, ts, ps, pb, n)
            if after_block is not None:
                after_block(bi)


def resident_lhsT(XT, XTb):
    def prep(bi, ti, t0, ts):
        return (lambda k: XT[:, k, t0:t0 + ts]), [XTb]
    return prep


def residual_epilogue(kb, h_ap, hB, HS):
    S = kb.S
    held = {}

    def pre(bi, ti, t0, ts, n):
        c0 = bi * 512
        hs, hsb = HS.next()
        S.dma("sp", hs[:ts, :n], h_ap[t0:t0 + ts, c0:c0 + n], reads=[hB[ti][bi]], writes=[hsb])
        held[(bi, ti)] = (hs, hsb)

    def ep(bi, ti, t0, ts, ps, pb, n):
        c0 = bi * 512
        hs, hsb = held.pop((bi, ti))
        S.op("dve", lambda e: e.scalar_tensor_tensor(out=hs[:ts, :n], in0=ps[:ts, :n], scalar=kb.keep[:ts, ti:ti + 1], in1=hs[:ts, :n],
                                                     op0=ALU.mult, op1=ALU.add),
             reads=[pb, hsb, kb.keep_b], writes=[hsb])
        S.dma("sp", h_ap[t0:t0 + ts, c0:c0 + n], hs[:ts, :n], reads=[hsb], writes=[hB[ti][bi]])

    return pre, ep


GROUPS = [[0, 1], [2, 3], [4, 5], [6, 7]]


class Exchange:
    def __init__(self, kb, name, ncols, dt, rpg):
        self.kb, self.ncols, self.rpg = kb, ncols, rpg
        self.P = kb.dt_int(name + "_p", [2 * T, ncols], dt)
        self.P3 = self.P.rearrange("(d t) c -> d t c", d=2)
        self.groups = [(j0, min(rpg, T - j0)) for j0 in range(0, T, rpg)]
        self.G = [kb.dt_int(f"{name}_g{j}", [2, 2 * rows, ncols], dt) for j, (j0, rows) in enumerate(self.groups)]
        self.Gb = [[Buf(), Buf()] for _ in self.groups]
        self.MY = kb.dt_int(name + "_my", [2 * T, ncols], dt)
        self.MY3 = self.MY.rearrange("(s t) c -> s t c", s=2)
        self.MYb = [Buf() for _ in self.groups]
        self.my_pending = set()
        self.Pb = [[], []]

    def wbuf(self, d):
        b_ = Buf()
        self.Pb[d].append(b_)
        return b_

    def gather_d(self, d):
        S = self.kb.S
        for j, (j0, rows) in enumerate(self.groups):
            src_ap = self.P3[d, j0:j0 + rows, :]
            dst_ap = self.G[j][d]
            S.cc(lambda e, src_ap=src_ap, dst_ap=dst_ap: e.collective_compute("AllGather", ALU.bypass, replica_groups=[list(g) for g in GROUPS],
                                                                              ins=[src_ap.opt()], outs=[dst_ap.opt()]),
                 reads=self.Pb[d], writes=[self.Gb[j][d]])
        self.Pb[d] = []

    def finish(self):
        self.my_pending = set(range(len(self.groups)))

    def _ensure_my(self, j):
        if j not in self.my_pending:
            return
        self.my_pending.discard(j)
        S = self.kb.S
        kb = self.kb
        j0, rows = self.groups[j]
        G = self.G[j]
        S.dma("pool", self.MY3[:, j0:j0 + rows, :],
              (lambda e, G=G: G[bass.ds(kb.rank(e), 1), :, :].rearrange("o (s t) c -> (o s) t c", s=2)),
              reads=self.Gb[j], writes=[self.MYb[j]])

    def gather(self):
        self.gather_d(0)
        self.gather_d(1)
        self.finish()

    def read(self, src, t0, n, c0, wd):
        j = t0 // self.rpg
        j0, rows = self.groups[j]
        assert t0 + n <= j0 + rows
        self._ensure_my(j)
        return self.MY3[src, t0:t0 + n, c0:c0 + wd], [self.MYb[j]]


def mixer_out_proj(kb, EX, colmap, w_ap, h_ap, hB):
    S = kb.S
    with kb.phase():
        XT = Tl(kb, "mx", [128, 16, T], BF16)
        with subscope(kb):
            O = Tl(kb, "go", [128, D], BF16, n=3)
            PT = Tl(kb, "gpt", [128, 4, 128], BF16, n=2, psum=True)
            for ti, (t0, ts) in enumerate(TILES):
                ot, ob = O.next()
                for pi, (src, c_src, c_dst, wd) in enumerate(colmap):
                    apf, gb = EX.read(src, t0, ts, c_src, wd)
                    S.dma("sp", ot[:ts, c_dst:c_dst + wd], apf,
                          reads=gb, writes=[ob] if pi == 0 else [], add_writes=[] if pi == 0 else [ob])
                transpose_rows(kb, ot, [ob], ts, 16, XT.t[0], XT.b[0], t0, PT)
        HS = Tl(kb, "mh", [128, 512], F32, n=6)
        wblocks = [{"pieces": [(w_ap[:, c0:c0 + 512], 0, 512)], "ncols": 512} for c0 in range(0, D, 512)]
        pre, ep = residual_epilogue(kb, h_ap, hB, HS)
        gemm_tm(kb, 16, resident_lhsT(XT.t[0], XT.b[0]), wblocks, ep, pre_tile=pre)


def ffn(kb, layer, h_ap, hB):
    S = kb.S
    dr = kb.dram
    aT = dr["aT"]
    aTb = [Buf() for _ in range(44)]
    with kb.phase():
        XT = Tl(kb, "fx", [128, 16, T], BF16)
        norm_T(kb, h_ap, hB, dr["ffn_norm"][layer:layer + 1, :], XT.t[0], XT.b[0])
        W = Tl(kb, "fw", [128, 16, 2, 256], BF16, n=2)
        CW = Tl(kb, "fcw", [128, 3, 88], F32)
        CB = Tl(kb, "fcb", [128, 88], F32)
        U = Tl(kb, "fu", [128, 2, T + 2], F32, n=2)
        C = Tl(kb, "fc", [128, 2, T], F32, n=2)
        A = Tl(kb, "fa", [128, T], BF16, n=2)
        PS = Tl(kb, "fp", [128, 512], F32, n=4, psum=True)
        S.dma("sp", CW.t[0][:], dr["conv_w"][layer].rearrange("t (c p) -> p t c", p=128), writes=[CW.b[0]], allow_slow_non_contiguous=True)
        S.dma("sp", CB.t[0][:], dr["conv_b"][layer:layer + 1, :].rearrange("o (c p) -> p (o c)", p=128), writes=[CB.b[0]], allow_slow_non_contiguous=True)
        for i in range(2):
            S.op("pool", lambda e, i=i: e.memset(U.t[i][:, :, 0:2], 0.0), writes=[U.b[i]])
        w_up = dr["w_up"][layer]
        NB = 22
        wl = []

        def loadw(b):
            wt, wb = W.next()
            for vg in range(2):
                c0 = vg * DFF + b * 256
                S.dma("pool", wt[:, :, vg, :], w_up[:, c0:c0 + 256].rearrange("(k p) c -> p k c", p=128),
                      writes=[wb] if vg == 0 else [], add_writes=[] if vg == 0 else [wb])
            wl.append((wt, wb))

        tslices = [(s0, min(512, T - s0)) for s0 in range(0, T, 512)]
        loadw(0)
        for b in range(NB):
            if b + 1 < NB:
                loadw(b + 1)
            wt, wb = wl[b]
            for j in range(2):
                fc = b * 2 + j
                ut, ub = U.next()
                for vg in range(2):
                    for (s0, sn) in tslices:
                        ps, pb = PS.next()
                        for k in range(16):
                            S.op("pe", lambda e, ps=ps, wt=wt, k=k, vg=vg, j=j, s0=s0, sn=sn: e.matmul(
                                ps[:, :sn], lhsT=wt[:, k, vg, j * 128:(j + 1) * 128], rhs=XT.t[0][:, k, s0:s0 + sn], start=(k == 0), stop=(k == 15)),
                                reads=[wb, XT.b[0]], writes=[pb])
                        S.op("act", lambda e, ps=ps, ut=ut, vg=vg, s0=s0, sn=sn: e.copy(out=ut[:, vg, 2 + s0:2 + s0 + sn], in_=ps[:, :sn]),
                             reads=[pb], writes=[ub])
                ct, cb = C.next()
                for vg in range(2):
                    ch = vg * 44 + fc
                    S.op("act", lambda e, ut=ut, ct=ct, vg=vg, ch=ch: e.activation(out=ct[:, vg, :], in_=ut[:, vg, 0:T], func=AF.Identity,
                                                                               scale=CW.t[0][:, 0, ch:ch + 1], bias=CB.t[0][:, ch:ch + 1]),
                         reads=[ub, CW.b[0], CB.b[0]], writes=[cb])
                    for tap in (1, 2):
                        S.op("dve", lambda e, ut=ut, ct=ct, vg=vg, ch=ch, tap=tap: e.scalar_tensor_tensor(
                            out=ct[:, vg, :], in0=ut[:, vg, tap:tap + T], scalar=CW.t[0][:, tap, ch:ch + 1], in1=ct[:, vg, :], op0=ALU.mult, op1=ALU.add),
                            reads=[ub, cb, CW.b[0]], writes=[cb])
                S.op("act", lambda e, ct=ct: e.activation(out=ct[:, 1, :], in_=ct[:, 1, :], func=AF.Silu), reads=[cb], writes=[cb])
                at, ab = A.next()
                S.op("dve", lambda e, ct=ct, at=at: e.tensor_tensor(out=at[:], in0=ct[:, 0, :], in1=ct[:, 1, :], op=ALU.mult), reads=[cb], writes=[ab])
                S.dma("sp", aT[fc * 128:(fc + 1) * 128, :], at[:], reads=[ab], writes=[aTb[fc]])
    with kb.phase():
        HS = Tl(kb, "dh", [128, 512], F32, n=6)
        AT = Tl(kb, "da", [128, 44, 256], BF16, n=2)
        w_down = dr["w_down"][layer]
        wblocks = [{"pieces": [(w_down[:, c0:c0 + 512], 0, 512)], "ncols": 512} for c0 in range(0, D, 512)]
        aT3 = aT.rearrange("(k p) t -> p k t", p=128)
        loaded = {}
        order = [(bi, g) for bi in range(len(wblocks)) for g in range((len(TILES) + 1) // 2)]

        def load_group(key):
            if key in loaded:
                return
            at, ab = AT.next()
            g0 = key[1] * 256
            gn = min(256, T - g0)
            S.dma("sp", at[:, :, :gn], aT3[:, :, g0:g0 + gn], reads=aTb, writes=[ab])
            loaded[key] = (at, ab)

        def prep(bi, ti, t0, ts):
            g = ti // 2
            if ti % 2 == 0:
                load_group((bi, g))
                nxt = order.index((bi, g)) + 1
                if nxt < len(order):
                    load_group(order[nxt])
            at, ab = loaded[(bi, g)]
            off = t0 - g * 256
            return (lambda k: at[:, k, off:off + ts]), [ab]

        pre, ep = residual_epilogue(kb, h_ap, hB, HS)
        gemm_tm(kb, 44, prep, wblocks, ep, pre_tile=pre)
C_QA, C_KA, C_VA, C_GA, C_QB, C_KB, C_VB, C_GB = 0, 256, 512, 1024, 1536, 2048, 2560, 3072
NP0 = 3584
DBG_L0 = {"la", "gemm", "ag"}
SC_INTRA, SC_XIB, SC_ZETA, SC_GCH, SC_MASK, SC_TRIU, SC_TRIL, SC_KBIAS, SC_N = 0, 128, 256, 258, 260, 516, 580, 644, 648


def stage_store(kb, ST, ps, pb, ts, n, dst_ap, dst_bufs, eng="act"):
    S = kb.S
    stt, stb = ST.next()
    if eng == "act":
        S.op("act", lambda e: e.copy(out=stt[:ts, :n], in_=ps[:ts, :n]), reads=[pb], writes=[stb])
    else:
        S.op("dve", lambda e: e.tensor_copy(out=stt[:ts, :n], in_=ps[:ts, :n]), reads=[pb], writes=[stb])
    S.dma("sp", dst_ap, stt[:ts, :n], reads=[stb], writes=dst_bufs)


def l0_proj(kb, h_ap, hB):
    S = kb.S
    dr = kb.dram
    w = dr["w_in_e"]
    P0 = kb.EX0.P3
    LA0 = kb.EXL.P3
    with kb.phase():
        XT = Tl(kb, "px", [128, 16, T], BF16)
        norm_T(kb, h_ap, hB, dr["mix_norm_e"], XT.t[0], XT.b[0])
        with subscope(kb) if "la" in DBG_L0 else contextlib.nullcontext():
          if "la" in DBG_L0:
              WL = Tl(kb, "wl", [128, 16, 16], BF16)
              LR = Tl(kb, "lr", [17, T], F32)
              LRH = Tl(kb, "lrh", [17, T], BF16)
              LRL = Tl(kb, "lrl", [17, T], BF16)
              W2 = Tl(kb, "w2", [17, 512], F32)
              W2H = Tl(kb, "w2h", [17, 512], BF16)
              W2L = Tl(kb, "w2l", [17, 512], BF16)
              PL = Tl(kb, "pl", [128, 512], F32, n=2, psum=True)
              E1 = Tl(kb, "e1", [128, 512], F32, n=2)
              S.dma("pool", WL.t[0][:], w[:, 3072:3088].rearrange("(k p) c -> p k c", p=128), writes=[WL.b[0]])
              S.dma("sp", W2.t[0][:], dr["gw2"], writes=[W2.b[0]])
              S.op("dve", lambda e: e.memset(LR.t[0][:], 1.0), writes=[LR.b[0]])
              S.op("dve", lambda e: e.tensor_copy(out=W2H.t[0][:], in_=W2.t[0][:]), reads=[W2.b[0]], writes=[W2H.b[0]])
              S.op("dve", lambda e: e.tensor_tensor(out=W2L.t[0][:], in0=W2.t[0][:], in1=W2H.t[0][:], op=ALU.subtract), reads=[W2.b[0], W2H.b[0]], writes=[W2L.b[0]])
              for s0 in range(0, T, 512):
                  sn = min(512, T - s0)
                  ps, pb = PL.next()
                  for k in range(16):
                      S.op("pe", lambda e, ps=ps, k=k, s0=s0, sn=sn: e.matmul(ps[:16, :sn], lhsT=WL.t[0][:, k, :], rhs=XT.t[0][:, k, s0:s0 + sn], start=(k == 0), stop=(k == 15)),
                           reads=[WL.b[0], XT.b[0]], writes=[pb])
                  S.op("act", lambda e, ps=ps, s0=s0, sn=sn: e.copy(out=LR.t[0][0:16, s0:s0 + sn], in_=ps[:16, :sn]), reads=[pb], writes=[LR.b[0]])
              S.op("dve", lambda e: e.tensor_copy(out=LRH.t[0][:], in_=LR.t[0][:]), reads=[LR.b[0]], writes=[LRH.b[0]])
              S.op("dve", lambda e: e.tensor_tensor(out=LRL.t[0][:], in0=LR.t[0][:], in1=LRH.t[0][:], op=ALU.subtract), reads=[LR.b[0], LRH.b[0]], writes=[LRL.b[0]])
              for ti, (t0, ts) in enumerate(TILES):
                  ps, pb = PL.next()
                  combos = [(LRH, W2H), (LRL, W2H), (LRH, W2L)]
                  for ci, (a, b_) in enumerate(combos):
                      S.op("pe", lambda e, ps=ps, a=a, b_=b_, ci=ci, t0=t0, ts=ts: e.matmul(ps[:ts, :], lhsT=a.t[0][:, t0:t0 + ts], rhs=b_.t[0][:, :], start=(ci == 0), stop=(ci == 2)),
                           reads=[a.b[0], b_.b[0]], writes=[pb])
                  e1, e1b = E1.next()
                  S.op("act", lambda e, ps=ps, e1=e1, ts=ts: e.activation(out=e1[:ts, :], in_=ps[:ts, :], func=AF.Exp, scale=-1.0), reads=[pb], writes=[e1b])
                  S.op("act", lambda e, e1=e1, ts=ts: e.activation(out=e1[:ts, :], in_=e1[:ts, :], func=AF.Ln, bias=kb.oneb[:ts, :], scale=1.0), reads=[e1b, kb.oneb_b], writes=[e1b])
                  S.op("dve", lambda e, e1=e1, ts=ts, ti=ti: e.tensor_scalar(out=e1[:ts, :], in0=e1[:ts, :], scalar1=kb.keep[:ts, ti:ti + 1], scalar2=-1.0 / 16.0,
                                                                          op0=ALU.mult, op1=ALU.mult), reads=[e1b, kb.keep_b], writes=[e1b])
                  for dst in range(2):
                      b_ = kb.EXL.wbuf(dst)
                      S.dma("sp", LA0[dst, t0:t0 + ts, :], e1[:ts, dst * 256:(dst + 1) * 256], reads=[e1b], writes=[b_])
        wblocks = []
        for dst in range(2):
            wblocks.append({"pieces": [(w[:, dst * 256:dst * 256 + 256], 0, 256), (w[:, 512 + dst * 256:512 + dst * 256 + 256], 256, 256)],
                            "ncols": 512, "dst": dst, "dcol": 0})
            for base, dcol in ((1024, C_VA), (2048, C_GA), (3088, C_QB), (4112, C_KB), (5136, C_VB), (6160, C_GB)):
                c0 = base + dst * 512
                wblocks.append({"pieces": [(w[:, c0:c0 + 512], 0, 512)], "ncols": 512, "dst": dst, "dcol": dcol})
        ST = Tl(kb, "pst", [128, 512], BF16, n=3)
        RTAB = Tl(kb, "prt", [128, 17, 256], F32)
        S.dma("sp", RTAB.t[0][:, 0:16, :], dr["rtab"][0:2048, :].rearrange("(i p) c -> p i c", p=128), writes=[RTAB.b[0]])
        S.dma("sp", RTAB.t[0][0:64, 16, :], dr["rtab"][2048:T, :], add_writes=[RTAB.b[0]])
        RA = Tl(kb, "pra", [128, 2, 2, 128], F32, n=2)
        RB = Tl(kb, "prb", [128, 2, 2, 128], F32, n=2)

        def ep(bi, ti, t0, ts, ps, pb, n):
            blk = wblocks[bi]
            b_ = kb.EX0.wbuf(blk["dst"])
            dst_ap = P0[blk["dst"], t0:t0 + ts, blk["dcol"]:blk["dcol"] + n]
            if blk["dcol"] in (C_QB, C_KB):
                x = ps[:ts, :].rearrange("p (g f d) -> p g f d", g=2, f=2)
                cosb = RTAB.t[0][:ts, ti, 0:128].unsqueeze(1).to_broadcast([ts, 2, 128])
                sinb = RTAB.t[0][:ts, ti, 128:256].unsqueeze(1).to_broadcast([ts, 2, 128])
                ra, rab = RA.next()
                rb, rbb = RB.next()
                S.op("dve", lambda e: e.tensor_tensor(out=ra[:ts, :, 0, :], in0=x[:, :, 0, :], in1=cosb, op=ALU.mult), reads=[pb, RTAB.b[0]], writes=[rab])
                S.op("dve", lambda e: e.tensor_tensor(out=rb[:ts, :, 0, :], in0=x[:, :, 1, :], in1=sinb, op=ALU.mult), reads=[pb, RTAB.b[0]], writes=[rbb])
                S.op("dve", lambda e: e.tensor_tensor(out=ra[:ts, :, 1, :], in0=x[:, :, 1, :], in1=cosb, op=ALU.mult), reads=[pb, RTAB.b[0], rab], writes=[rab])
                S.op("dve", lambda e: e.tensor_tensor(out=rb[:ts, :, 1, :], in0=x[:, :, 0, :], in1=sinb, op=ALU.mult), reads=[pb, RTAB.b[0], rbb], writes=[rbb])
                stt, stb = ST.next()
                so = stt[:ts, :].rearrange("p (g f d) -> p g f d", g=2, f=2)
                S.op("pool", lambda e: e.tensor_tensor(out=so[:, :, 0, :], in0=ra[:ts, :, 0, :], in1=rb[:ts, :, 0, :], op=ALU.subtract), reads=[rab, rbb], writes=[stb])
                S.op("pool", lambda e: e.tensor_tensor(out=so[:, :, 1, :], in0=ra[:ts, :, 1, :], in1=rb[:ts, :, 1, :], op=ALU.add), reads=[rab, rbb, stb], writes=[stb])
                S.dma("sp", dst_ap, stt[:ts, :n], reads=[stb], writes=[b_])
            else:
                stage_store(kb, ST, ps, pb, ts, n, dst_ap, [b_], eng="act")

        if "gemm" in DBG_L0:
            kb.EXL.gather()

            def after_block(bi):
                if bi == 6:
                    kb.EX0.gather_d(0)

            gemm_tm(kb, 16, resident_lhsT(XT.t[0], XT.b[0]), wblocks, ep, after_block=after_block)
            kb.EX0.gather_d(1)
            kb.EX0.finish()


def rms_gate_store(kb, OPt, OPb, nh, hd, rows, GN, GNb, gate_fn, OUT, out_cb, W):
    S = kb.S
    sq, sqb = W["SQ"].next()
    S.op("act", lambda e: e.activation(out=sq[:rows, :nh * hd], in_=OPt, func=AF.Square), reads=[OPb], writes=[sqb])
    ms, msb = W["MS"].next()
    S.op("dve", lambda e: e.tensor_reduce(out=ms[:rows, :nh], in_=sq[:rows, :nh * hd].rearrange("p (h d) -> p h d", h=nh), axis=AX.X, op=ALU.add),
         reads=[sqb], writes=[msb])
    rstd_inplace(kb, ms[:rows, :nh], msb, rows, 1.0 / hd)
    on, onb = W["ON"].next()
    S.op("dve", lambda e: e.tensor_tensor(out=on[:rows, :nh * hd].rearrange("p (h d) -> p h d", h=nh), in0=OPt.rearrange("p (h d) -> p h d", h=nh) if len(OPt.shape) == 2 else OPt,
                                          in1=ms[:rows, :nh].unsqueeze(2).to_broadcast([rows, nh, hd]), op=ALU.mult),
         reads=[OPb, msb], writes=[onb])
    if GN is not None:
        S.op("pool", lambda e: e.tensor_tensor(out=on[:rows, :nh * hd], in0=on[:rows, :nh * hd], in1=GN[:rows, :nh * hd], op=ALU.mult),
             reads=[onb, GNb], writes=[onb])
    ot, ob = OUT.next()
    if gate_fn is not None:
        sg, sgb = gate_fn()
        S.op("dve", lambda e: e.tensor_tensor(out=ot[:rows, :nh * hd], in0=on[:rows, :nh * hd], in1=sg[:rows, :nh * hd], op=ALU.mult),
             reads=[onb, sgb], writes=[ob])
    else:
        S.op("dve", lambda e: e.tensor_copy(out=ot[:rows, :nh * hd], in_=on[:rows, :nh * hd]), reads=[onb], writes=[ob])
    out_cb(ot, ob)


def l0_scan(kb):
    S = kb.S
    dr = kb.dram
    O1 = kb.EXO.P3
    SCL = 128.0 ** -0.5
    with kb.phase():
        SC = Tl(kb, "sc", [128, SC_N], F32)
        GN = Tl(kb, "sgn", [64, 1024], F32)
        TRI = Tl(kb, "tri", [64, 2, 64], BF16)
        ONE = Tl(kb, "one", [64, 1], BF16)
        Sg = Tl(kb, "Sg", [128, 2, 256], F32)
        Sgb = Tl(kb, "Sgb", [128, 2, 256], BF16)
        Sr = Tl(kb, "Sr", [128, 2, 2, 256], F32)
        Srb = Tl(kb, "Srb", [128, 2, 2, 256], BF16)
        S.dma("sp", SC.t[0][:], dr["scc"], writes=[SC.b[0]])
        S.dma("sp", GN.t[0][:], dr["gnorm"].partition_broadcast(64), writes=[GN.b[0]])
        S.op("dve", lambda e: e.tensor_copy(out=TRI.t[0][:], in_=SC.t[0][0:64, SC_TRIU:SC_TRIU + 128].rearrange("p (a b) -> p a b", a=2)), reads=[SC.b[0]], writes=[TRI.b[0]])
        S.op("dve", lambda e: e.memset(ONE.t[0][:], 1.0), writes=[ONE.b[0]])
        S.op("dve", lambda e: e.memset(Sg.t[0][:], 0.0), writes=[Sg.b[0]])
        S.op("dve", lambda e: e.memset(Sgb.t[0][:], 0.0), writes=[Sgb.b[0]])
        S.op("pool", lambda e: e.memset(Sr.t[0][:], 0.0), writes=[Sr.b[0]])
        S.op("pool", lambda e: e.memset(Srb.t[0][:], 0.0), writes=[Srb.b[0]])
        sc = SC.t[0]
        scb = SC.b[0]
        Dt = Tl(kb, "sD", [64, NP0], BF16, n=6)
        LAt = Tl(kb, "sLA", [64, 256], F32, n=5)
        LH = Tl(kb, "sLH", [64, 2, 256], BF16, n=2)
        EX = Tl(kb, "sEX", [64, 3, 256], F32, n=2)
        DEC = Tl(kb, "sDEC", [128, 2], F32, n=2)
        GPa = Tl(kb, "sGPa", [64, 3, 256], BF16, n=2)
        GPb = Tl(kb, "sGPb", [64, 2, 256], BF16, n=2)
        TT = Tl(kb, "sTT", [128, 8, 64], BF16, n=2)
        STt = Tl(kb, "sST", [64, 256], BF16, n=2)
        KBr = Tl(kb, "sKB", [64, 2, 256], BF16, n=2)
        TR = Tl(kb, "sTR", [128, 8, 64], BF16, n=2)
        TQ = Tl(kb, "sTQ", [128, 4, 64], BF16, n=2)
        STr = Tl(kb, "sSTr", [64, 2, 64], BF16, n=2)
        SG = Tl(kb, "sSG", [64, 1024], F32, n=2)
        OUT = Tl(kb, "sOUT", [64, 1024], BF16, n=2)
        OEV = Tl(kb, "sOEV", [64, 1024], F32, n=2)
        Wk = {"SQ": Tl(kb, "sSQ", [64, 1024], F32), "MS": Tl(kb, "sMS", [64, 4], F32, n=2), "ON": Tl(kb, "sON", [64, 1024], F32)}
        cumP = Tl(kb, "pcum", [64, 2, 256], F32, psum=True)
        MISC = Tl(kb, "pmisc", [128, 512], F32, psum=True)
        TPB = Tl(kb, "ptp", [128, 16, 64], BF16, psum=True)
        OP = Tl(kb, "pop", [64, 4, 256], F32, psum=True)
        SPs = Tl(kb, "psps", [128, 2, 256], F32, n=3, psum=True)
        misc = MISC.t[0]
        spB = sprB = decB = MISC.b[0]
        tpB = [TPB.b[0], TPB.b[0]]
        ident, ident_b = kb.ident, kb.ident_b

        def stage_l(c):
                src, slot = (0, c) if c <= 32 else (1, c - 32)
                r0 = slot * 64
                dt_, db = Dt.next()
                apf, gb = kb.EX0.read(src, r0, 64, 0, NP0)
                S.dma("sp", dt_[:], apf, reads=gb, writes=[db])
                lat, lab = LAt.next()
                apf, gb = kb.EXL.read(src, r0, 64, 0, 256)
                S.dma("sp", lat[:], apf, reads=gb, writes=[lab])
                return dt_, db, lat, lab

        def stage_a(c, ld):
                dt_, db, lat, lab = ld
                lh, lhb = LH.next()
                S.op("dve", lambda e, lh=lh, lat=lat: e.tensor_copy(out=lh[:, 0, :], in_=lat[:]), reads=[lab], writes=[lhb])
                S.op("dve", lambda e, lh=lh, lat=lat: e.tensor_tensor(out=lh[:, 1, :], in0=lat[:], in1=lh[:, 0, :], op=ALU.subtract), reads=[lab, lhb], writes=[lhb])
                cp, cpb = cumP.next()
                for w_ in range(2):
                    for hl in range(2):
                        S.op("pe", lambda e, cp=cp, lh=lh, w_=w_, hl=hl: e.matmul(cp[:, w_, :], lhsT=TRI.t[0][:, w_, :], rhs=lh[:, hl, :], start=(hl == 0), stop=(hl == 1)),
                             reads=[TRI.b[0], lhb], writes=[cpb])
                dp, dpb = misc[:, 384:386], decB
                for h in range(2):
                    for hl in range(2):
                        S.op("pe", lambda e, dp=dp, lh=lh, h=h, hl=hl: e.matmul(dp[:, h:h + 1], lhsT=lh[:, hl, h * 128:(h + 1) * 128], rhs=ONE.t[0][:, :], start=(hl == 0), stop=(hl == 1)),
                             reads=[ONE.b[0], lhb], writes=[dpb])
                ex, exb = EX.next()
                S.op("act", lambda e, ex=ex, cp=cp: e.activation(out=ex[:, 0, :], in_=cp[:, 0, :], func=AF.Exp), reads=[cpb], writes=[exb])
                S.op("act", lambda e, ex=ex, cp=cp: e.activation(out=ex[:, 1, :], in_=cp[:, 0, :], func=AF.Exp, scale=-1.0), reads=[cpb], writes=[exb])
                S.op("act", lambda e, ex=ex, cp=cp: e.activation(out=ex[:, 2, :], in_=cp[:, 1, :], func=AF.Exp), reads=[cpb], writes=[exb])
                dec, decb = DEC.next()
                S.op("act", lambda e, dec=dec, dp=dp: e.activation(out=dec[:], in_=dp, func=AF.Exp), reads=[dpb], writes=[decb])
                ga_, gab = GPa.next()
                gb_, gbb = GPb.next()
                q_ap = dt_[:, C_QA:C_QA + 256]
                k_ap = dt_[:, C_KA:C_KA + 256]
                S.op("dve", lambda e, ga_=ga_, ex=ex, q_ap=q_ap: e.scalar_tensor_tensor(out=ga_[:, 0, :], in0=q_ap, scalar=SCL, in1=ex[:, 0, :], op0=ALU.mult, op1=ALU.mult), reads=[db, exb], writes=[gab])
                S.op("pool", lambda e, gb_=gb_, ex=ex, q_ap=q_ap: e.tensor_tensor(out=gb_[:, 0, :], in0=q_ap, in1=ex[:, 1, :], op=ALU.mult), reads=[db, exb], writes=[gbb])
                S.op("dve", lambda e, ga_=ga_, ex=ex, k_ap=k_ap: e.tensor_tensor(out=ga_[:, 1, :], in0=k_ap, in1=ex[:, 1, :], op=ALU.mult), reads=[db, exb, gab], writes=[gab])
                S.op("pool", lambda e, gb_=gb_, ex=ex, k_ap=k_ap: e.tensor_tensor(out=gb_[:, 1, :], in0=k_ap, in1=ex[:, 0, :], op=ALU.mult), reads=[db, exb, gbb], writes=[gbb])
                S.op("dve", lambda e, ga_=ga_, ex=ex, k_ap=k_ap: e.tensor_tensor(out=ga_[:, 2, :], in0=k_ap, in1=ex[:, 2, :], op=ALU.mult), reads=[db, exb, gab], writes=[gab])
                tp, tpb = TPB.t[0][:, 0:8, :], tpB[0]
                srcs = [(ga_, 0, gab), (gb_, 0, gbb), (ga_, 1, gab), (gb_, 1, gbb)]
                for it in range(4):
                    st_, si, sb_ = srcs[it]
                    for h in range(2):
                        S.op("pe", lambda e, tp=tp, st_=st_, si=si, it=it, h=h: e.transpose(out=tp[:, it * 2 + h, :], in_=st_[:, si, h * 128:(h + 1) * 128], identity=ident[:64, :64]),
                             reads=[sb_, ident_b], writes=[tpb])
                tt, ttb = TT.next()
                S.op("act", lambda e, tt=tt, tp=tp: e.copy(out=tt[:], in_=tp), reads=[tpb], writes=[ttb])
                rr = dt_[:, C_QB:C_QB + 1024].rearrange("p (g f d) -> p g f d", g=4, f=2)
                rrb = db
                kbr, kbrb = KBr.next()
                S.op("dve", lambda e, kbr=kbr, rr=rr: e.tensor_tensor(out=kbr[:], in0=rr[:, 2:4, :, :].rearrange("p g f d -> p g (f d)"),
                                                                      in1=sc[0:64, SC_ZETA:SC_ZETA + 2].unsqueeze(2).to_broadcast([64, 2, 256]), op=ALU.mult),
                     reads=[rrb, scb], writes=[kbrb])
                tp2, tp2b = TPB.t[0][:, 8:16, :], tpB[1]
                for g in range(4):
                    for f in range(2):
                        S.op("pe", lambda e, tp2=tp2, rr=rr, g=g, f=f: e.transpose(out=tp2[:, g * 2 + f, :], in_=rr[:, g, f, :], identity=ident[:64, :64]),
                             reads=[rrb, ident_b], writes=[tp2b])
                tr, trb = TR.next()
                S.op("act", lambda e, tr=tr, tp2=tp2: e.copy(out=tr[:], in_=tp2), reads=[tp2b], writes=[trb])
                tq, tqb = TQ.next()
                S.op("dve", lambda e, tq=tq, tp2=tp2: e.tensor_tensor(out=tq[:].rearrange("p (h f) i -> p h f i", h=2), in0=tp2[:, 0:4, :].rearrange("p (h f) i -> p h f i", h=2),
                                                                      in1=sc[:, SC_XIB:SC_XIB + 128].rearrange("p (h i) -> p h i", h=2).unsqueeze(2).to_broadcast([128, 2, 2, 64]), op=ALU.mult),
                     reads=[tp2b, scb], writes=[tqb])
                sg, sgb = SG.next()
                S.op("act", lambda e, sg=sg, dt_=dt_: e.activation(out=sg[:, 0:512], in_=dt_[:, C_GA:C_GA + 512], func=AF.Exp, scale=-1.0), reads=[db], writes=[sgb])
                S.op("act", lambda e, sg=sg, dt_=dt_: e.activation(out=sg[:, 512:1024], in_=dt_[:, C_GB:C_GB + 512], func=AF.Exp, scale=-1.0), reads=[db, sgb], writes=[sgb])
                S.op("act", lambda e, sg=sg: e.activation(out=sg[:, :], in_=sg[:, :], func=AF.Ln, bias=kb.oneb[:64, :], scale=1.0), reads=[sgb, kb.oneb_b], writes=[sgb])
                S.op("act", lambda e, sg=sg: e.activation(out=sg[:, :], in_=sg[:, :], func=AF.Exp, scale=-1.0), reads=[sgb], writes=[sgb])
                S.op("pool", lambda e, sg=sg, dt_=dt_: e.tensor_tensor(out=sg[:, 0:512], in0=sg[:, 0:512], in1=dt_[:, C_GA:C_GA + 512], op=ALU.mult), reads=[db, sgb], writes=[sgb])
                S.op("pool", lambda e, sg=sg, dt_=dt_: e.tensor_tensor(out=sg[:, 512:1024], in0=sg[:, 512:1024], in1=dt_[:, C_GB:C_GB + 512], op=ALU.mult), reads=[db, sgb], writes=[sgb])
                S.op("pool", lambda e, sg=sg: e.tensor_tensor(out=sg[:, :], in0=sg[:, :], in1=GN.t[0][:, :], op=ALU.mult), reads=[sgb, GN.b[0]], writes=[sgb])

                return dict(c=c, dt_=dt_, db=db, dec=dec, decb=decb, ga_=ga_, gab=gab, tt=tt, ttb=ttb, kbr=kbr, kbrb=kbrb, tr=tr, trb=trb, tq=tq, tqb=tqb, sg=sg, sgb=sgb)

        def stage_b(Lc):
                c, dt_, db, dec, decb, ga_, gab, tt, ttb = Lc['c'], Lc['dt_'], Lc['db'], Lc['dec'], Lc['decb'], Lc['ga_'], Lc['gab'], Lc['tt'], Lc['ttb']
                kbr, kbrb, tr, trb, tq, tqb, sg, sgb = Lc['kbr'], Lc['kbrb'], Lc['tr'], Lc['trb'], Lc['tq'], Lc['tqb'], Lc['sg'], Lc['sgb']
                sp_, spb = misc[0:64, 0:256], spB
                for h in range(2):
                    S.op("pe", lambda e, sp_=sp_, tt=tt, h=h: e.matmul(sp_[:, (h * 2 + 0) * 64:(h * 2 + 1) * 64], lhsT=tt[:, 2 * 2 + h, :], rhs=tt[:, 0 * 2 + h, :], start=True, stop=True),
                         reads=[ttb], writes=[spb])
                    S.op("pe", lambda e, sp_=sp_, tt=tt, h=h: e.matmul(sp_[:, (h * 2 + 1) * 64:(h * 2 + 2) * 64], lhsT=tt[:, 3 * 2 + h, :], rhs=tt[:, 1 * 2 + h, :], start=True, stop=True),
                         reads=[ttb], writes=[spb])
                stt, stb = STt.next()
                S.op("dve", lambda e, stt=stt, sp_=sp_: e.tensor_tensor(out=stt[:], in0=sp_, in1=sc[0:64, SC_MASK:SC_MASK + 256], op=ALU.mult), reads=[spb, scb], writes=[stb])
                op_, opb = OP.next()
                for h in range(2):
                    v_ap = dt_[:, C_VA + h * 256:C_VA + (h + 1) * 256]
                    S.op("pe", lambda e, op_=op_, stt=stt, h=h, v_ap=v_ap: e.matmul(op_[:, h, :], lhsT=stt[:, (h * 2) * 64:(h * 2 + 1) * 64], rhs=v_ap, start=True, stop=False),
                         reads=[stb, db], writes=[opb])
                    S.op("pe", lambda e, op_=op_, stt=stt, h=h, v_ap=v_ap: e.matmul(op_[:, h, :], lhsT=stt[:, (h * 2 + 1) * 64:(h * 2 + 2) * 64], rhs=v_ap, start=False, stop=False),
                         reads=[stb, db], writes=[opb])
                    S.op("pe", lambda e, op_=op_, tt=tt, h=h: e.matmul(op_[:, h, :], lhsT=tt[:, 0 * 2 + h, :], rhs=Sgb.t[0][:, h, :], start=False, stop=True),
                         reads=[ttb, Sgb.b[0]], writes=[opb])
                ss, ssb = SPs.next()
                for h in range(2):
                    v_ap = dt_[:, C_VA + h * 256:C_VA + (h + 1) * 256]
                    S.op("pe", lambda e, ss=ss, ga_=ga_, h=h, v_ap=v_ap: e.matmul(ss[:, h, :], lhsT=ga_[:, 2, h * 128:(h + 1) * 128], rhs=v_ap, start=True, stop=True),
                         reads=[gab, db], writes=[ssb])
                for h in range(2):
                    S.op("dve", lambda e, ss=ss, dec=dec, h=h: e.scalar_tensor_tensor(out=Sg.t[0][:, h, :], in0=Sg.t[0][:, h, :], scalar=dec[:, h:h + 1], in1=ss[:, h, :], op0=ALU.mult, op1=ALU.add),
                         reads=[ssb, decb, Sg.b[0]], writes=[Sg.b[0]])
                S.op("act", lambda e: e.copy(out=Sgb.t[0][:], in_=Sg.t[0][:]), reads=[Sg.b[0]], writes=[Sgb.b[0]])
                spr, sprb = misc[0:64, 256:384], sprB
                for h in range(2):
                    for f in range(2):
                        S.op("pe", lambda e, spr=spr, tr=tr, h=h, f=f: e.matmul(spr[:, h * 64:(h + 1) * 64], lhsT=tr[:, 4 + h * 2 + f, :], rhs=tr[:, h * 2 + f, :], start=(f == 0), stop=(f == 1)),
                             reads=[trb], writes=[sprb])
                strt, strb = STr.next()
                S.op("dve", lambda e, strt=strt, spr=spr: e.tensor_tensor(out=strt[:].rearrange("p h i -> p (h i)"), in0=spr, in1=sc[0:64, SC_INTRA:SC_INTRA + 128], op=ALU.mult),
                     reads=[sprb, scb], writes=[strb])
                for h in range(2):
                    v_ap = dt_[:, C_VB + h * 256:C_VB + (h + 1) * 256]
                    S.op("pe", lambda e, op_=op_, strt=strt, h=h, v_ap=v_ap: e.matmul(op_[:, 2 + h, :], lhsT=strt[:, h, :], rhs=v_ap, start=True, stop=False),
                         reads=[strb, db], writes=[opb])
                    for f in range(2):
                        S.op("pe", lambda e, op_=op_, tq=tq, h=h, f=f: e.matmul(op_[:, 2 + h, :], lhsT=tq[:, h * 2 + f, :], rhs=Srb.t[0][:, h, f, :], start=False, stop=(f == 1)),
                             reads=[tqb, Srb.b[0]], writes=[opb])
                for h in range(2):
                    v_ap = dt_[:, C_VB + h * 256:C_VB + (h + 1) * 256]
                    ss, ssb = SPs.next()
                    for f in range(2):
                        S.op("pe", lambda e, ss=ss, kbr=kbr, h=h, f=f, v_ap=v_ap: e.matmul(ss[:, f, :], lhsT=kbr[:, h, f * 128:(f + 1) * 128], rhs=v_ap, start=True, stop=True),
                             reads=[kbrb, db], writes=[ssb])
                    S.op("dve", lambda e, ss=ss, h=h: e.scalar_tensor_tensor(out=Sr.t[0][:, h, :, :], in0=Sr.t[0][:, h, :, :], scalar=sc[:, SC_GCH + h:SC_GCH + h + 1], in1=ss[:],
                                                                            op0=ALU.mult, op1=ALU.add),
                         reads=[ssb, scb, Sr.b[0]], writes=[Sr.b[0]])
                S.op("act", lambda e: e.copy(out=Srb.t[0][:], in_=Sr.t[0][:]), reads=[Sr.b[0]], writes=[Srb.b[0]])
                def out_cb(ot, ob, c=c):
                    if c <= 32:
                        S.dma("sp", O1[0, c * 64:(c + 1) * 64, :], ot[:, :], reads=[ob], writes=[kb.EXO.wbuf(0)])
                    if c >= 32:
                        S.dma("sp", O1[1, (c - 32) * 64:(c - 31) * 64, :], ot[:, :], reads=[ob], writes=[kb.EXO.wbuf(1)])

                oe, oeb = OEV.next()
                S.op("act", lambda e, oe=oe, op_=op_: e.copy(out=oe[:], in_=op_[:].rearrange("p h d -> p (h d)")), reads=[opb], writes=[oeb])
                rms_gate_store(kb, oe[:], oeb, 4, 256, 64, None, None, lambda sg=sg, sgb=sgb: (sg, sgb), OUT, out_cb, Wk)

        AHEAD = 3
        loads = {c: stage_l(c) for c in range(min(AHEAD, SEQC))}
        pend = stage_a(0, loads.pop(0))
        for c in range(1, SEQC):
            if c + AHEAD - 1 < SEQC:
                loads[c + AHEAD - 1] = stage_l(c + AHEAD - 1)
            nxt = stage_a(c, loads.pop(c))
            stage_b(pend)
            if pend["c"] == 32:
                kb.EXO.gather_d(0)
            pend = nxt
        stage_b(pend)
        kb.EXO.gather_d(1)
        kb.EXO.finish()
NP1 = 3072


def l1_proj(kb, h_ap, hB):
    dr = kb.dram
    w = dr["w_in_o"]
    P1 = kb.EX1.P3
    with kb.phase():
        XT = Tl(kb, "px", [128, 16, T], BF16)
        norm_T(kb, h_ap, hB, dr["mix_norm_o"], XT.t[0], XT.b[0])
        wblocks = []
        for dst in range(2):
            for base, dcol in ((0, 0), (2048, 1024), (4096, 2048)):
                for j in range(2):
                    c0 = base + dst * 1024 + j * 512
                    wblocks.append({"pieces": [(w[:, c0:c0 + 512], 0, 512)], "ncols": 512, "dst": dst, "dcol": dcol + j * 512})
        ST = Tl(kb, "pst", [128, 512], BF16, n=3)
        S_ = kb.S
        DTAB = Tl(kb, "pdt", [128, 17, 128], F32)
        S_.dma("sp", DTAB.t[0][:, 0:16, :], dr["dtab"][0:2048, :].rearrange("(i p) c -> p i c", p=128), writes=[DTAB.b[0]])
        S_.dma("sp", DTAB.t[0][0:64, 16, :], dr["dtab"][2048:T, :], add_writes=[DTAB.b[0]])
        GQK = Tl(kb, "pgqk", [128, 2, 128], F32)
        S_.dma("sp", GQK.t[0][:].rearrange("p a b -> p (a b)"), dr["qk_norm"].partition_broadcast(128), writes=[GQK.b[0]])
        SQ = Tl(kb, "psq", [128, 512], F32, n=2)
        MS4 = Tl(kb, "pms4", [128, 4], F32, n=3)
        XN = Tl(kb, "pxn", [128, 4, 128], F32, n=2)
        RA = Tl(kb, "pra", [128, 4, 2, 64], F32, n=2)
        RB = Tl(kb, "prb", [128, 4, 2, 64], F32, n=2)

        def ep(bi, ti, t0, ts, ps, pb, n):
            S = S_
            blk = wblocks[bi]
            b_ = kb.EX1.wbuf(blk["dst"])
            dst_ap = P1[blk["dst"], t0:t0 + ts, blk["dcol"]:blk["dcol"] + n]
            if blk["dcol"] < 2048:
                qk = 0 if blk["dcol"] < 1024 else 1
                sq, sqb = SQ.next()
                S.op("act", lambda e: e.activation(out=sq[:ts, :], in_=ps[:ts, :], func=AF.Square), reads=[pb], writes=[sqb])
                ms, msb = MS4.next()
                S.op("dve", lambda e: e.tensor_reduce(out=ms[:ts, :], in_=sq[:ts, :].rearrange("p (g d) -> p g d", g=4), axis=AX.X, op=ALU.add), reads=[sqb], writes=[msb])
                rstd_inplace(kb, ms[:ts, :], msb, ts, 1.0 / 128)
                xn, xnb = XN.next()
                S.op("dve", lambda e: e.tensor_tensor(out=xn[:ts], in0=ps[:ts, :].rearrange("p (g d) -> p g d", g=4), in1=ms[:ts, :].unsqueeze(2).to_broadcast([ts, 4, 128]), op=ALU.mult),
                     reads=[pb, msb], writes=[xnb])
                S.op("pool", lambda e: e.tensor_tensor(out=xn[:ts], in0=xn[:ts], in1=GQK.t[0][:ts, qk, :].unsqueeze(1).to_broadcast([ts, 4, 128]), op=ALU.mult),
                     reads=[xnb, GQK.b[0]], writes=[xnb])
                cosb = DTAB.t[0][:ts, ti, 0:64].unsqueeze(1).to_broadcast([ts, 4, 64])
                sinb = DTAB.t[0][:ts, ti, 64:128].unsqueeze(1).to_broadcast([ts, 4, 64])
                x1 = xn[:ts, :, 0:64]
                x2 = xn[:ts, :, 64:128]
                ra, rab = RA.next()
                rb, rbb = RB.next()
                S.op("dve", lambda e: e.tensor_tensor(out=ra[:ts, :, 0, :], in0=x1, in1=cosb, op=ALU.mult), reads=[xnb, DTAB.b[0]], writes=[rab])
                S.op("pool", lambda e: e.tensor_tensor(out=rb[:ts, :, 0, :], in0=x2, in1=sinb, op=ALU.mult), reads=[xnb, DTAB.b[0]], writes=[rbb])
                S.op("dve", lambda e: e.tensor_tensor(out=ra[:ts, :, 1, :], in0=x2, in1=cosb, op=ALU.mult), reads=[xnb, DTAB.b[0], rab], writes=[rab])
                S.op("pool", lambda e: e.tensor_tensor(out=rb[:ts, :, 1, :], in0=x1, in1=sinb, op=ALU.mult), reads=[xnb, DTAB.b[0], rbb], writes=[rbb])
                stt, stb = ST.next()
                so = stt[:ts, :].rearrange("p (g d) -> p g d", g=4)
                S.op("dve", lambda e: e.tensor_tensor(out=so[:, :, 0:64], in0=ra[:ts, :, 0, :], in1=rb[:ts, :, 0, :], op=ALU.subtract), reads=[rab, rbb], writes=[stb])
                S.op("pool", lambda e: e.tensor_tensor(out=so[:, :, 64:128], in0=ra[:ts, :, 1, :], in1=rb[:ts, :, 1, :], op=ALU.add), reads=[rab, rbb, stb], writes=[stb])
                S.dma("sp", dst_ap, stt[:ts, :n], reads=[stb], writes=[b_])
            else:
                stage_store(kb, ST, ps, pb, ts, n, dst_ap, [b_], eng="act")

        def after_block(bi):
            if bi == 5:
                kb.EX1.gather_d(0)

        gemm_tm(kb, 16, resident_lhsT(XT.t[0], XT.b[0]), wblocks, ep, after_block=after_block)
        kb.EX1.gather_d(1)
        kb.EX1.finish()


def seq_rows(n0, n):
    out = []
    c0, c1 = n0 // 64, (n0 + n) // 64
    c = c0
    while c < c1:
        if c <= 32:
            ce = min(c1, 33)
            out.append((0, c * 64, (ce - c) * 64, (c - c0) * 64))
        else:
            ce = c1
            out.append((1, (c - 32) * 64, (ce - c) * 64, (c - c0) * 64))
        c = ce
    return out


def l1_attn(kb):
    S = kb.S
    dr = kb.dram
    O2 = kb.EXO.P3
    SCALE = 128.0 ** -0.5
    NTT = 33
    with kb.phase():
        GD = Tl(kb, "aGD", [128, 256], F32)
        LV = Tl(kb, "aLV", [128, 4, 128], F32)
        LS = Tl(kb, "aLS", [128, 4], F32)
        NLAM = Tl(kb, "aNL", [128, 1], F32)
        KBI = Tl(kb, "aKB", [128, 1], F32)
        ZB = Tl(kb, "aZB", [128, 1], F32)
        S.dma("sp", GD.t[0][:], dr["diff_norm"].partition_broadcast(128), writes=[GD.b[0]])
        S.op("dve", lambda e: e.tensor_scalar(out=GD.t[0][:], in0=GD.t[0][:], scalar1=1.0 - LAM_INIT, scalar2=None, op0=ALU.mult), reads=[GD.b[0]], writes=[GD.b[0]])
        S.dma("sp", LV.t[0][:].rearrange("p a b -> p (a b)"), dr["lamv"].partition_broadcast(128), writes=[LV.b[0]])
        S.op("dve", lambda e: e.tensor_tensor(out=LV.t[0][:, 0, :], in0=LV.t[0][:, 0, :], in1=LV.t[0][:, 1, :], op=ALU.mult), reads=[LV.b[0]], writes=[LV.b[0]])
        S.op("dve", lambda e: e.tensor_tensor(out=LV.t[0][:, 2, :], in0=LV.t[0][:, 2, :], in1=LV.t[0][:, 3, :], op=ALU.mult), reads=[LV.b[0]], writes=[LV.b[0]])
        S.op("dve", lambda e: e.tensor_reduce(out=LS.t[0][:], in_=LV.t[0][:], axis=AX.X, op=ALU.add), reads=[LV.b[0]], writes=[LS.b[0]])
        S.op("act", lambda e: e.activation(out=LS.t[0][:], in_=LS.t[0][:], func=AF.Exp), reads=[LS.b[0]], writes=[LS.b[0]])
        S.op("dve", lambda e: e.tensor_tensor(out=NLAM.t[0][:], in0=LS.t[0][:, 2:3], in1=LS.t[0][:, 0:1], op=ALU.subtract), reads=[LS.b[0]], writes=[NLAM.b[0]])
        S.op("dve", lambda e: e.tensor_scalar(out=NLAM.t[0][:], in0=NLAM.t[0][:], scalar1=-LAM_INIT, scalar2=None, op0=ALU.add), reads=[NLAM.b[0]], writes=[NLAM.b[0]])
        S.dma("sp", KBI.t[0][:], dr["scc"][:, SC_KBIAS:SC_KBIAS + 1], writes=[KBI.b[0]], allow_slow_non_contiguous=True)
        S.op("dve", lambda e: e.memset(ZB.t[0][:], 0.0), writes=[ZB.b[0]])

        QKT = Tl(kb, "aQKT", [128, 4, LSEQ], BF16, n=2)
        VA = Tl(kb, "aVA", [128, NTT, 257], BF16, n=2)
        for i in range(2):
            S.op("pool", lambda e, i=i: e.memset(VA.t[i][:, :, 256:257], 1.0), writes=[VA.b[i]])
        X = Tl(kb, "aX", [128, 768], BF16, n=7)
        PTr = Tl(kb, "aPTr", [128, 4, 128], BF16, n=2, psum=True)
        PT = Tl(kb, "aPT", [128, 512], BF16, n=5)
        SPS = Tl(kb, "aSPS", [128, 512], F32, n=2, psum=True)
        OPS = Tl(kb, "aOPS", [128, 512], F32, n=4, psum=True)
        O1s = Tl(kb, "aO1s", [128, 4, 256], F32)
        O1sb = [Buf() for _ in range(4)]
        RD = Tl(kb, "aRD", [128, 1], F32, n=6)
        OC = Tl(kb, "aOC", [128, 256], F32, n=3)
        OUT = Tl(kb, "aOUT", [128, 256], BF16, n=3)
        Wk = {"SQ": Tl(kb, "aSQ2", [128, 256], F32), "MS": Tl(kb, "aMS", [128, 1], F32, n=2), "ON": Tl(kb, "aON", [128, 256], F32)}

        def prep_gen(h):
            qkt, qktb = QKT.t[h % 2], QKT.b[h % 2]
            va, vab = VA.t[h % 2], VA.b[h % 2]
            LOOK = 4

            def load(tt):
                n0 = tt * 128
                ts = min(128, LSEQ - n0)
                xt, xb = X.next()
                first = True
                for (src, r0, nr, doff) in seq_rows(n0, ts):
                    for ci, cbase in enumerate((0, 1024, 2048)):
                        apf, gb = kb.EX1.read(src, r0, nr, cbase + h * 256, 256)
                        S.dma("sp", xt[doff:doff + nr, ci * 256:(ci + 1) * 256], apf,
                              reads=gb, writes=[xb] if first else [], add_writes=[] if first else [xb])
                        first = False
                return xt, xb, n0, ts

            pending = [load(tt) for tt in range(min(LOOK, NTT))]
            for tt in range(NTT):
                if tt + LOOK < NTT:
                    pending.append(load(tt + LOOK))
                xt, xb, n0, ts = pending.pop(0)
                S.op("pool", lambda e, va=va, xt=xt, tt=tt, ts=ts: e.tensor_copy(out=va[:ts, tt, 0:256], in_=xt[:ts, 512:768]), reads=[xb], writes=[vab])
                transpose_rows(kb, xt, [xb], ts, 4, qkt, qktb, n0, PTr, evac="dve")
                yield

        def chunk_store(ot, ob, h, a, rows):
            for cc in range(rows // 64):
                c = a // 64 + cc
                dsts = ([(0, c)] if c <= 32 else []) + ([(1, c - 32)] if c >= 32 else [])
                for (d_, sl) in dsts:
                    S.dma("sp", O2[d_, sl * 64:(sl + 1) * 64, h * 256:(h + 1) * 256], ot[cc * 64:(cc + 1) * 64, :], reads=[ob], writes=[kb.EXO.wbuf(d_)])

        def attention(h, bg):
            qkt, qktb = QKT.t[h % 2], QKT.b[h % 2]
            va, vab = VA.t[h % 2], VA.b[h % 2]
            for q0 in range(0, LSEQ, 512):
                qn = min(512, LSEQ - q0)
                subs = [(a, min(128, q0 + qn - a)) for a in range(q0, q0 + qn, 128)]
                for p in range(2):
                    ops = [OPS.next() for _ in subs]
                    tiles = [t for t in range(NTT) if t * 128 < q0 + qn]

                    def stage1(t):
                        k0 = t * 128
                        kn = min(128, LSEQ - k0)
                        vstart = max(q0, k0)
                        n = q0 + qn - vstart
                        sp_, spb = SPS.next()
                        S.op("pe", lambda e, sp_=sp_, k0=k0, kn=kn, vstart=vstart, n=n, p=p: e.matmul(sp_[:kn, :n], lhsT=qkt[:, 2 + p, k0:k0 + kn], rhs=qkt[:, p, vstart:vstart + n],
                                                                                             start=True, stop=True), reads=[qktb], writes=[spb])
                        pt, ptb = PT.next()
                        bias_t, bias_b = (KBI, KBI.b[0]) if t == 0 else (ZB, ZB.b[0])
                        S.op("act", lambda e, pt=pt, sp_=sp_, kn=kn, n=n, bias_t=bias_t: e.activation(out=pt[:kn, :n], in_=sp_[:kn, :n], func=AF.Exp, scale=SCALE, bias=bias_t.t[0][:kn, :]),
                             reads=[spb, bias_b], writes=[ptb])
                        if k0 >= q0 and kn == 128:
                            S.op("pool", lambda e, pt=pt: e.memset(pt[64:128, 0:64], 0.0), reads=[ptb], writes=[ptb])
                        return (t, kn, vstart, pt, ptb)

                    def finish(m):
                        a, rows = subs[m]
                        op_, opb = ops[m]
                        rd, rdb = RD.next()
                        S.op("dve", lambda e, rd=rd, op_=op_, rows=rows: e.reciprocal(out=rd[:rows, :], in_=op_[:rows, 256:257]), reads=[opb], writes=[rdb])
                        if p == 0:
                            S.op("dve", lambda e, rd=rd, op_=op_, rows=rows, m=m: e.tensor_scalar(out=O1s.t[0][:rows, m, :], in0=op_[:rows, 0:256], scalar1=rd[:rows, 0:1], scalar2=None, op0=ALU.mult),
                                 reads=[opb, rdb], writes=[O1sb[m]])
                        else:
                            oc, ocb = OC.next()
                            S.op("dve", lambda e, rd=rd, op_=op_, rows=rows, oc=oc: e.tensor_scalar(out=oc[:rows, :], in0=op_[:rows, 0:256], scalar1=rd[:rows, 0:1], scalar2=NLAM.t[0][:rows, 0:1],
                                                                                                op0=ALU.mult, op1=ALU.mult), reads=[opb, rdb, NLAM.b[0]], writes=[ocb])
                            S.op("pool", lambda e, rows=rows, oc=oc, m=m: e.tensor_tensor(out=oc[:rows, :], in0=oc[:rows, :], in1=O1s.t[0][:rows, m, :], op=ALU.add),
                                 reads=[ocb, O1sb[m]], writes=[ocb])
                            rms_gate_store(kb, oc[:rows, :], ocb, 1, 256, rows, GD.t[0], GD.b[0], None, OUT,
                                           lambda ot, ob, a=a, rows=rows: chunk_store(ot, ob, h, a, rows), Wk)

                    def stage2(st):
                        t, kn, vstart, pt, ptb = st
                        for m, (a, rows) in enumerate(subs):
                            if a < vstart:
                                continue
                            rel = a - vstart
                            op_, opb = ops[m]
                            last = (t == a // 128)
                            S.op("pe", lambda e, op_=op_, pt=pt, kn=kn, rel=rel, rows=rows, t=t, last=last: e.matmul(op_[:rows, 0:257], lhsT=pt[:kn, rel:rel + rows], rhs=va[:kn, t, :],
                                                                                                                 start=(t == 0), stop=last), reads=[ptb, vab], writes=[opb])
                            if last:
                                finish(m)

                    SKEW = 2
                    pend = []
                    for t in tiles:
                        pend.append(stage1(t))
                        if len(pend) > SKEW:
                            stage2(pend.pop(0))
                    while pend:
                        stage2(pend.pop(0))
                    if bg is not None:
                        for _ in range(2):
                            next(bg, None)
            if bg is not None:
                for _ in bg:
                    pass

        g0 = prep_gen(0)
        for _ in g0:
            pass
        for h in range(4):
            bg = prep_gen(h + 1) if h + 1 < 4 else None
            attention(h, bg)
INPUT_SPECS = [
    ("xin", [T, D]), ("keep", [128, 17]), ("scc", [128, SC_N]), ("gnorm", [1, 1024]), ("rtab", [T, 256]), ("dtab", [T, 128]),
    ("mix_norm_e", [1, D]), ("w_in_e", [D, 7184]), ("gw2", [17, 512]), ("w_out_e", [D, D]),
    ("mix_norm_o", [1, D]), ("w_in_o", [D, 6144]), ("qk_norm", [1, 256]), ("lamv", [1, 512]), ("diff_norm", [1, 256]), ("w_out_o", [D, D]),
    ("ffn_norm", [2, D]), ("w_up", [2, D, 2 * DFF]), ("conv_w", [2, 3, 2 * DFF]), ("conv_b", [2, 2 * DFF]), ("w_down", [2, DFF, D]),
]
PHASES = ["none", "l0_proj", "l0_scan", "l0_out", "l0_ffn", "l1_proj", "l1_attn", "l1_out", "l1_ffn"]


def build(stop_after=None, dump=(), start_at=None):
    nc = bass.Bass("TRN2", target_bir_lowering=False)
    kb = KB(nc)
    S = kb.S
    for name, shape in INPUT_SPECS:
        kb.dt_in(name, shape)
    out = kb.dt_out("out", [2048, D])
    hbuf = kb.dt_int("hbuf", [T, D])
    kb.dt_int("aT", [DFF, T], BF16)
    kb.EX0 = Exchange(kb, "ex0", NP0, BF16, 256)
    kb.EXL = Exchange(kb, "exl", 256, F32, 1024)
    kb.EXO = Exchange(kb, "exo", 1024, BF16, 1024)
    kb.EX1 = Exchange(kb, "ex1", NP1, BF16, 256)
    kb.dram["P0"] = kb.EX0.P
    kb.dram["LA0"] = kb.EXL.P
    kb.dram["O1"] = kb.EXO.P
    kb.dram["P1"] = kb.EX1.P
    dr = kb.dram
    dump_out = {}
    for name in dump:
        src = dr[name]
        dump_out[name] = nc.dram_tensor("dbg_" + name, list(src.shape), src.dtype, kind="ExternalOutput").ap()
    with contextlib.ExitStack() as st:
        kb.st = st
        ident = Tl(kb, "ident", [128, 128], BF16)
        kb.ident, kb.ident_b = ident.t[0], ident.b[0]
        build_identity(kb, kb.ident, kb.ident_b)
        keep = Tl(kb, "keep", [128, 17], F32)
        kb.keep, kb.keep_b = keep.t[0], keep.b[0]
        S.dma("sp", kb.keep[:], dr["keep"], writes=[kb.keep_b])
        epsb = Tl(kb, "epsb", [128, 1], F32)
        kb.epsb, kb.epsb_b = epsb.t[0], epsb.b[0]
        S.op("dve", lambda e: e.memset(kb.epsb[:], EPS), writes=[kb.epsb_b])
        oneb = Tl(kb, "oneb", [128, 1], F32)
        kb.oneb, kb.oneb_b = oneb.t[0], oneb.b[0]
        S.op("dve", lambda e: e.memset(kb.oneb[:], 1.0), writes=[kb.oneb_b])
        hB = [[Buf() for _ in range(4)] for _ in TILES]
        allh = [b for row in hB for b in row]
        S.dma("sp", hbuf, dr["xin"], writes=allh)
        S.barrier()

        def done(phase):
            return stop_after is not None and PHASES.index(phase) >= PHASES.index(stop_after)

        def active(phase):
            return start_at is None or PHASES.index(phase) >= PHASES.index(start_at)

        def run():
            if done("none"):
                return
            if active("l0_proj"):
                l0_proj(kb, hbuf, hB)
            if done("l0_proj"):
                return
            if active("l0_scan"):
                l0_scan(kb)
            if done("l0_scan"):
                return
            if active("l0_out"):
                cm0 = [(0, 0, 0, 512), (0, 512, 1024, 512), (1, 0, 512, 512), (1, 512, 1536, 512)]
                mixer_out_proj(kb, kb.EXO, cm0, dr["w_out_e"], hbuf, hB)
            if done("l0_out"):
                return
            if active("l0_ffn"):
                ffn(kb, 0, hbuf, hB)
            if done("l0_ffn"):
                return
            if active("l1_proj"):
                l1_proj(kb, hbuf, hB)
            if done("l1_proj"):
                return
            if active("l1_attn"):
                l1_attn(kb)
                kb.EXO.gather()
            if done("l1_attn"):
                return
            if active("l1_out"):
                cm1 = [(0, 0, 0, 1024), (1, 0, 1024, 1024)]
                mixer_out_proj(kb, kb.EXO, cm1, dr["w_out_o"], hbuf, hB)
            if done("l1_out"):
                return
            if active("l1_ffn"):
                ffn(kb, 1, hbuf, hB)

        run()
        S.barrier()
        S.dma("sp", out, hbuf[64:T, :], reads=allh, writes=[Buf()])
        for name in dump:
            S.dma("sp", dump_out[name], dr[name], writes=[Buf()])
        S.emit()
    return nc


def _consts(rank):
    scc = np.zeros((128, SC_N), np.float64)
    j = np.arange(64)[:, None]
    i = np.arange(64)[None, :]
    for hl in range(2):
        h = 2 * rank + hl
        lg = math.log(1.0 - 2.0 ** (-5.0 - h))
        scc[0:64, SC_INTRA + hl * 64:SC_INTRA + (hl + 1) * 64] = np.exp(np.abs(i - j) * lg) / 16.0
        scc[:, SC_XIB + hl * 64:SC_XIB + (hl + 1) * 64] = (np.exp((np.arange(64) + 1.0) * lg) / 16.0)[None, :]
        scc[0:64, SC_ZETA + hl] = np.exp((63.0 - np.arange(64)) * lg)
        scc[:, SC_GCH + hl] = math.exp(64.0 * lg)
        scc[0:64, SC_MASK + (hl * 2 + 0) * 64:SC_MASK + (hl * 2 + 1) * 64] = (i >= j)
        scc[0:64, SC_MASK + (hl * 2 + 1) * 64:SC_MASK + (hl * 2 + 2) * 64] = (i < j) * (128.0 ** -0.5)
    scc[0:64, SC_TRIU:SC_TRIU + 64] = (i >= j)
    scc[0:64, SC_TRIL:SC_TRIL + 64] = (i < j)
    scc[0:48, SC_KBIAS] = -30000.0
    return scc.astype(np.float32)


def _tables():
    f32 = np.float32
    pos = (np.arange(LSEQ) - 48).astype(f32)
    inv_r = (f32(1.0) / (f32(10000.0) ** np.linspace(0.0, 1.0, 128, dtype=f32))).astype(f32)
    ang = (pos[:, None] * inv_r[None, :]).astype(f32)
    ret_tab = np.concatenate([np.cos(ang), np.sin(ang)], axis=1).astype(f32)
    inv_d = (f32(1.0) / (f32(10000.0) ** (np.arange(0, 128, 2, dtype=f32) / f32(128)))).astype(f32)
    ang = (pos[:, None] * inv_d[None, :]).astype(f32)
    diff_tab = np.concatenate([np.cos(ang), np.sin(ang)], axis=1).astype(f32)
    return ret_tab, diff_tab


def make_in_maps(x, meta, mix_norm_e, w_in_e, gla_w_gate_e, gla_b_gate_e, gla_norm_e, ret_norm_e, w_out_e,
                 mix_norm_o, w_in_o, q_norm_o, k_norm_o, lam_q1_o, lam_k1_o, lam_q2_o, lam_k2_o, diff_norm_o, w_out_o,
                 ffn_norm, w_up, conv_w, conv_b, w_down):
    f = lambda a: np.ascontiguousarray(np.asarray(a, dtype=np.float32))
    x = f(x)
    meta = f(meta)
    ret_tab, diff_tab = _tables()
    shared = {
        "mix_norm_e": f(mix_norm_e).reshape(1, D), "w_in_e": f(w_in_e)[0],
        "gw2": np.concatenate([f(gla_w_gate_e)[0], f(gla_b_gate_e)[0][None, :]], axis=0),
        "w_out_e": f(w_out_e)[0], "mix_norm_o": f(mix_norm_o).reshape(1, D), "w_in_o": f(w_in_o)[0],
        "qk_norm": np.concatenate([f(q_norm_o)[0], f(k_norm_o)[0]])[None, :],
        "lamv": np.concatenate([f(lam_q1_o)[0], f(lam_k1_o)[0], f(lam_q2_o)[0], f(lam_k2_o)[0]])[None, :],
        "diff_norm": f(diff_norm_o).reshape(1, 256), "w_out_o": f(w_out_o)[0],
        "ffn_norm": f(ffn_norm), "w_up": f(w_up), "conv_w": f(conv_w), "conv_b": f(conv_b), "w_down": f(w_down),
    }
    consts = [_consts(0), _consts(1)]
    gn = f(gla_norm_e)[0]
    rn = f(ret_norm_e)[0]
    in_maps = []
    for c in range(8):
        b, r = c // 2, c % 2
        if r == 0:
            xin = np.concatenate([np.zeros((48, D), np.float32), meta, x[b, 0:2048]], axis=0)
        else:
            xin = x[b, 1984:4096]
        tok = np.arange(17 * 128)
        valid = (tok < T) & ((tok >= 48) if r == 0 else True)
        keep = np.ascontiguousarray(valid.reshape(17, 128).T.astype(np.float32))
        m = dict(shared)
        m["xin"] = np.ascontiguousarray(xin)
        m["keep"] = keep
        m["scc"] = consts[r]
        lo = 0 if r == 0 else 2048
        m["rtab"] = np.ascontiguousarray(ret_tab[lo:lo + T])
        m["dtab"] = np.ascontiguousarray(diff_tab[lo:lo + T])
        m["gnorm"] = np.concatenate([gn[2 * r:2 * r + 2].reshape(-1), rn[2 * r:2 * r + 2].reshape(-1)])[None, :]
        in_maps.append(m)
    return in_maps


_NC_CACHE = {}


def kernel(**inputs):
    in_maps = make_in_maps(**inputs)
    if "nc" not in _NC_CACHE:
        _NC_CACHE["nc"] = build()
    res = run_bass_kernel_spmd(_NC_CACHE["nc"], in_maps, core_ids=list(range(8)))
    outp = np.zeros((4, 4096, D), np.float32)
    for c in range(8):
        b, r = c // 2, c % 2
        outp[b, r * 2048:(r + 1) * 2048] = res.results[c]["out"]
    return outp
```
